# Optimizing a Trainium2 kernel written in Bass

```python
import jax, jax.numpy as jnp
from jax import lax
import numpy as np

D_MODEL = 2048
BATCH = 8
SEQ = 2048
DEPTH = 1

MLA_HEADS = 16
MLA_Q_LORA = 512
MLA_KV_LORA = 512
MLA_NOPE_DIM = 128
MLA_ROPE_DIM = 64
MLA_V_DIM = 128
ROPE_THETA = 10000.0
Q_BLOCK = 128
HGRN_HEADS = 16
HGRN_EXPAND = 128
HGRN_HEAD_DIM = D_MODEL // HGRN_HEADS
HGRN_CHUNK = 64
D_FF = 5632
MACARON_SCALE = 0.5
NORM_EPS = 1e-6

MLA_WIDTH = MLA_HEADS * MLA_V_DIM
HGRN_KEY_WIDTH = HGRN_HEADS * HGRN_EXPAND
HGRN_VAL_WIDTH = HGRN_HEADS * HGRN_HEAD_DIM
IN_SPLITS = (MLA_Q_LORA, MLA_KV_LORA, MLA_ROPE_DIM, HGRN_KEY_WIDTH, HGRN_KEY_WIDTH,
             HGRN_VAL_WIDTH, HGRN_VAL_WIDTH, D_MODEL, D_MODEL)
IN_COLS = 13376

kernel_name = "hybrid_mla_hgrn2_macaron_sandwich"


def rmsnorm(x, w):
    x32 = x.astype(jnp.float32)
    y = x32 * lax.rsqrt(jnp.mean(x32 * x32, axis=-1, keepdims=True) + NORM_EPS)
    return (y * w.astype(jnp.float32)).astype(x.dtype)


def swiglu(h, w_gate, w_up, w_down):
    return (jax.nn.silu(h @ w_gate) * (h @ w_up)) @ w_down


def rope(t, cos, sin):
    half = t.shape[-1] // 2
    t1, t2 = t[..., :half], t[..., half:]
    return jnp.concatenate([t1 * cos - t2 * sin, t1 * sin + t2 * cos], axis=-1).astype(t.dtype)


def split_columns(proj):
    parts, start = [], 0
    for width in IN_SPLITS:
        parts.append(proj[..., start:start + width])
        start += width
    return parts


def mla_branch(c_q, c_kv, k_rope, positions, q_norm, w_q_up, kv_norm, w_kv_up, w_o):
    B, S, _ = c_q.shape
    q = (rmsnorm(c_q, q_norm) @ w_q_up).reshape(B, S, MLA_HEADS, MLA_NOPE_DIM + MLA_ROPE_DIM)
    q_nope, q_rope = q[..., :MLA_NOPE_DIM], q[..., MLA_NOPE_DIM:]
    kv = (rmsnorm(c_kv, kv_norm) @ w_kv_up).reshape(B, S, MLA_HEADS, MLA_NOPE_DIM + MLA_V_DIM)
    k_nope, v = kv[..., :MLA_NOPE_DIM], kv[..., MLA_NOPE_DIM:]
    half = MLA_ROPE_DIM // 2
    inv_freq = ROPE_THETA ** (-jnp.arange(half, dtype=jnp.float32) / half)
    ang = positions.astype(jnp.float32)[..., None] * inv_freq
    cos, sin = jnp.cos(ang), jnp.sin(ang)
    q_rope = rope(q_rope, cos[:, :, None, :], sin[:, :, None, :])
    k_rope = rope(k_rope, cos, sin)
    scale = (MLA_NOPE_DIM + MLA_ROPE_DIM) ** -0.5
    nb = S // Q_BLOCK
    key_idx = jnp.arange(S)

    def blocks(t):
        return t.reshape((B, nb, Q_BLOCK) + t.shape[2:]).swapaxes(0, 1)

    def attend(args):
        qn, qr, blk = args
        q_idx = blk * Q_BLOCK + jnp.arange(Q_BLOCK)
        s = (jnp.einsum('bqhd,bkhd->bhqk', qn, k_nope)
             + jnp.einsum('bqhr,bkr->bhqk', qr, k_rope)).astype(jnp.float32) * scale
        s = jnp.where(key_idx[None, :] <= q_idx[:, None], s, -jnp.inf)
        p = jax.nn.softmax(s, axis=-1).astype(v.dtype)
        return jnp.einsum('bhqk,bkhd->bqhd', p, v)

    o = lax.map(attend, (blocks(q_nope), blocks(q_rope), jnp.arange(nb)))
    o = o.swapaxes(0, 1).reshape(B, S, MLA_WIDTH)
    return o @ w_o


def hgrn2_branch(hq, hf, hi, hg, lower_bound, out_norm, w_o):
    B, S, _ = hq.shape
    C = HGRN_CHUNK
    nc = S // C

    def heads(t, d):
        return t.reshape(B, S, HGRN_HEADS, d).transpose(0, 2, 1, 3).astype(jnp.float32)

    q = heads(jax.nn.silu(hq), HGRN_EXPAND)
    lb = lower_bound.astype(jnp.float32)
    f_gate = lb + (1.0 - lb) * jax.nn.sigmoid(hf.astype(jnp.float32))
    k = heads(1.0 - f_gate, HGRN_EXPAND)
    log_f = heads(jnp.log(f_gate), HGRN_EXPAND)
    v = heads(hi, HGRN_HEAD_DIM)

    def chunks(t):
        return t.reshape(B, HGRN_HEADS, nc, C, t.shape[-1]).transpose(2, 0, 1, 3, 4)

    causal = jnp.tril(jnp.ones((C, C), dtype=bool))

    def step(state, inp):
        qc, kc, vc, gc = inp
        b = jnp.cumsum(gc, axis=2)
        inter = jnp.einsum('bhtk,bhkv->bhtv', qc * jnp.exp(b), state)
        diff = b[:, :, :, None, :] - b[:, :, None, :, :]
        decay = jnp.exp(jnp.where(causal[:, :, None], diff, -jnp.inf))
        scores = jnp.einsum('bhtk,bhtsk,bhsk->bhts', qc, decay, kc)
        intra = jnp.einsum('bhts,bhsv->bhtv', scores, vc)
        b_last = b[:, :, -1:, :]
        new_state = (jnp.exp(b_last[:, :, 0, :])[..., None] * state
                     + jnp.einsum('bhsk,bhsv->bhkv', kc * jnp.exp(b_last - b), vc))
        return new_state, inter + intra

    s0 = jnp.zeros((B, HGRN_HEADS, HGRN_EXPAND, HGRN_HEAD_DIM), jnp.float32)
    _, o = lax.scan(step, s0, (chunks(q), chunks(k), chunks(v), chunks(log_f)))
    o = o.transpose(1, 0, 3, 2, 4).reshape(B, S, HGRN_HEADS, HGRN_HEAD_DIM)
    o = rmsnorm(o, out_norm) * jax.nn.silu(hg.reshape(B, S, HGRN_HEADS, HGRN_HEAD_DIM).astype(jnp.float32))
    return o.reshape(B, S, HGRN_VAL_WIDTH).astype(hq.dtype) @ w_o


def token_mixer(h, positions, w_in, mla_q_norm, mla_w_q_up, mla_kv_norm, mla_w_kv_up, mla_w_o,
                lower_bound, hgrn_out_norm, hgrn_w_o, w_out):
    proj = h @ w_in
    c_q, c_kv, k_rope, hq, hf, hi, hg, gate_a, gate_b = split_columns(proj)
    y_a = mla_branch(c_q, c_kv, k_rope, positions, mla_q_norm, mla_w_q_up, mla_kv_norm, mla_w_kv_up, mla_w_o)
    y_b = hgrn2_branch(hq, hf, hi, hg, lower_bound, hgrn_out_norm, hgrn_w_o)
    merged = jax.nn.sigmoid(gate_a) * y_a + jax.nn.sigmoid(gate_b) * y_b
    return merged @ w_out


def setup_inputs(seed: int = 0) -> dict:
    key = jax.random.key(seed)
    ks = jax.random.split(key, 28)

    def w(k, shape, fan_in):
        return jax.random.normal(k, shape, jnp.float32) * (fan_in ** -0.5)

    def gain(k, shape):
        return 1.0 + 0.05 * jax.random.normal(k, shape, jnp.float32)

    x = jax.random.normal(ks[0], (BATCH, SEQ, D_MODEL), jnp.float32)
    offsets = jax.random.randint(ks[1], (BATCH, 1), 0, 4096, dtype=jnp.int32)
    positions = offsets + jnp.arange(SEQ, dtype=jnp.int32)[None, :]
    return {
        "x": x,
        "positions": positions,
        "ffn1_norm_pre": gain(ks[2], (DEPTH, D_MODEL)),
        "ffn1_w_gate": w(ks[3], (DEPTH, D_MODEL, D_FF), D_MODEL),
        "ffn1_w_up": w(ks[4], (DEPTH, D_MODEL, D_FF), D_MODEL),
        "ffn1_w_down": w(ks[5], (DEPTH, D_FF, D_MODEL), D_FF),
        "ffn1_norm_post": gain(ks[6], (DEPTH, D_MODEL)),
        "mix_norm_pre": gain(ks[7], (DEPTH, D_MODEL)),
        "w_in": w(ks[8], (DEPTH, D_MODEL, IN_COLS), D_MODEL),
        "mla_q_norm": gain(ks[9], (DEPTH, MLA_Q_LORA)),
        "mla_w_q_up": w(ks[10], (DEPTH, MLA_Q_LORA, MLA_HEADS * (MLA_NOPE_DIM + MLA_ROPE_DIM)), MLA_Q_LORA),
        "mla_kv_norm": gain(ks[11], (DEPTH, MLA_KV_LORA)),
        "mla_w_kv_up": w(ks[12], (DEPTH, MLA_KV_LORA, MLA_HEADS * (MLA_NOPE_DIM + MLA_V_DIM)), MLA_KV_LORA),
        "mla_w_o": w(ks[13], (DEPTH, MLA_WIDTH, D_MODEL), MLA_WIDTH),
        "hgrn_lb_logits": 0.5 * jax.random.normal(ks[14], (DEPTH + 1, HGRN_KEY_WIDTH), jnp.float32),
        "hgrn_out_norm": gain(ks[15], (DEPTH, HGRN_HEAD_DIM)),
        "hgrn_w_o": w(ks[16], (DEPTH, HGRN_VAL_WIDTH, D_MODEL), HGRN_VAL_WIDTH),
        "w_out": w(ks[17], (DEPTH, D_MODEL, D_MODEL), D_MODEL),
        "mix_norm_post": gain(ks[18], (DEPTH, D_MODEL)),
        "ffn2_norm_pre": gain(ks[19], (DEPTH, D_MODEL)),
        "ffn2_w_gate": w(ks[20], (DEPTH, D_MODEL, D_FF), D_MODEL),
        "ffn2_w_up": w(ks[21], (DEPTH, D_MODEL, D_FF), D_MODEL),
        "ffn2_w_down": w(ks[22], (DEPTH, D_FF, D_MODEL), D_FF),
        "ffn2_norm_post": gain(ks[23], (DEPTH, D_MODEL)),
    }


def reference(x, positions, ffn1_norm_pre, ffn1_w_gate, ffn1_w_up, ffn1_w_down, ffn1_norm_post,
              mix_norm_pre, w_in, mla_q_norm, mla_w_q_up, mla_kv_norm, mla_w_kv_up, mla_w_o,
              hgrn_lb_logits, hgrn_out_norm, hgrn_w_o, w_out, mix_norm_post,
              ffn2_norm_pre, ffn2_w_gate, ffn2_w_up, ffn2_w_down, ffn2_norm_post):
    lb_table = jnp.cumsum(jax.nn.softmax(hgrn_lb_logits.astype(jnp.float32), axis=0), axis=0)
    for l in range(DEPTH):
        h = rmsnorm(x, ffn1_norm_pre[l])
        x = x + MACARON_SCALE * rmsnorm(swiglu(h, ffn1_w_gate[l], ffn1_w_up[l], ffn1_w_down[l]), ffn1_norm_post[l])
        h = rmsnorm(x, mix_norm_pre[l])
        y = token_mixer(h, positions, w_in[l], mla_q_norm[l], mla_w_q_up[l], mla_kv_norm[l], mla_w_kv_up[l],
                        mla_w_o[l], lb_table[l], hgrn_out_norm[l], hgrn_w_o[l], w_out[l])
        x = x + rmsnorm(y, mix_norm_post[l])
        h = rmsnorm(x, ffn2_norm_pre[l])
        x = x + MACARON_SCALE * rmsnorm(swiglu(h, ffn2_w_gate[l], ffn2_w_up[l], ffn2_w_down[l]), ffn2_norm_post[l])
    return x
```

```python
import math
from contextlib import ExitStack
import numpy as np
import concourse.bass as bass
import concourse.mybir as mybir
from concourse.bass_utils import run_bass_kernel_spmd

F32 = mybir.dt.float32
BF16 = mybir.dt.bfloat16
I32 = mybir.dt.int32
AF = mybir.ActivationFunctionType
ALU = mybir.AluOpType
AX = mybir.AxisListType

S_TOK = 2048
D = 2048
KC = 16
FF = 5632
FC = 44
NT = 16
NG = 4
EPS = 1e-6
NH = 16
ATT_SCALE = 192.0 ** -0.5


class Op:
    __slots__ = ("eng", "fn", "deps", "sig", "sigval", "dma_sem")


class Sched:
    ENGS = ("pe", "act", "dve", "pool", "sp")

    def __init__(self, nc, stack, n_dma_sems=8):
        self.nc = nc
        self.ops = {e: [] for e in self.ENGS}
        self.last_w = {}
        self.readers = {}
        self.sems = {e: stack.enter_context(nc.semaphore("s_" + e)) for e in ("pe", "act", "dve", "pool")}
        self.dma_sems = {}
        self.dma_cnt = {}
        self.dma_ring = {}
        for q in ("sp", "pool"):
            self.dma_sems[q] = [stack.enter_context(nc.semaphore(f"d_{q}{i}")) for i in range(n_dma_sems)]
            self.dma_cnt[q] = 0
            self.dma_ring[q] = [None] * n_dma_sems
        self.emitted = {e: 0 for e in self.ENGS}
        self.sigcnt = {e: 0 for e in self.ENGS}
        self.waited = {e: {} for e in self.ENGS}

    def op(self, eng, fn, reads=(), writes=(), dma=False):
        o = Op()
        o.eng = eng; o.fn = fn; o.sig = False; o.sigval = None; o.dma_sem = None
        deps = set()
        for r in reads:
            w = self.last_w.get(r)
            if w is not None:
                deps.add(w)
        for w_ in writes:
            w = self.last_w.get(w_)
            if w is not None:
                deps.add(w)
            for rd in self.readers.get(w_, ()):
                deps.add(rd)
        if dma:
            n = len(self.dma_sems[eng])
            i = self.dma_cnt[eng]
            slot = i % n
            prev = self.dma_ring[eng][slot]
            if prev is not None:
                deps.add(prev)
            o.dma_sem = (self.dma_sems[eng][slot], 16 * (i // n + 1))
            self.dma_ring[eng][slot] = o
            self.dma_cnt[eng] = i + 1
        o.deps = deps
        for d in deps:
            if d.dma_sem is None:
                d.sig = True
        for r in reads:
            self.readers.setdefault(r, []).append(o)
        for w_ in writes:
            self.last_w[w_] = o
            self.readers[w_] = []
        self.ops[eng].append(o)
        return o

    def finish_phase(self):
        lasts = []
        for e in ("pe", "act", "dve", "pool"):
            if len(self.ops[e]) > self.emitted[e]:
                lasts.append(self.ops[e][-1])
        for q in self.dma_ring:
            for o in self.dma_ring[q]:
                if o is not None:
                    lasts.append(o)
        for e in self.ENGS:
            o = self.op(e, lambda eng: eng.nop())
            o.deps = set(lasts)
            for d in o.deps:
                if d.dma_sem is None:
                    d.sig = True
        self.last_w = {}
        self.readers = {}

    def emit(self, block):
        for e in ("pe", "act", "dve", "pool"):
            c = self.sigcnt[e]
            for o in self.ops[e][self.emitted[e]:]:
                if o.dma_sem is None and o.sig:
                    c += 1
                    o.sigval = c
            self.sigcnt[e] = c

        def run_engine(ename, eng):
            waited = self.waited[ename]
            for o in self.ops[ename][self.emitted[ename]:]:
                for d in o.deps:
                    if d.dma_sem is not None:
                        sem, val = d.dma_sem
                    else:
                        if d.eng == "pe" and ename == "pe":
                            continue
                        sem, val = self.sems[d.eng], d.sigval
                    key = id(sem)
                    if waited.get(key, 0) >= val:
                        continue
                    waited[key] = val
                    eng.wait_ge(sem, val)
                ins = o.fn(eng)
                if o.dma_sem is not None:
                    ins.then_inc(o.dma_sem[0], 16)
                elif o.sig:
                    ins.then_inc(self.sems[ename], 1)
                o.fn = None
            self.emitted[ename] = len(self.ops[ename])

        @block.tensor
        def _(eng):
            run_engine("pe", eng)

        @block.scalar
        def _(eng):
            run_engine("act", eng)

        @block.vector
        def _(eng):
            run_engine("dve", eng)

        @block.gpsimd
        def _(eng):
            run_engine("pool", eng)

        @block.sync
        def _(eng):
            run_engine("sp", eng)


class Ring:
    def __init__(self, sb, name, n, shape, dtype):
        self.tiles = [sb(f"{name}{i}", shape, dtype) for i in range(n)]
        self.name = name
        self.i = 0

    def next(self):
        j = self.i % len(self.tiles)
        self.i += 1
        return self.tiles[j], (self.name, j)


def build(stage=3):
    nc = bass.Bass("TRN2", target_bir_lowering=False)
    dt_in = lambda name, shape, dt=F32: nc.dram_tensor(name, list(shape), dt, kind="ExternalInput").ap()
    scr = lambda name, shape, dt: nc.dram_tensor(name, list(shape), dt).ap()
    x_d = dt_in("x", [S_TOK, D])
    pos_d = dt_in("pos", [64, S_TOK], I32)
    gains_d = dt_in("gains", [6, 128, D])
    glat_d = dt_in("glat", [128, 1024])
    lbl_d = dt_in("lbl", [2, 128, D])
    gout_d = dt_in("gout", [128, 1])
    ident_d = dt_in("ident", [128, 128])
    tri_d = dt_in("tri", [128, 128])
    tris_d = dt_in("tris", [128, 128])
    caus_d = dt_in("caus", [128, 128])
    blk2_d = dt_in("blk2", [128, 2])
    cst_d = dt_in("cst", [64, 8])
    ffn_w = []
    for i in (1, 2):
        ffn_w.append((dt_in(f"wg{i}", [FC, 128, KC, 128]), dt_in(f"wu{i}", [FC, 128, KC, 128]),
                      dt_in(f"wd{i}", [4, 128, FC, 512])))
    wlat_d = dt_in("wlat", [2, 128, KC, 512])
    wkr_d = dt_in("wkr", [128, KC, 128])
    wgate_d = dt_in("wgate", [32, 128, KC, 128])
    wh3_d = dt_in("wh3", [3, 8, 128, KC, 256])
    whg_d = dt_in("whg", [16, 128, KC, 128])
    wq_d = dt_in("wq", [16, 128, 4, 256])
    wkn_d = dt_in("wkn", [16, 128, 4, 128])
    wv_d = dt_in("wv", [4, 128, 4, 512])
    wo_d = dt_in("wo", [16, 128, KC, 128])
    wob_d = dt_in("wob", [16, 128, KC, 128])
    wout_d = dt_in("wout", [4, 128, KC, 512])
    out_d = nc.dram_tensor("out", [S_TOK, D], F32, kind="ExternalOutput").ap()
    A1_d = scr("A1", [NT, 128, FC, 128], BF16)
    Y_d = scr("Y", [S_TOK, D], F32)
    X1_d = scr("X1", [S_TOK, D], F32)
    X2_d = scr("X2", [S_TOK, D], F32)
    GG_d = scr("GG", [32, 128, S_TOK], BF16)
    OG_d = scr("OG", [16, 128, S_TOK], BF16)
    OA_d = scr("OA", [16, 128, S_TOK], BF16)
    MT_d = scr("MT", [16, 128, S_TOK], BF16)
    CS_d = scr("CS", [2, 64, S_TOK], F32)
    LAT_d = scr("LAT", [128, 8, S_TOK], BF16)

    with ExitStack() as st0:
        S = Sched(nc, st0)
        uid = [0]

        def uname(name):
            uid[0] += 1
            return f"s{uid[0]}_{name}"
        sb0 = lambda name, shape, dt: st0.enter_context(nc.sbuf_tensor(uname(name), list(shape), dt))
        banks = [st0.enter_context(nc.psum_tensor(f"bank{i}", [128, 512], F32)) for i in range(8)]
        bk = lambda i: ("bank", i)
        ident = sb0("ident", [128, 128], BF16)
        ones_bf = sb0("ones_bf", [128, 128], BF16)
        tri = sb0("tri", [128, 128], F32)
        tris = sb0("tris", [128, 128], F32)
        caus = sb0("caus", [128, 128], BF16)
        blk2 = sb0("blk2", [128, 2], F32)
        cst = sb0("cst_s", [64, 8], F32)
        gout = sb0("gout_s", [128, 1], F32)
        stat = sb0("stat", [128, 64], F32)

        def phase(fn):
            with ExitStack() as st:
                sb = lambda name, shape, dt: st.enter_context(nc.sbuf_tensor(uname(name), list(shape), dt))
                fn(sb)
                S.finish_phase()
                with nc.Block() as block:
                    S.emit(block)

        def dma(q, out, in_, reads=(), writes=(), flat=False):
            if flat:
                out = out.rearrange("p a b -> p (a b)")
                in_ = in_.rearrange("p a b -> p (a b)")
            return S.op(q, lambda e: e.dma_start(out=out, in_=in_), reads=reads, writes=writes, dma=True)

        def wait_all(keys):
            for en in ("pe", "act", "dve", "pool"):
                S.op(en, lambda e: e.nop(), reads=list(keys))

        def p_consts(sb):
            dma("pool", ident[:], ident_d, writes=["ident"])
            dma("pool", caus[:], caus_d, writes=["caus"])
            dma("sp", tri[:], tri_d, writes=["tri"])
            dma("sp", tris[:], tris_d, writes=["tris"])
            dma("sp", blk2[:], blk2_d, writes=["blk2"])
            dma("sp", cst[:], cst_d, writes=["cst"])
            dma("sp", gout[:], gout_d, writes=["gout"])
            S.op("dve", lambda e: e.memset(ones_bf[:], 1.0), writes=["ones"])
            S.op("dve", lambda e: e.memset(stat[:], 0.0), writes=["stat"])
        phase(p_consts)

        def rstd_chain(sc, ksc, c_in, c0, mul, add, extra_reads=()):
            S.op("dve", lambda e: e.tensor_scalar(out=sc[:, c0:c0 + 1], in0=sc[:, c_in:c_in + 1], scalar1=mul, scalar2=add,
                                                   op0=ALU.mult, op1=ALU.add),
                 reads=[(ksc, c_in)] + list(extra_reads), writes=[(ksc, c0)])
            S.op("act", lambda e: e.activation(out=sc[:, c0 + 1:c0 + 2], in_=sc[:, c0:c0 + 1], func=AF.Sqrt),
                 reads=[(ksc, c0)], writes=[(ksc, c0 + 1)])
            S.op("dve", lambda e: e.reciprocal(out=sc[:, c0 + 2:c0 + 3], in_=sc[:, c0 + 1:c0 + 2]),
                 reads=[(ksc, c0 + 1)], writes=[(ksc, c0 + 2)])

        def rn_pass(sb, hT, src_x, y_d, post_idx, post_scale, dst_x, pre_idx):
            xring = Ring(sb, "rn_x", 3, [128, D], F32)
            yring = Ring(sb, "rn_y", 3, [128, D], F32) if y_d is not None else None
            xsring = Ring(sb, "rn_xs", 2, [128, D], BF16)
            scring = Ring(sb, "rn_sc", 4, [128, 12], F32)
            junk = sb("rn_junk", [128, D], BF16)
            gpost = sb("rn_gpost", [128, D], F32) if y_d is not None else None
            gpre = sb("rn_gpre", [128, D], F32) if pre_idx is not None else None
            if gpost is not None:
                dma("sp", gpost[:], gains_d[post_idx], writes=["gpost"])
            if gpre is not None:
                dma("sp", gpre[:], gains_d[pre_idx], writes=["gpre"])
            pre = {}

            def load_t(tt):
                if tt >= NT or tt in pre:
                    return
                rows = slice(tt * 128, (tt + 1) * 128)
                xt, kx = xring.next()
                dma("sp", xt[:], src_x[rows, :], writes=[kx])
                yt = ky = None
                if y_d is not None:
                    yt, ky = yring.next()
                    dma("sp", yt[:], y_d[rows, :], writes=[ky])
                pre[tt] = (xt, kx, yt, ky)
            load_t(0)
            for tt in range(NT):
                rows = slice(tt * 128, (tt + 1) * 128)
                load_t(tt + 1)
                xt, kx, yt, ky = pre.pop(tt)
                sc, ksc = scring.next()
                S.op("pool", lambda e, sc=sc: e.memset(sc[:], 0.0), writes=[(ksc, i) for i in range(12)])
                if y_d is not None:
                    S.op("dve", lambda e, sc=sc, tt=tt: e.reduce_sum(out=sc[:, 0:1], in_=stat[:, tt * 4:(tt + 1) * 4], axis=AX.X),
                         reads=[], writes=[(ksc, 0)])
                    s2 = post_scale * post_scale
                    rstd_chain(sc, ksc, 0, 1, 1.0 / (D * s2), EPS / s2)
                    S.op("dve", lambda e, yt=yt, sc=sc: e.scalar_tensor_tensor(out=yt[:], in0=yt[:], scalar=sc[:, 3:4], in1=gpost[:],
                                                                               op0=ALU.mult, op1=ALU.mult),
                         reads=[ky, (ksc, 3), "gpost"], writes=[ky])
                    S.op("pool", lambda e, xt=xt, yt=yt: e.tensor_tensor(out=xt[:], in0=xt[:], in1=yt[:], op=ALU.add),
                         reads=[kx, ky], writes=[kx])
                if dst_x is not None:
                    dma("sp", dst_x[rows, :], xt[:], reads=[kx], writes=[("dst", tt)])
                if pre_idx is not None:
                    S.op("act", lambda e, xt=xt, sc=sc: e.activation(out=junk[:], in_=xt[:], func=AF.Square, accum_out=sc[:, 4:5]),
                         reads=[kx], writes=["junk", (ksc, 4)])
                    rstd_chain(sc, ksc, 4, 5, 1.0 / D, EPS)
                    xs, kxs = xsring.next()
                    S.op("dve", lambda e, xs=xs, xt=xt, sc=sc: e.scalar_tensor_tensor(out=xs[:], in0=xt[:], scalar=sc[:, 7:8], in1=gpre[:],
                                                                                       op0=ALU.mult, op1=ALU.mult),
                         reads=[kx, (ksc, 7), "gpre"], writes=[kxs])
                    for half in range(2):
                        bi = (tt * 2 + half) % 4
                        bv = banks[bi][:].bitcast(BF16)

                        def tr(e, xs=xs, bv=bv, half=half):
                            ins = None
                            for j in range(8):
                                c = half * 8 + j
                                ins = e.transpose(bv[:, j * 128:(j + 1) * 128], xs[:, c * 128:(c + 1) * 128], ident[:])
                            return ins
                        S.op("pe", tr, reads=[kxs], writes=[bk(bi)])
                        dst = hT[:, half * 8:(half + 1) * 8, tt * 128:(tt + 1) * 128]
                        src = bv.rearrange("p (a b) -> p a b", b=128)
                        if half == 0:
                            S.op("act", lambda e, dst=dst, src=src: e.copy(out=dst, in_=src), reads=[bk(bi)], writes=[("hT", tt, half)])
                        else:
                            S.op("dve", lambda e, dst=dst, src=src: e.tensor_copy(out=dst, in_=src), reads=[bk(bi)], writes=[("hT", tt, half)])

        def gu_phase(sb, hT, wg_d, wu_d):
            wgr = Ring(sb, "gu_wg", 3, [128, KC, 128], BF16)
            wur = Ring(sb, "gu_wu", 3, [128, KC, 128], BF16)
            sgr = Ring(sb, "gu_sg", 2, [128, 512], F32)
            atr = Ring(sb, "gu_at", 3, [128, 512], BF16)
            it = 0
            for fc in range(FC):
                wg, kwg = wgr.next()
                wu, kwu = wur.next()
                dma("pool", wg[:], wg_d[fc], writes=[kwg], flat=True)
                dma("pool", wu[:], wu_d[fc], writes=[kwu], flat=True)
                for tg in range(NG):
                    cols = slice(tg * 512, (tg + 1) * 512)
                    iA, iB = 2 * (it % 4), 2 * (it % 4) + 1
                    it += 1

                    def mm(e, w, b, cols=cols):
                        ins = None
                        for kc in range(KC):
                            ins = e.matmul(banks[b][:], lhsT=w[:, kc, :], rhs=hT[:, kc, cols], start=(kc == 0), stop=(kc == KC - 1))
                        return ins
                    S.op("pe", lambda e, wg=wg, iA=iA, mm=mm: mm(e, wg, iA), reads=[kwg], writes=[bk(iA)])
                    S.op("pe", lambda e, wu=wu, iB=iB, mm=mm: mm(e, wu, iB), reads=[kwu], writes=[bk(iB)])
                    sg, ksg = sgr.next()
                    at, kat = atr.next()
                    S.op("act", lambda e, sg=sg, iA=iA: e.activation(out=sg[:], in_=banks[iA][:], func=AF.Silu), reads=[bk(iA)], writes=[ksg])
                    S.op("dve", lambda e, at=at, sg=sg, iB=iB: e.tensor_tensor(out=at[:], in0=sg[:], in1=banks[iB][:], op=ALU.mult),
                         reads=[ksg, bk(iB)], writes=[kat])
                    dst = A1_d[tg * 4:(tg + 1) * 4, :, fc, :].rearrange("t p n -> p t n")
                    src = at[:].rearrange("p (t n) -> p t n", n=128)
                    dma("sp", dst, src, reads=[kat], writes=[("A1", fc, tg)])

        def dn_phase(sb, kcn, w_d, a_src, a_res=None):
            nq = 4
            per = kcn // nq
            wr = Ring(sb, "dn_w", 2, [128, kcn, 512], BF16)
            ar = Ring(sb, "dn_a", 3, [128, kcn, 128], BF16) if a_res is None else None
            ysr = Ring(sb, "dn_ys", 3, [128, 512], F32)
            junk = sb("dn_junk", [128, 512], BF16)
            S.op("pool", lambda e: e.memset(stat[:], 0.0), writes=["stat"])
            iters = [(dg, tt) for dg in range(4) for tt in range(NT)]
            loaded = {}

            def load_a(j):
                if a_res is not None or j >= len(iters) or j in loaded:
                    return
                at, ka = ar.next()
                dma("sp", at[:], a_src[iters[j][1]], writes=[ka], flat=True)
                loaded[j] = (at, ka)
            wts = {}

            def load_w(dg):
                if dg >= 4 or dg in wts:
                    return
                w, kw = wr.next()
                for q in range(nq):
                    dma("pool", w[:, q * per:(q + 1) * per, :], w_d[dg][:, q * per:(q + 1) * per, :], writes=[(kw, q)], flat=True)
                wts[dg] = (w, kw)
            load_w(0)
            load_a(0)
            load_a(1)
            for j, (dg, tt) in enumerate(iters):
                if tt == 0:
                    load_w(dg + 1)
                load_a(j + 2)
                w, kw = wts[dg]
                b = j % 4
                if a_res is None:
                    at, ka = loaded.pop(j)
                    lhs = lambda kc, at=at: at[:, kc, :]
                    rd = [ka]
                else:
                    lhs = lambda kc, tt=tt: a_res[:, kc, tt * 128:(tt + 1) * 128]
                    rd = []

                def mm(e, lhs=lhs, w=w, b=b):
                    ins = None
                    for kc in range(kcn):
                        ins = e.matmul(banks[b][:], lhsT=lhs(kc), rhs=w[:, kc, :], start=(kc == 0), stop=(kc == kcn - 1))
                    return ins
                S.op("pe", mm, reads=rd + [(kw, q) for q in range(nq)], writes=[bk(b)])
                c = tt * 4 + dg
                ys, kys = ysr.next()
                S.op("dve", lambda e, ys=ys, b=b: e.tensor_copy(out=ys[:], in_=banks[b][:]), reads=[bk(b)], writes=[kys])
                S.op("act", lambda e, ys=ys, c=c: e.activation(out=junk[:], in_=ys[:], func=AF.Square, accum_out=stat[:, c:c + 1]),
                     reads=[kys, "stat"], writes=["dn_junk", ("stat", c)])
                dma("sp", Y_d[tt * 128:(tt + 1) * 128, dg * 512:(dg + 1) * 512], ys[:], reads=[kys], writes=[("Y", tt, dg)])

        def ffn_block(idx, src_x, dst_x, final_pre):
            pass

        wg1, wu1, wd1 = ffn_w[0]
        wg2, wu2, wd2 = ffn_w[1]
        with ExitStack() as stA:
            hT = stA.enter_context(nc.sbuf_tensor("hT_a", [128, KC, S_TOK], BF16))
            if stage == 0:
                phase(lambda sb: rn_pass(sb, hT, x_d, None, None, 1.0, out_d, 0))
                return nc
            import os as _os
            if not _os.environ.get("K_SKIP_GU"):
                phase(lambda sb: rn_pass(sb, hT, x_d, None, None, 1.0, None, 0))
                phase(lambda sb: gu_phase(sb, hT, wg1, wu1))
            if stage == 0.5:
                phase(lambda sb: rn_pass(sb, hT, x_d, None, None, 1.0, out_d, None))
                return nc
        phase(lambda sb: dn_phase(sb, FC, wd1, A1_d))
        if stage == 0.75:
            phase(lambda sb: rn_pass(sb, None, x_d, None, None, 1.0, out_d, None))
            return nc
        if stage == 1:
            phase(lambda sb: rn_pass(sb, None, x_d, Y_d, 1, 0.5, out_d, None))
            return nc

        with ExitStack() as stCD:
            sbCD = lambda name, shape, dt: stCD.enter_context(nc.sbuf_tensor(uname(name), list(shape), dt))
            krT = sbCD("krT", [64, S_TOK], BF16)
            with ExitStack() as stC:
                hT = stC.enter_context(nc.sbuf_tensor("hT_c", [128, KC, S_TOK], BF16))
                phase(lambda sb: rn_pass(sb, hT, x_d, Y_d, 1, 0.5, X1_d, 2))

                def p_ma0(sb):
                    cos2 = sb("cos2a", [64, S_TOK], F32)
                    sin2 = sb("sin2a", [64, S_TOK], F32)
                    pos_i = sb("pos_i", [64, S_TOK], I32)
                    ang = sb("ang", [64, S_TOK], F32)
                    tq = sb("tq", [64, S_TOK], F32)
                    rr = sb("rr", [64, S_TOK], F32)
                    mm_ = sb("mm_", [64, S_TOK], F32)
                    ki = pos_i
                    kf = tq
                    dma("sp", pos_i[:], pos_d, writes=["pos_i"])
                    S.op("dve", lambda e: e.tensor_copy(out=ang[:], in_=pos_i[:]), reads=["pos_i"], writes=["ang"])
                    S.op("dve", lambda e: e.tensor_scalar(out=ang[:], in0=ang[:], scalar1=cst[:, 0:1], scalar2=None, op0=ALU.mult),
                         reads=["ang"], writes=["ang"])
                    for which in range(2):
                        shift = 0.0 if which == 0 else math.pi / 2
                        S.op("dve", lambda e, shift=shift: e.tensor_scalar(out=tq[:], in0=ang[:], scalar1=shift, scalar2=1.0 / (2 * math.pi),
                                                                            op0=ALU.add, op1=ALU.mult), reads=["ang"], writes=["tq"])
                        S.op("dve", lambda e: e.tensor_copy(out=ki[:], in_=tq[:]), reads=["tq", "pos_i"], writes=["pos_i"])
                        S.op("dve", lambda e: e.tensor_copy(out=kf[:], in_=ki[:]), reads=["pos_i"], writes=["tq"])
                        S.op("dve", lambda e: e.scalar_tensor_tensor(out=rr[:], in0=kf[:], scalar=-2 * math.pi, in1=ang[:],
                                                                     op0=ALU.mult, op1=ALU.add), reads=["tq", "ang"], writes=["rr"])
                        if which == 1:
                            S.op("dve", lambda e: e.tensor_scalar(out=rr[:], in0=rr[:], scalar1=math.pi / 2, scalar2=None, op0=ALU.add),
                                 reads=["rr"], writes=["rr"])
                        S.op("dve", lambda e: e.tensor_scalar(out=mm_[:], in0=rr[:], scalar1=0.0, scalar2=2 * math.pi,
                                                               op0=ALU.is_lt, op1=ALU.mult), reads=["rr"], writes=["mm_"])
                        S.op("dve", lambda e: e.tensor_tensor(out=rr[:], in0=rr[:], in1=mm_[:], op=ALU.add), reads=["rr", "mm_"], writes=["rr"])
                        S.op("dve", lambda e: e.tensor_scalar(out=rr[:], in0=rr[:], scalar1=0.0, scalar2=2 * math.pi,
                                                               op0=ALU.max, op1=ALU.min), reads=["rr"], writes=["rr"])
                        if which == 0:
                            S.op("act", lambda e: e.activation(out=sin2[:], in_=rr[:], func=AF.Sin, scale=cst[:, 1:2], bias=cst[:, 2:3]),
                                 reads=["rr"], writes=["sin2"])
                        else:
                            S.op("act", lambda e: e.activation(out=cos2[:], in_=rr[:], func=AF.Sin, scale=cst[:, 3:4], bias=cst[:, 4:5]),
                                 reads=["rr"], writes=["cos2"])
                    dma("sp", CS_d[0], cos2[:], reads=["cos2"], writes=["CS0"])
                    dma("sp", CS_d[1], sin2[:], reads=["sin2"], writes=["CS1"])
                phase(p_ma0)

                def p_ma(sb):
                    cos2 = sb("cos2b", [64, S_TOK], F32)
                    sin2 = sb("sin2b", [64, S_TOK], F32)
                    latT = sb("latTb", [128, 8, S_TOK], BF16)
                    dma("sp", cos2[:], CS_d[0], writes=["cos2"])
                    dma("sp", sin2[:], CS_d[1], writes=["sin2"])
                    glat = sb("glat_s", [128, 1024], F32)
                    dma("sp", glat[:], glat_d, writes=["glat"])
                    wr = Ring(sb, "ma_w", 2, [128, KC, 512], BF16)
                    scring = Ring(sb, "ma_sc", 4, [128, 4], F32)
                    cnr = Ring(sb, "ma_cn", 2, [128, 512], BF16)
                    junk = sb("ma_junk", [128, 512], BF16)
                    it = 0
                    for cg in range(2):
                        w, kw = wr.next()
                        dma("pool", w[:], wlat_d[cg], writes=[kw], flat=True)
                        for tt in range(NT):
                            b = it % 3
                            it += 1

                            def mm(e, w=w, b=b, tt=tt):
                                ins = None
                                for kc in range(KC):
                                    ins = e.matmul(banks[b][:], lhsT=hT[:, kc, tt * 128:(tt + 1) * 128], rhs=w[:, kc, :],
                                                   start=(kc == 0), stop=(kc == KC - 1))
                                return ins
                            S.op("pe", mm, reads=[kw], writes=[bk(b)])
                            sc, ksc = scring.next()
                            S.op("pool", lambda e, sc=sc: e.memset(sc[:], 0.0), writes=[(ksc, i) for i in range(4)])
                            S.op("act", lambda e, sc=sc, b=b: e.activation(out=junk[:], in_=banks[b][:], func=AF.Square, accum_out=sc[:, 0:1]),
                                 reads=[bk(b)], writes=["ma_junk", (ksc, 0)])
                            rstd_chain(sc, ksc, 0, 1, 1.0 / 512, EPS)
                            cn, kcn_ = cnr.next()
                            S.op("dve", lambda e, cn=cn, sc=sc, b=b, cg=cg: e.scalar_tensor_tensor(
                                out=cn[:], in0=banks[b][:], scalar=sc[:, 3:4], in1=glat[:, cg * 512:(cg + 1) * 512], op0=ALU.mult, op1=ALU.mult),
                                reads=[bk(b), (ksc, 3), "glat"], writes=[kcn_])
                            b2 = 3 + (it % 2)
                            bv = banks[b2][:].bitcast(BF16)

                            def tr(e, cn=cn, bv=bv):
                                ins = None
                                for j in range(4):
                                    ins = e.transpose(bv[:, j * 128:(j + 1) * 128], cn[:, j * 128:(j + 1) * 128], ident[:])
                                return ins
                            S.op("pe", tr, reads=[kcn_], writes=[bk(b2)])
                            dst = latT[:, cg * 4:(cg + 1) * 4, tt * 128:(tt + 1) * 128]
                            src = bv[:, 0:512].rearrange("p (a b) -> p a b", b=128)
                            S.op("act", lambda e, dst=dst, src=src: e.copy(out=dst, in_=src), reads=[bk(b2)], writes=[("latT", cg, tt)])
                    wk = sb("ma_wkr", [128, KC, 128], BF16)
                    dma("pool", wk[:], wkr_d, writes=["wkr"], flat=True)
                    t1r = Ring(sb, "ma_t1", 2, [64, 512], F32)
                    t2r = Ring(sb, "ma_t2", 2, [64, 512], F32)
                    for tg in range(NG):
                        cols = slice(tg * 512, (tg + 1) * 512)
                        bA, bB = 5, 6

                        def mmk(e, b, c0, cols=cols):
                            ins = None
                            for kc in range(KC):
                                ins = e.matmul(banks[b][0:64, :], lhsT=wk[:, kc, c0:c0 + 64], rhs=hT[:, kc, cols], start=(kc == 0), stop=(kc == KC - 1))
                            return ins
                        S.op("pe", lambda e, mmk=mmk: mmk(e, bA, 0), reads=["wkr"], writes=[bk(bA)])
                        S.op("pe", lambda e, mmk=mmk: mmk(e, bB, 64), reads=["wkr"], writes=[bk(bB)])
                        t1, k1 = t1r.next()
                        t2, k2 = t2r.next()
                        S.op("dve", lambda e, t1=t1, cols=cols: e.tensor_tensor(out=t1[:], in0=banks[bA][0:64, :], in1=cos2[:, cols], op=ALU.mult),
                             reads=[bk(bA), "cos2"], writes=[k1])
                        S.op("dve", lambda e, t2=t2, cols=cols: e.tensor_tensor(out=t2[:], in0=banks[bB][0:64, :], in1=sin2[:, cols], op=ALU.mult),
                             reads=[bk(bB), "sin2"], writes=[k2])
                        S.op("pool", lambda e, t1=t1, t2=t2, cols=cols: e.tensor_tensor(out=krT[:, cols], in0=t1[:], in1=t2[:], op=ALU.add),
                             reads=[k1, k2], writes=[("krT", tg)])
                    dma("sp", LAT_d, latT[:], reads=[("latT", cg, tt) for cg in range(2) for tt in range(NT)], writes=["LAT"], flat=True)
                phase(p_ma)

                def p_mb(sb):
                    wr = Ring(sb, "mb_w", 3, [128, KC, 128], BF16)
                    gsr = Ring(sb, "mb_gs", 2, [128, S_TOK], BF16)
                    it = 0
                    for oc in range(32):
                        w, kw = wr.next()
                        dma("pool", w[:], wgate_d[oc], writes=[kw], flat=True)
                        gs, kgs = gsr.next()
                        for tg in range(NG):
                            cols = slice(tg * 512, (tg + 1) * 512)
                            b = it % 4
                            it += 1

                            def mm(e, w=w, b=b, cols=cols):
                                ins = None
                                for kc in range(KC):
                                    ins = e.matmul(banks[b][:], lhsT=w[:, kc, :], rhs=hT[:, kc, cols], start=(kc == 0), stop=(kc == KC - 1))
                                return ins
                            S.op("pe", mm, reads=[kw], writes=[bk(b)])
                            S.op("act", lambda e, gs=gs, b=b, cols=cols: e.activation(out=gs[:, cols], in_=banks[b][:], func=AF.Sigmoid),
                                 reads=[bk(b)], writes=[(kgs, tg)])
                        dma("sp", GG_d[oc], gs[:], reads=[(kgs, tg) for tg in range(NG)], writes=[("GG", oc)])
                phase(p_mb)

                def p_mc(sb):
                    lb_bc = sb("lb_bc", [128, D], F32)
                    l1_bc = sb("l1_bc", [128, D], F32)
                    oml_bc = l1_bc
                    dma("sp", lb_bc[:], lbl_d[0], writes=["lb"])
                    dma("sp", l1_bc[:], lbl_d[1], writes=["l1"])
                    S.op("dve", lambda e: e.tensor_tensor(out=l1_bc[:], in0=lb_bc[:], in1=l1_bc[:], op=ALU.subtract), reads=["lb", "l1"], writes=["l1"])
                    S.op("act", lambda e: e.activation(out=lb_bc[:], in_=l1_bc[:], func=AF.Sigmoid), reads=["l1"], writes=["lb"])
                    S.op("dve", lambda e: e.tensor_scalar(out=oml_bc[:], in0=lb_bc[:], scalar1=-1.0, scalar2=1.0, op0=ALU.mult, op1=ALU.add),
                         reads=["lb", "l1"], writes=["oml", "l1"])
                    GW = 256
                    w3 = [sb(f"mc_w{i}", [128, KC, GW], BF16) for i in range(3)]
                    whgr = Ring(sb, "mc_whg", 2, [128, KC, 128], BF16)
                    qT = sb("mc_qT", [128, 2, S_TOK], BF16)
                    kT = sb("mc_kT", [128, 2, S_TOK], BF16)
                    kd = sb("mc_kd", [128, NT, GW], BF16)
                    vt = sb("mc_vt", [128, NT, GW], BF16)
                    dec = sb("mc_dec", [128, 2, 32], F32)
                    tmp = {n: Ring(sb, "mc_" + n, 2, [128, GW], F32) for n in ("qs", "f", "lf", "k", "e1", "e2", "e3")}
                    qtr = Ring(sb, "mc_qt", 2, [128, GW], BF16)
                    ktr = Ring(sb, "mc_kt", 2, [128, GW], BF16)
                    st32 = [sb(f"mc_st32_{h}", [128, 128], F32) for h in range(2)]
                    stbf = [sb(f"mc_stbf_{h}", [128, 128], BF16) for h in range(2)]
                    scmr = Ring(sb, "mc_scm", 4, [128, 128], BF16)
                    sqr = Ring(sb, "mc_sq", 2, [128, 512], BF16)
                    tr_ = Ring(sb, "mc_t", 1, [128, 512], F32)
                    t2r = Ring(sb, "mc_t2", 1, [128, 512], F32)
                    rsr = Ring(sb, "mc_rs", 1, [128, 512], F32)
                    sgr = Ring(sb, "mc_sg", 1, [128, 512], F32)
                    o1r = Ring(sb, "mc_o1", 1, [128, 512], F32)
                    ogr = Ring(sb, "mc_og", 2, [128, 512], BF16)
                    for g in range(8):
                        gcols = slice(g * GW, (g + 1) * GW)
                        for i in range(3):
                            dma("pool", w3[i][:], wh3_d[i, g], writes=[("w3", i)], flat=True)
                        whg = []
                        for hh in range(2):
                            w, kw = whgr.next()
                            dma("pool", w[:], whg_d[g * 2 + hh], writes=[kw], flat=True)
                            whg.append((w, kw))
                        for hh in range(2):
                            S.op("pool", lambda e, hh=hh: e.memset(st32[hh][:], 0.0), writes=[("st32", hh)])
                            S.op("pool", lambda e, hh=hh: e.memset(stbf[hh][:], 0.0), writes=[("stbf", hh)])
                        for tt in range(NT):
                            tcols = slice(tt * 128, (tt + 1) * 128)
                            for i in range(3):
                                def mm(e, i=i, tcols=tcols):
                                    ins = None
                                    for kc in range(KC):
                                        ins = e.matmul(banks[i][:, 0:GW], lhsT=hT[:, kc, tcols], rhs=w3[i][:, kc, :], start=(kc == 0), stop=(kc == KC - 1))
                                    return ins
                                S.op("pe", mm, reads=[("w3", i)], writes=[bk(i)])
                            qs, kqs = tmp["qs"].next()
                            f, kf_ = tmp["f"].next()
                            lf, klf = tmp["lf"].next()
                            k_, kk = tmp["k"].next()
                            e1, ke1 = tmp["e1"].next()
                            e2, ke2 = tmp["e2"].next()
                            e3, ke3 = tmp["e3"].next()
                            S.op("act", lambda e, qs=qs: e.activation(out=qs[:], in_=banks[0][:, 0:GW], func=AF.Silu), reads=[bk(0)], writes=[kqs])
                            S.op("act", lambda e, f=f: e.activation(out=f[:], in_=banks[1][:, 0:GW], func=AF.Sigmoid), reads=[bk(1)], writes=[kf_])
                            S.op("dve", lambda e, tt=tt: e.tensor_copy(out=vt[:, tt, :], in_=banks[2][:, 0:GW]), reads=[bk(2)], writes=[("vt", tt)])
                            S.op("dve", lambda e, f=f, gcols=gcols: e.tensor_tensor(out=f[:], in0=f[:], in1=oml_bc[:, gcols], op=ALU.mult),
                                 reads=[kf_, "oml"], writes=[kf_])
                            S.op("dve", lambda e, f=f, gcols=gcols: e.tensor_tensor(out=f[:], in0=f[:], in1=lb_bc[:, gcols], op=ALU.add),
                                 reads=[kf_, "lb"], writes=[kf_])
                            S.op("act", lambda e, f=f, lf=lf: e.activation(out=lf[:], in_=f[:], func=AF.Ln), reads=[kf_], writes=[klf])
                            S.op("pool", lambda e, f=f, k_=k_: e.tensor_scalar(out=k_[:], in0=f[:], scalar1=-1.0, scalar2=1.0, op0=ALU.mult, op1=ALU.add),
                                 reads=[kf_], writes=[kk])
                            S.op("pe", lambda e, lf=lf: e.matmul(banks[3][:, 0:GW], lhsT=tri[:], rhs=lf[:], start=True, stop=True),
                                 reads=[klf], writes=[bk(3)])
                            S.op("pe", lambda e, lf=lf: e.matmul(banks[4][:, 0:GW], lhsT=tris[:], rhs=lf[:], start=True, stop=True),
                                 reads=[klf], writes=[bk(4)])

                            def mmdec(e, lf=lf):
                                ins = None
                                for hh in range(2):
                                    ins = e.matmul(banks[6][:, hh * 2:hh * 2 + 2], lhsT=lf[:, hh * 128:(hh + 1) * 128], rhs=blk2[:], start=True, stop=True)
                                return ins
                            S.op("pe", mmdec, reads=[klf], writes=[bk(6)])
                            S.op("act", lambda e, e1=e1: e.activation(out=e1[:], in_=banks[3][:, 0:GW], func=AF.Exp), reads=[bk(3)], writes=[ke1])
                            S.op("act", lambda e, e2=e2: e.activation(out=e2[:], in_=banks[3][:, 0:GW], func=AF.Exp, scale=-1.0), reads=[bk(3)], writes=[ke2])
                            S.op("act", lambda e, e3=e3: e.activation(out=e3[:], in_=banks[4][:, 0:GW], func=AF.Exp), reads=[bk(4)], writes=[ke3])
                            S.op("act", lambda e, tt=tt: e.activation(out=dec[:, :, tt * 2:tt * 2 + 2],
                                                                      in_=banks[6][:, 0:4].rearrange("p (h c) -> p h c", c=2), func=AF.Exp),
                                 reads=[bk(6)], writes=[("dec", tt)])
                            qt, kqt = qtr.next()
                            kt, kkt = ktr.next()
                            S.op("dve", lambda e, qt=qt, qs=qs, e1=e1: e.tensor_tensor(out=qt[:], in0=qs[:], in1=e1[:], op=ALU.mult), reads=[kqs, ke1], writes=[kqt])
                            S.op("dve", lambda e, kt=kt, k_=k_, e2=e2: e.tensor_tensor(out=kt[:], in0=k_[:], in1=e2[:], op=ALU.mult), reads=[kk, ke2], writes=[kkt])
                            S.op("pool", lambda e, tt=tt, k_=k_, e3=e3: e.tensor_tensor(out=kd[:, tt, :], in0=k_[:], in1=e3[:], op=ALU.mult),
                                 reads=[kk, ke3], writes=[("kd", tt)])
                            bv = banks[5][:].bitcast(BF16)

                            def tr4(e, qt=qt, kt=kt, bv=bv):
                                ins = None
                                for hh in range(2):
                                    ins = e.transpose(bv[:, hh * 128:(hh + 1) * 128], qt[:, hh * 128:(hh + 1) * 128], ident[:])
                                for hh in range(2):
                                    ins = e.transpose(bv[:, 256 + hh * 128:256 + (hh + 1) * 128], kt[:, hh * 128:(hh + 1) * 128], ident[:])
                                return ins
                            S.op("pe", tr4, reads=[kqt, kkt], writes=[bk(5)])
                            S.op("act", lambda e, bv=bv, tcols=tcols: e.copy(out=qT[:, :, tcols], in_=bv[:, 0:256].rearrange("p (h c) -> p h c", c=128)),
                                 reads=[bk(5)], writes=[("qT", tt)])
                            S.op("act", lambda e, bv=bv, tcols=tcols: e.copy(out=kT[:, :, tcols], in_=bv[:, 256:512].rearrange("p (h c) -> p h c", c=128)),
                                 reads=[bk(5)], writes=[("kT", tt)])
                        for tt in range(NT):
                            tok0 = tt * 128
                            q4 = (tt % 4) * 128
                            scms = []
                            for hh in range(2):
                                S.op("pe", lambda e, hh=hh, tok0=tok0: e.matmul(banks[hh][:, 0:128], lhsT=kT[:, hh, tok0:tok0 + 128],
                                                                                 rhs=qT[:, hh, tok0:tok0 + 128], start=True, stop=True),
                                     reads=[("qT", tt), ("kT", tt)], writes=[bk(hh)])
                            for hh in range(2):
                                scm, kscm = scmr.next()
                                scms.append((scm, kscm))
                                S.op("dve", lambda e, scm=scm, hh=hh: e.tensor_tensor(out=scm[:], in0=banks[hh][:, 0:128], in1=tri[:], op=ALU.mult),
                                     reads=[bk(hh)], writes=[kscm])
                            for half in range(2):
                                prow = slice(half * 64, (half + 1) * 64)
                                for hh in range(2):
                                    scm, kscm = scms[hh]
                                    hc = slice(hh * 128, (hh + 1) * 128)
                                    oc_ = slice(q4 + half * 64, q4 + (half + 1) * 64)
                                    qc = slice(tok0 + half * 64, tok0 + (half + 1) * 64)

                                    def mmo(e, hh=hh, scm=scm, hc=hc, oc_=oc_, qc=qc, half=half, tt=tt):
                                        e.matmul(banks[2 + hh][:, oc_], lhsT=stbf[hh][:], rhs=qT[:, hh, qc], start=True, stop=False)
                                        return e.matmul(banks[2 + hh][:, oc_], lhsT=vt[:, tt, hc], rhs=scm[:, half * 64:(half + 1) * 64], start=False, stop=True)
                                    S.op("pe", mmo, reads=[("stbf", hh), kscm, ("qT", tt), ("vt", tt)], writes=[bk(2 + hh)])
                                for hh in range(2):
                                    hc = slice(hh * 128, (hh + 1) * 128)
                                    S.op("pe", lambda e, hh=hh, hc=hc, prow=prow, tt=tt: e.matmul(banks[6 + hh][:, 0:128], lhsT=kd[prow, tt, hc], rhs=vt[prow, tt, hc],
                                                                                                   start=True, stop=True),
                                         reads=[("kd", tt), ("vt", tt)], writes=[bk(6 + hh)])
                                for hh in range(2):
                                    hc = slice(hh * 128, (hh + 1) * 128)
                                    ci = tt * 2 + half
                                    S.op("dve", lambda e, hh=hh, hc=hc, ci=ci: e.scalar_tensor_tensor(out=st32[hh][:], in0=st32[hh][:], scalar=dec[:, hh, ci:ci + 1],
                                                                                                       in1=banks[6 + hh][:, 0:128], op0=ALU.mult, op1=ALU.add),
                                         reads=[bk(6 + hh), ("st32", hh), ("dec", tt)], writes=[("st32", hh)])
                                    S.op("act", lambda e, hh=hh: e.copy(out=stbf[hh][:], in_=st32[hh][:]), reads=[("st32", hh)], writes=[("stbf", hh)])
                            if tt % 4 == 3:
                                tg = tt // 4
                                cols = slice(tg * 512, (tg + 1) * 512)
                                for hh in range(2):
                                    head = g * 2 + hh
                                    w, kw = whg[hh]
                                    sq, ksq = sqr.next()
                                    S.op("act", lambda e, sq=sq, hh=hh: e.activation(out=sq[:], in_=banks[2 + hh][:], func=AF.Square), reads=[bk(2 + hh)], writes=[ksq])
                                    S.op("pe", lambda e, sq=sq: e.matmul(banks[4][:], lhsT=ones_bf[:], rhs=sq[:], start=True, stop=True), reads=[ksq], writes=[bk(4)])
                                    t_, kt_ = tr_.next()
                                    t2, kt2 = t2r.next()
                                    rs, krs = rsr.next()
                                    S.op("dve", lambda e, t_=t_: e.tensor_scalar(out=t_[:], in0=banks[4][:], scalar1=1.0 / 128, scalar2=EPS, op0=ALU.mult, op1=ALU.add),
                                         reads=[bk(4)], writes=[kt_])
                                    S.op("act", lambda e, t_=t_, t2=t2: e.activation(out=t2[:], in_=t_[:], func=AF.Sqrt), reads=[kt_], writes=[kt2])
                                    S.op("dve", lambda e, rs=rs, t2=t2: e.reciprocal(out=rs[:], in_=t2[:]), reads=[kt2], writes=[krs])

                                    def mmg(e, w=w, cols=cols):
                                        ins = None
                                        for kc in range(KC):
                                            ins = e.matmul(banks[5][:], lhsT=w[:, kc, :], rhs=hT[:, kc, cols], start=(kc == 0), stop=(kc == KC - 1))
                                        return ins
                                    S.op("pe", mmg, reads=[kw], writes=[bk(5)])
                                    sg, ksg = sgr.next()
                                    o1, ko1 = o1r.next()
                                    og, kog = ogr.next()
                                    S.op("act", lambda e, sg=sg: e.activation(out=sg[:], in_=banks[5][:], func=AF.Silu), reads=[bk(5)], writes=[ksg])
                                    S.op("dve", lambda e, o1=o1, rs=rs, hh=hh: e.scalar_tensor_tensor(out=o1[:], in0=banks[2 + hh][:], scalar=gout[:, 0:1], in1=rs[:],
                                                                                                       op0=ALU.mult, op1=ALU.mult),
                                         reads=[bk(2 + hh), krs], writes=[ko1])
                                    S.op("pool", lambda e, og=og, o1=o1, sg=sg: e.tensor_tensor(out=og[:], in0=o1[:], in1=sg[:], op=ALU.mult), reads=[ko1, ksg], writes=[kog])
                                    dma("sp", OG_d[head][:, cols], og[:], reads=[kog], writes=[("OG", head, tg)])
                phase(p_mc)

            def p_md(sb):
                cos2 = sb("cos2d", [64, S_TOK], F32)
                sin2 = sb("sin2d", [64, S_TOK], F32)
                latT = sb("latTd", [128, 8, S_TOK], BF16)
                dma("sp", cos2[:], CS_d[0], writes=["cos2"])
                dma("sp", sin2[:], CS_d[1], writes=["sin2"])
                dma("sp", latT[:], LAT_d, writes=["latT"], flat=True)
                wait_all(["cos2", "sin2", "latT"])
                wqr = Ring(sb, "md_wq", 2, [128, 4, 256], BF16)
                wknr = Ring(sb, "md_wkn", 2, [128, 4, 128], BF16)
                wvr = Ring(sb, "md_wv", 2, [128, 4, 512], BF16)
                qnr = Ring(sb, "md_qn", 2, [128, S_TOK], BF16)
                qrr = Ring(sb, "md_qr", 2, [64, S_TOK], BF16)
                knr = Ring(sb, "md_kn", 2, [128, S_TOK], BF16)
                vtr = Ring(sb, "md_vt", 2, [128, NT, 512], BF16)
                ptr = Ring(sb, "md_pt", 4, [128, 512], BF16)
                t1r = Ring(sb, "md_t1", 2, [64, 512], F32)
                t2r = Ring(sb, "md_t2", 2, [64, 512], F32)
                rsr = Ring(sb, "md_rs", 2, [128, 512], F32)
                osr = Ring(sb, "md_os", 2, [128, 512], BF16)
                zi = 0
                ai = 0
                vt = kvt = None
                for h in range(NH):
                    wq, kwq = wqr.next()
                    wkn, kwkn = wknr.next()
                    dma("pool", wq[:], wq_d[h], writes=[kwq], flat=True)
                    dma("pool", wkn[:], wkn_d[h], writes=[kwkn], flat=True)
                    if h % 4 == 0:
                        wv, kwv = wvr.next()
                        dma("pool", wv[:], wv_d[h // 4], writes=[kwv], flat=True)
                        vt, kvt = vtr.next()
                        for tt in range(NT):
                            b = zi % 4
                            zi += 1

                            def mmv(e, b=b, tt=tt, wv=wv):
                                ins = None
                                for kc in range(4):
                                    ins = e.matmul(banks[b][:], lhsT=latT[:, 4 + kc, tt * 128:(tt + 1) * 128], rhs=wv[:, kc, :], start=(kc == 0), stop=(kc == 3))
                                return ins
                            S.op("pe", mmv, reads=[kwv], writes=[bk(b)])
                            if tt % 2 == 0:
                                S.op("act", lambda e, vt=vt, b=b, tt=tt: e.copy(out=vt[:, tt, :], in_=banks[b][:]), reads=[bk(b)], writes=[(kvt, tt)])
                            else:
                                S.op("dve", lambda e, vt=vt, b=b, tt=tt: e.tensor_copy(out=vt[:, tt, :], in_=banks[b][:]), reads=[bk(b)], writes=[(kvt, tt)])
                    hh = h % 4
                    qn, kqn = qnr.next()
                    qr, kqr = qrr.next()
                    kn, kkn = knr.next()
                    for tg in range(NG):
                        cols = slice(tg * 512, (tg + 1) * 512)
                        b = zi % 4
                        zi += 1

                        def mmq(e, b=b, wq=wq, cols=cols):
                            ins = None
                            for kc in range(4):
                                ins = e.matmul(banks[b][:], lhsT=wq[:, kc, 0:128], rhs=latT[:, kc, cols], start=(kc == 0), stop=(kc == 3))
                            return ins
                        S.op("pe", mmq, reads=[kwq], writes=[bk(b)])
                        S.op("act", lambda e, qn=qn, b=b, cols=cols: e.copy(out=qn[:, cols], in_=banks[b][:]), reads=[bk(b)], writes=[(kqn, tg)])
                        b = zi % 4
                        zi += 1

                        def mmk(e, b=b, wkn=wkn, cols=cols):
                            ins = None
                            for kc in range(4):
                                ins = e.matmul(banks[b][:], lhsT=wkn[:, kc, :], rhs=latT[:, 4 + kc, cols], start=(kc == 0), stop=(kc == 3))
                            return ins
                        S.op("pe", mmk, reads=[kwkn], writes=[bk(b)])
                        S.op("dve", lambda e, kn=kn, b=b, cols=cols: e.tensor_copy(out=kn[:, cols], in_=banks[b][:]), reads=[bk(b)], writes=[(kkn, tg)])
                        bA = zi % 4
                        zi += 1
                        bB = zi % 4
                        zi += 1

                        def mmr(e, b, c0, wq=wq, cols=cols):
                            ins = None
                            for kc in range(4):
                                ins = e.matmul(banks[b][0:64, :], lhsT=wq[:, kc, c0:c0 + 64], rhs=latT[:, kc, cols], start=(kc == 0), stop=(kc == 3))
                            return ins
                        S.op("pe", lambda e, mmr=mmr, bA=bA: mmr(e, bA, 128), reads=[kwq], writes=[bk(bA)])
                        S.op("pe", lambda e, mmr=mmr, bB=bB: mmr(e, bB, 192), reads=[kwq], writes=[bk(bB)])
                        t1, k1 = t1r.next()
                        t2, k2 = t2r.next()
                        S.op("dve", lambda e, t1=t1, bA=bA, cols=cols: e.tensor_tensor(out=t1[:], in0=banks[bA][0:64, :], in1=cos2[:, cols], op=ALU.mult),
                             reads=[bk(bA)], writes=[k1])
                        S.op("dve", lambda e, t2=t2, bB=bB, cols=cols: e.tensor_tensor(out=t2[:], in0=banks[bB][0:64, :], in1=sin2[:, cols], op=ALU.mult),
                             reads=[bk(bB)], writes=[k2])
                        S.op("pool", lambda e, qr=qr, t1=t1, t2=t2, cols=cols: e.tensor_tensor(out=qr[:, cols], in0=t1[:], in1=t2[:], op=ALU.add),
                             reads=[k1, k2], writes=[(kqr, tg)])
                    allq = [(kqn, t) for t in range(NG)] + [(kqr, t) for t in range(NG)] + [(kkn, t) for t in range(NG)]
                    for qg in range(NG):
                        bO = 4 + 2 * (ai % 2)
                        bS = bO + 1
                        ai += 1
                        nkb = 4 * (qg + 1)
                        for kb in range(nkb):
                            i = kb - 4 * qg
                            c0 = max(i, 0) * 128
                            kcols = slice(kb * 128, (kb + 1) * 128)
                            qcols = slice(qg * 512 + c0, (qg + 1) * 512)
                            b = zi % 4
                            zi += 1

                            def mms(e, b=b, c0=c0, kcols=kcols, qcols=qcols, qn=qn, kn=kn, qr=qr):
                                e.matmul(banks[b][:, c0:512], lhsT=kn[:, kcols], rhs=qn[:, qcols], start=True, stop=False)
                                return e.matmul(banks[b][:, c0:512], lhsT=krT[:, kcols], rhs=qr[:, qcols], start=False, stop=True)
                            S.op("pe", mms, reads=allq, writes=[bk(b)])
                            pt, kpt = ptr.next()
                            S.op("act", lambda e, pt=pt, b=b, c0=c0: e.activation(out=pt[:, c0:512], in_=banks[b][:, c0:512], func=AF.Exp, scale=ATT_SCALE),
                                 reads=[bk(b)], writes=[kpt])
                            if i >= 0:
                                S.op("dve", lambda e, pt=pt, c0=c0: e.tensor_tensor(out=pt[:, c0:c0 + 128], in0=pt[:, c0:c0 + 128], in1=caus[:], op=ALU.mult),
                                     reads=[kpt], writes=[kpt])

                            def mmpv(e, pt=pt, c0=c0, kb=kb, nkb=nkb, bO=bO, bS=bS, vt=vt, hh=hh):
                                e.matmul(banks[bO][:, c0:512], lhsT=vt[:, kb, hh * 128:(hh + 1) * 128], rhs=pt[:, c0:512], start=(kb == 0), stop=(kb == nkb - 1),
                                         skip_group_check=True)
                                return e.matmul(banks[bS][:, c0:512], lhsT=ones_bf[:], rhs=pt[:, c0:512], start=(kb == 0), stop=(kb == nkb - 1),
                                                skip_group_check=True)
                            S.op("pe", mmpv, reads=[kpt, (kvt, kb)], writes=[bk(bO), bk(bS)])
                        rs, krs = rsr.next()
                        os_, kos = osr.next()
                        S.op("dve", lambda e, rs=rs, bS=bS: e.reciprocal(out=rs[:], in_=banks[bS][:]), reads=[bk(bS)], writes=[krs])
                        S.op("dve", lambda e, os_=os_, rs=rs, bO=bO: e.tensor_tensor(out=os_[:], in0=banks[bO][:], in1=rs[:], op=ALU.mult),
                             reads=[bk(bO), krs], writes=[kos])
                        dma("sp", OA_d[h][:, qg * 512:(qg + 1) * 512], os_[:], reads=[kos], writes=[("OA", h, qg)])
            phase(p_md)
        def p_me(sb):
            OAs = sb("me_oa", [128, 16, S_TOK], BF16)
            OGs = sb("me_og", [128, 16, S_TOK], BF16)
            for h in range(16):
                dma("sp", OAs[:, h, :], OA_d[h], writes=[("oas", h)])
                dma("sp", OGs[:, h, :], OG_d[h], writes=[("ogs", h)])
            allo = [("oas", h) for h in range(16)] + [("ogs", h) for h in range(16)]
            war = Ring(sb, "me_wa", 2, [128, KC, 128], BF16)
            wbr = Ring(sb, "me_wb", 2, [128, KC, 128], BF16)
            gar = Ring(sb, "me_ga", 2, [128, S_TOK], BF16)
            gbr = Ring(sb, "me_gb", 2, [128, S_TOK], BF16)
            m1r = Ring(sb, "me_m1", 2, [128, 512], F32)
            m2r = Ring(sb, "me_m2", 2, [128, 512], F32)
            mgr = Ring(sb, "me_mg", 2, [128, S_TOK], BF16)
            it = 0
            pre = {}

            def load_dc(dc):
                if dc >= 16 or dc in pre:
                    return
                wa, kwa = war.next()
                wb, kwb = wbr.next()
                ga, kga = gar.next()
                gb, kgb = gbr.next()
                dma("pool", wa[:], wo_d[dc], writes=[kwa], flat=True)
                dma("pool", wb[:], wob_d[dc], writes=[kwb], flat=True)
                dma("sp", ga[:], GG_d[dc], writes=[kga])
                dma("sp", gb[:], GG_d[16 + dc], writes=[kgb])
                pre[dc] = (wa, kwa, wb, kwb, ga, kga, gb, kgb)
            load_dc(0)
            for dc in range(16):
                load_dc(dc + 1)
                wa, kwa, wb, kwb, ga, kga, gb, kgb = pre.pop(dc)
                mg, kmg = mgr.next()
                for tg in range(NG):
                    cols = slice(tg * 512, (tg + 1) * 512)
                    iA, iB = 2 * (it % 4), 2 * (it % 4) + 1
                    it += 1

                    def mm(e, w, src, b, cols=cols):
                        ins = None
                        for kc in range(KC):
                            ins = e.matmul(banks[b][:], lhsT=w[:, kc, :], rhs=src[:, kc, cols], start=(kc == 0), stop=(kc == KC - 1))
                        return ins
                    S.op("pe", lambda e, mm=mm, wa=wa, iA=iA: mm(e, wa, OAs, iA), reads=[kwa] + allo, writes=[bk(iA)])
                    S.op("pe", lambda e, mm=mm, wb=wb, iB=iB: mm(e, wb, OGs, iB), reads=[kwb] + allo, writes=[bk(iB)])
                    m1, km1 = m1r.next()
                    m2, km2 = m2r.next()
                    S.op("dve", lambda e, m1=m1, iA=iA, ga=ga, cols=cols: e.tensor_tensor(out=m1[:], in0=banks[iA][:], in1=ga[:, cols], op=ALU.mult),
                         reads=[bk(iA), kga], writes=[km1])
                    S.op("dve", lambda e, m2=m2, iB=iB, gb=gb, cols=cols: e.tensor_tensor(out=m2[:], in0=banks[iB][:], in1=gb[:, cols], op=ALU.mult),
                         reads=[bk(iB), kgb], writes=[km2])
                    S.op("pool", lambda e, mg=mg, m1=m1, m2=m2, cols=cols: e.tensor_tensor(out=mg[:, cols], in0=m1[:], in1=m2[:], op=ALU.add),
                         reads=[km1, km2], writes=[(kmg, tg)])
                dma("sp", MT_d[dc], mg[:], reads=[(kmg, tg) for tg in range(NG)], writes=[("MT", dc)])
        phase(p_me)

        def p_mf(sb):
            mT = sb("mf_mT", [128, 16, S_TOK], BF16)
            for dc in range(16):
                dma("sp", mT[:, dc, :], MT_d[dc], writes=[("mT", dc)])
            S.op("pe", lambda e: e.nop(), reads=[("mT", dc) for dc in range(16)])
            dn_phase(sb, KC, wout_d, None, a_res=mT)
        phase(p_mf)
        if stage == 2:
            phase(lambda sb: rn_pass(sb, None, X1_d, Y_d, 3, 1.0, out_d, None))
            return nc

        with ExitStack() as stE:
            hT = stE.enter_context(nc.sbuf_tensor("hT_e", [128, KC, S_TOK], BF16))
            phase(lambda sb: rn_pass(sb, hT, X1_d, Y_d, 3, 1.0, X2_d, 4))
            phase(lambda sb: gu_phase(sb, hT, wg2, wu2))
        phase(lambda sb: dn_phase(sb, FC, wd2, A1_d))
        phase(lambda sb: rn_pass(sb, None, X2_d, Y_d, 5, 0.5, out_d, None))
    return nc


def _fm(W, kc):
    K, N = W.shape
    return np.ascontiguousarray(W.reshape(kc, 128, N // 128, 128).transpose(2, 1, 0, 3))


def _tm(W, kc, n=512):
    K, N = W.shape
    return np.ascontiguousarray(W.reshape(kc, 128, N // n, n).transpose(2, 1, 0, 3))


def _prep_shared(inp):
    f32 = np.float32
    sh = {}
    gains = [inp[k][0] for k in ("ffn1_norm_pre", "ffn1_norm_post", "mix_norm_pre", "mix_norm_post", "ffn2_norm_pre", "ffn2_norm_post")]
    sh["gains"] = np.ascontiguousarray(np.broadcast_to(np.stack(gains)[:, None, :], (6, 128, D))).astype(f32, copy=False)
    glat = np.concatenate([inp["mla_q_norm"][0], inp["mla_kv_norm"][0]])
    sh["glat"] = np.ascontiguousarray(np.broadcast_to(glat[None, :], (128, 1024)))
    sh["lbl"] = np.ascontiguousarray(np.broadcast_to(inp["hgrn_lb_logits"][:, None, :], (2, 128, D)))
    sh["gout"] = np.ascontiguousarray(inp["hgrn_out_norm"][0].reshape(128, 1))
    sh["ident"] = np.eye(128, dtype=f32)
    s = np.arange(128)[:, None]
    t = np.arange(128)[None, :]
    same = (s // 64) == (t // 64)
    sh["tri"] = (same & (s <= t)).astype(f32)
    sh["tris"] = (same & (s > t)).astype(f32)
    sh["caus"] = (t >= s).astype(f32)
    sh["blk2"] = ((np.arange(128)[:, None] // 64) == np.arange(2)[None, :]).astype(f32)
    cst = np.zeros((64, 8), f32)
    half = 32
    invf = (10000.0 ** (-(np.arange(half, dtype=np.float32)) / half)).astype(f32)
    cst[:, 0] = np.concatenate([invf, invf])
    cst[:32, 1] = 1.0
    cst[32:, 1] = -1.0
    cst[:32, 2] = -math.pi
    cst[32:, 2] = math.pi
    cst[:, 3] = -1.0
    cst[:, 4] = math.pi
    sh["cst"] = cst
    for i in (1, 2):
        sh[f"wg{i}"] = _fm(inp[f"ffn{i}_w_gate"][0], KC)
        sh[f"wu{i}"] = _fm(inp[f"ffn{i}_w_up"][0], KC)
        sh[f"wd{i}"] = _tm(inp[f"ffn{i}_w_down"][0], FC)
    w_in = inp["w_in"][0]
    o = 0
    sh["wlat"] = _tm(w_in[:, 0:1024], KC)
    kr = w_in[:, 1024:1088]
    krs = np.concatenate([kr[:, 32:64], kr[:, 0:32]], axis=1)
    sh["wkr"] = np.ascontiguousarray(np.concatenate([kr, krs], axis=1).reshape(KC, 128, 128).transpose(1, 0, 2))
    o = 1088
    hq = w_in[:, o:o + 2048]; hf = w_in[:, o + 2048:o + 4096]; hi = w_in[:, o + 4096:o + 6144]; hg = w_in[:, o + 6144:o + 8192]
    ga = w_in[:, o + 8192:o + 10240]; gb = w_in[:, o + 10240:o + 12288]
    sh["wh3"] = np.stack([_tm(hq, KC, 256), _tm(hf, KC, 256), _tm(hi, KC, 256)])
    sh["whg"] = _fm(hg, KC)
    sh["wgate"] = np.concatenate([_fm(ga, KC), _fm(gb, KC)], axis=0)
    wq = inp["mla_w_q_up"][0].reshape(512, 16, 192)
    qn = wq[:, :, 0:128]; qr = wq[:, :, 128:192]
    qrs = np.concatenate([qr[:, :, 32:64], qr[:, :, 0:32]], axis=2)
    wq_all = np.concatenate([qn, qr, qrs], axis=2)
    sh["wq"] = np.ascontiguousarray(wq_all.reshape(4, 128, 16, 256).transpose(2, 1, 0, 3))
    wkv = inp["mla_w_kv_up"][0].reshape(512, 16, 256)
    sh["wkn"] = np.ascontiguousarray(wkv[:, :, 0:128].reshape(4, 128, 16, 128).transpose(2, 1, 0, 3))
    wv = wkv[:, :, 128:256].reshape(512, 4, 512)
    sh["wv"] = np.ascontiguousarray(wv.reshape(4, 128, 4, 512).transpose(2, 1, 0, 3))
    sh["wo"] = _fm(inp["mla_w_o"][0], KC)
    sh["wob"] = _fm(inp["hgrn_w_o"][0], KC)
    sh["wout"] = _tm(inp["w_out"][0], KC)
    return {k: np.ascontiguousarray(v, dtype=np.float32) for k, v in sh.items()}


def run(inputs, n_cores=8, stage=3, trace=False):
    inp = {k: np.asarray(v) for k, v in inputs.items()}
    sh = _prep_shared(inp)
    nc = build(stage)
    in_maps = []
    for b in range(n_cores):
        m = dict(sh)
        m["x"] = np.ascontiguousarray(inp["x"][b], dtype=np.float32)
        m["pos"] = np.ascontiguousarray(np.broadcast_to(inp["positions"][b][None, :], (64, S_TOK))).astype(np.int32, copy=False)
        in_maps.append(m)
    res = run_bass_kernel_spmd(nc, in_maps, core_ids=list(range(n_cores)), **({"trace": True} if trace else {}))
    out = np.stack([np.asarray(r["out"]) for r in res.results], axis=0)
    return out, res


def kernel(**inputs):
    out, _ = run(inputs, n_cores=8, stage=3)
    return out.astype(np.float32, copy=False)
```

```python
import math
from contextlib import ExitStack
import numpy as np
import concourse.bass as bass
import concourse.mybir as mybir
from concourse.bass_utils import run_bass_kernel_spmd

F32 = mybir.dt.float32
BF16 = mybir.dt.bfloat16
I32 = mybir.dt.int32
AF = mybir.ActivationFunctionType
ALU = mybir.AluOpType
AX = mybir.AxisListType

S_TOK = 2048
D = 2048
KC = 16
FF = 5632
FC = 44
NT = 16
NG = 4
EPS = 1e-6
NH = 16
ATT_SCALE = 192.0 ** -0.5


class Op:
    __slots__ = ("eng", "fn", "deps", "sig", "sigval", "dma_sem")


class Sched:
    ENGS = ("pe", "act", "dve", "pool", "sp")

    def __init__(self, nc, stack, n_dma_sems=8):
        self.nc = nc
        self.ops = {e: [] for e in self.ENGS}
        self.last_w = {}
        self.readers = {}
        self.sems = {e: stack.enter_context(nc.semaphore("s_" + e)) for e in ("pe", "act", "dve", "pool")}
        self.dma_sems = {}
        self.dma_cnt = {}
        self.dma_ring = {}
        for q in ("sp", "pool"):
            self.dma_sems[q] = [stack.enter_context(nc.semaphore(f"d_{q}{i}")) for i in range(n_dma_sems)]
            self.dma_cnt[q] = 0
            self.dma_ring[q] = [None] * n_dma_sems
        self.emitted = {e: 0 for e in self.ENGS}
        self.sigcnt = {e: 0 for e in self.ENGS}
        self.waited = {e: {} for e in self.ENGS}

    def op(self, eng, fn, reads=(), writes=(), dma=False):
        o = Op()
        o.eng = eng; o.fn = fn; o.sig = False; o.sigval = None; o.dma_sem = None
        deps = set()
        for r in reads:
            w = self.last_w.get(r)
            if w is not None:
                deps.add(w)
        for w_ in writes:
            w = self.last_w.get(w_)
            if w is not None:
                deps.add(w)
            for rd in self.readers.get(w_, ()):
                deps.add(rd)
        if dma:
            n = len(self.dma_sems[eng])
            i = self.dma_cnt[eng]
            slot = i % n
            prev = self.dma_ring[eng][slot]
            if prev is not None:
                deps.add(prev)
            o.dma_sem = (self.dma_sems[eng][slot], 16 * (i // n + 1))
            self.dma_ring[eng][slot] = o
            self.dma_cnt[eng] = i + 1
        o.deps = deps
        for d in deps:
            if d.dma_sem is None:
                d.sig = True
        for r in reads:
            self.readers.setdefault(r, []).append(o)
        for w_ in writes:
            self.last_w[w_] = o
            self.readers[w_] = []
        self.ops[eng].append(o)
        return o

    def finish_phase(self):
        lasts = []
        for e in ("pe", "act", "dve", "pool"):
            if len(self.ops[e]) > self.emitted[e]:
                lasts.append(self.ops[e][-1])
        for q in self.dma_ring:
            for o in self.dma_ring[q]:
                if o is not None:
                    lasts.append(o)
        for e in self.ENGS:
            o = self.op(e, lambda eng: eng.nop())
            o.deps = set(lasts)
            for d in o.deps:
                if d.dma_sem is None:
                    d.sig = True
        self.last_w = {}
        self.readers = {}

    def emit(self, block):
        for e in ("pe", "act", "dve", "pool"):
            c = self.sigcnt[e]
            for o in self.ops[e][self.emitted[e]:]:
                if o.dma_sem is None and o.sig:
                    c += 1
                    o.sigval = c
            self.sigcnt[e] = c

        def run_engine(ename, eng):
            waited = self.waited[ename]
            for o in self.ops[ename][self.emitted[ename]:]:
                for d in o.deps:
                    if d.dma_sem is not None:
                        sem, val = d.dma_sem
                    else:
                        if d.eng == "pe" and ename == "pe":
                            continue
                        sem, val = self.sems[d.eng], d.sigval
                    key = id(sem)
                    if waited.get(key, 0) >= val:
                        continue
                    waited[key] = val
                    eng.wait_ge(sem, val)
                ins = o.fn(eng)
                if o.dma_sem is not None:
                    ins.then_inc(o.dma_sem[0], 16)
                elif o.sig:
                    ins.then_inc(self.sems[ename], 1)
                o.fn = None
            self.emitted[ename] = len(self.ops[ename])

        @block.tensor
        def _(eng):
            run_engine("pe", eng)

        @block.scalar
        def _(eng):
            run_engine("act", eng)

        @block.vector
        def _(eng):
            run_engine("dve", eng)

        @block.gpsimd
        def _(eng):
            run_engine("pool", eng)

        @block.sync
        def _(eng):
            run_engine("sp", eng)


class Ring:
    def __init__(self, sb, name, n, shape, dtype):
        self.tiles = [sb(f"{name}{i}", shape, dtype) for i in range(n)]
        self.name = name
        self.i = 0

    def next(self):
        j = self.i % len(self.tiles)
        self.i += 1
        return self.tiles[j], (self.name, j)


def build(stage=3):
    nc = bass.Bass("TRN2", target_bir_lowering=False)
    dt_in = lambda name, shape, dt=F32: nc.dram_tensor(name, list(shape), dt, kind="ExternalInput").ap()
    scr = lambda name, shape, dt: nc.dram_tensor(name, list(shape), dt).ap()
    x_d = dt_in("x", [S_TOK, D])
    pos_d = dt_in("pos", [64, S_TOK], I32)
    gains_d = dt_in("gains", [6, 128, D])
    glat_d = dt_in("glat", [128, 1024])
    lbl_d = dt_in("lbl", [2, 128, D])
    gout_d = dt_in("gout", [128, 1])
    ident_d = dt_in("ident", [128, 128])
    tri_d = dt_in("tri", [128, 128])
    tris_d = dt_in("tris", [128, 128])
    caus_d = dt_in("caus", [128, 128])
    blk2_d = dt_in("blk2", [128, 2])
    cst_d = dt_in("cst", [64, 8])
    ffn_w = []
    for i in (1, 2):
        ffn_w.append((dt_in(f"wg{i}", [FC, 128, KC, 128]), dt_in(f"wu{i}", [FC, 128, KC, 128]),
                      dt_in(f"wd{i}", [4, 128, FC, 512])))
    wlat_d = dt_in("wlat", [2, 128, KC, 512])
    wkr_d = dt_in("wkr", [128, KC, 128])
    wgate_d = dt_in("wgate", [32, 128, KC, 128])
    wqf_d = dt_in("wqf", [8, 128, KC, 512])
    wi4_d = dt_in("wi4", [4, 128, KC, 512])
    whg_d = dt_in("whg", [16, 128, KC, 128])
    wq_d = dt_in("wq", [16, 128, 4, 256])
    wkn_d = dt_in("wkn", [16, 128, 4, 128])
    wv_d = dt_in("wv", [4, 128, 4, 512])
    wo_d = dt_in("wo", [16, 128, KC, 128])
    wob_d = dt_in("wob", [16, 128, KC, 128])
    wout_d = dt_in("wout", [4, 128, KC, 512])
    out_d = nc.dram_tensor("out", [S_TOK, D], F32, kind="ExternalOutput").ap()
    A1_d = scr("A1", [NT, 128, FC, 128], BF16)
    Y_d = scr("Y", [S_TOK, D], F32)
    X1_d = scr("X1", [S_TOK, D], F32)
    X2_d = scr("X2", [S_TOK, D], F32)
    GG_d = scr("GG", [32, 128, S_TOK], BF16)
    OG_d = scr("OG", [16, 128, S_TOK], BF16)
    OA_d = scr("OA", [16, 128, S_TOK], BF16)
    MT_d = scr("MT", [16, 128, S_TOK], BF16)
    CS_d = scr("CS", [2, 64, S_TOK], F32)
    LAT_d = scr("LAT", [128, 8, S_TOK], BF16)

    with ExitStack() as st0:
        S = Sched(nc, st0)
        uid = [0]

        def uname(name):
            uid[0] += 1
            return f"s{uid[0]}_{name}"
        sb0 = lambda name, shape, dt: st0.enter_context(nc.sbuf_tensor(uname(name), list(shape), dt))
        banks = [st0.enter_context(nc.psum_tensor(f"bank{i}", [128, 512], F32)) for i in range(8)]
        bk = lambda i: ("bank", i)
        ident = sb0("ident", [128, 128], BF16)
        ones_bf = sb0("ones_bf", [128, 128], BF16)
        tri = sb0("tri", [128, 128], F32)
        tris = sb0("tris", [128, 128], F32)
        caus = sb0("caus", [128, 128], BF16)
        blk2 = sb0("blk2", [128, 2], F32)
        cst = sb0("cst_s", [64, 8], F32)
        gout = sb0("gout_s", [128, 1], F32)
        stat = sb0("stat", [128, 64], F32)

        def phase(fn):
            with ExitStack() as st:
                sb = lambda name, shape, dt: st.enter_context(nc.sbuf_tensor(uname(name), list(shape), dt))
                fn(sb)
                S.finish_phase()
                with nc.Block() as block:
                    S.emit(block)

        def dma(q, out, in_, reads=(), writes=(), flat=False):
            if flat:
                out = out.rearrange("p a b -> p (a b)")
                in_ = in_.rearrange("p a b -> p (a b)")
            return S.op(q, lambda e: e.dma_start(out=out, in_=in_), reads=reads, writes=writes, dma=True)

        def wait_all(keys):
            for en in ("pe", "act", "dve", "pool"):
                S.op(en, lambda e: e.nop(), reads=list(keys))

        def p_consts(sb):
            dma("pool", ident[:], ident_d, writes=["ident"])
            dma("pool", caus[:], caus_d, writes=["caus"])
            dma("sp", tri[:], tri_d, writes=["tri"])
            dma("sp", tris[:], tris_d, writes=["tris"])
            dma("sp", blk2[:], blk2_d, writes=["blk2"])
            dma("sp", cst[:], cst_d, writes=["cst"])
            dma("sp", gout[:], gout_d, writes=["gout"])
            S.op("dve", lambda e: e.memset(ones_bf[:], 1.0), writes=["ones"])
            S.op("dve", lambda e: e.memset(stat[:], 0.0), writes=["stat"])
        phase(p_consts)

        def rstd_chain(sc, ksc, c_in, c0, mul, add, extra_reads=()):
            S.op("dve", lambda e: e.tensor_scalar(out=sc[:, c0:c0 + 1], in0=sc[:, c_in:c_in + 1], scalar1=mul, scalar2=add,
                                                   op0=ALU.mult, op1=ALU.add),
                 reads=[(ksc, c_in)] + list(extra_reads), writes=[(ksc, c0)])
            S.op("act", lambda e: e.activation(out=sc[:, c0 + 1:c0 + 2], in_=sc[:, c0:c0 + 1], func=AF.Sqrt),
                 reads=[(ksc, c0)], writes=[(ksc, c0 + 1)])
            S.op("dve", lambda e: e.reciprocal(out=sc[:, c0 + 2:c0 + 3], in_=sc[:, c0 + 1:c0 + 2]),
                 reads=[(ksc, c0 + 1)], writes=[(ksc, c0 + 2)])

        def rn_pass(sb, hT, src_x, y_d, post_idx, post_scale, dst_x, pre_idx):
            xring = Ring(sb, "rn_x", 3, [128, D], F32)
            yring = Ring(sb, "rn_y", 3, [128, D], F32) if y_d is not None else None
            xsring = Ring(sb, "rn_xs", 2, [128, D], BF16)
            scring = Ring(sb, "rn_sc", 4, [128, 12], F32)
            junk = sb("rn_junk", [128, D], BF16)
            gpost = sb("rn_gpost", [128, D], F32) if y_d is not None else None
            gpre = sb("rn_gpre", [128, D], F32) if pre_idx is not None else None
            if gpost is not None:
                dma("sp", gpost[:], gains_d[post_idx], writes=["gpost"])
            if gpre is not None:
                dma("sp", gpre[:], gains_d[pre_idx], writes=["gpre"])
            pre = {}

            def load_t(tt):
                if tt >= NT or tt in pre:
                    return
                rows = slice(tt * 128, (tt + 1) * 128)
                xt, kx = xring.next()
                dma("sp", xt[:], src_x[rows, :], writes=[kx])
                yt = ky = None
                if y_d is not None:
                    yt, ky = yring.next()
                    dma("sp", yt[:], y_d[rows, :], writes=[ky])
                pre[tt] = (xt, kx, yt, ky)
            load_t(0)
            for tt in range(NT):
                rows = slice(tt * 128, (tt + 1) * 128)
                load_t(tt + 1)
                xt, kx, yt, ky = pre.pop(tt)
                sc, ksc = scring.next()
                S.op("pool", lambda e, sc=sc: e.memset(sc[:], 0.0), writes=[(ksc, i) for i in range(12)])
                if y_d is not None:
                    S.op("dve", lambda e, sc=sc, tt=tt: e.reduce_sum(out=sc[:, 0:1], in_=stat[:, tt * 4:(tt + 1) * 4], axis=AX.X),
                         reads=[], writes=[(ksc, 0)])
                    s2 = post_scale * post_scale
                    rstd_chain(sc, ksc, 0, 1, 1.0 / (D * s2), EPS / s2)
                    S.op("dve", lambda e, yt=yt, sc=sc: e.scalar_tensor_tensor(out=yt[:], in0=yt[:], scalar=sc[:, 3:4], in1=gpost[:],
                                                                               op0=ALU.mult, op1=ALU.mult),
                         reads=[ky, (ksc, 3), "gpost"], writes=[ky])
                    S.op("pool", lambda e, xt=xt, yt=yt: e.tensor_tensor(out=xt[:], in0=xt[:], in1=yt[:], op=ALU.add),
                         reads=[kx, ky], writes=[kx])
                if dst_x is not None:
                    dma("sp", dst_x[rows, :], xt[:], reads=[kx], writes=[("dst", tt)])
                if pre_idx is not None:
                    S.op("act", lambda e, xt=xt, sc=sc: e.activation(out=junk[:], in_=xt[:], func=AF.Square, accum_out=sc[:, 4:5]),
                         reads=[kx], writes=["junk", (ksc, 4)])
                    rstd_chain(sc, ksc, 4, 5, 1.0 / D, EPS)
                    xs, kxs = xsring.next()
                    S.op("dve", lambda e, xs=xs, xt=xt, sc=sc: e.scalar_tensor_tensor(out=xs[:], in0=xt[:], scalar=sc[:, 7:8], in1=gpre[:],
                                                                                       op0=ALU.mult, op1=ALU.mult),
                         reads=[kx, (ksc, 7), "gpre"], writes=[kxs])
                    for half in range(2):
                        bi = (tt * 2 + half) % 4
                        bv = banks[bi][:].bitcast(BF16)

                        def tr(e, xs=xs, bv=bv, half=half):
                            ins = None
                            for j in range(8):
                                c = half * 8 + j
                                ins = e.transpose(bv[:, j * 128:(j + 1) * 128], xs[:, c * 128:(c + 1) * 128], ident[:])
                            return ins
                        S.op("pe", tr, reads=[kxs], writes=[bk(bi)])
                        dst = hT[:, half * 8:(half + 1) * 8, tt * 128:(tt + 1) * 128]
                        src = bv.rearrange("p (a b) -> p a b", b=128)
                        if half == 0:
                            S.op("act", lambda e, dst=dst, src=src: e.copy(out=dst, in_=src), reads=[bk(bi)], writes=[("hT", tt, half)])
                        else:
                            S.op("dve", lambda e, dst=dst, src=src: e.tensor_copy(out=dst, in_=src), reads=[bk(bi)], writes=[("hT", tt, half)])

        def gu_phase(sb, hT, wg_d, wu_d):
            wgr = Ring(sb, "gu_wg", 3, [128, KC, 128], BF16)
            wur = Ring(sb, "gu_wu", 3, [128, KC, 128], BF16)
            sgr = Ring(sb, "gu_sg", 2, [128, 512], F32)
            atr = Ring(sb, "gu_at", 3, [128, 512], BF16)
            it = 0
            for fc in range(FC):
                wg, kwg = wgr.next()
                wu, kwu = wur.next()
                dma("pool", wg[:], wg_d[fc], writes=[kwg], flat=True)
                dma("pool", wu[:], wu_d[fc], writes=[kwu], flat=True)
                for tg in range(NG):
                    cols = slice(tg * 512, (tg + 1) * 512)
                    iA, iB = 2 * (it % 4), 2 * (it % 4) + 1
                    it += 1

                    def mm(e, w, b, cols=cols):
                        ins = None
                        for kc in range(KC):
                            ins = e.matmul(banks[b][:], lhsT=w[:, kc, :], rhs=hT[:, kc, cols], start=(kc == 0), stop=(kc == KC - 1))
                        return ins
                    S.op("pe", lambda e, wg=wg, iA=iA, mm=mm: mm(e, wg, iA), reads=[kwg], writes=[bk(iA)])
                    S.op("pe", lambda e, wu=wu, iB=iB, mm=mm: mm(e, wu, iB), reads=[kwu], writes=[bk(iB)])
                    sg, ksg = sgr.next()
                    at, kat = atr.next()
                    S.op("act", lambda e, sg=sg, iA=iA: e.activation(out=sg[:], in_=banks[iA][:], func=AF.Silu), reads=[bk(iA)], writes=[ksg])
                    S.op("dve", lambda e, at=at, sg=sg, iB=iB: e.tensor_tensor(out=at[:], in0=sg[:], in1=banks[iB][:], op=ALU.mult),
                         reads=[ksg, bk(iB)], writes=[kat])
                    dst = A1_d[tg * 4:(tg + 1) * 4, :, fc, :].rearrange("t p n -> p t n")
                    src = at[:].rearrange("p (t n) -> p t n", n=128)
                    dma("sp", dst, src, reads=[kat], writes=[("A1", fc, tg)])

        def dn_phase(sb, kcn, w_d, a_src, a_res=None):
            nq = 4
            per = kcn // nq
            wr = Ring(sb, "dn_w", 2, [128, kcn, 512], BF16)
            ar = Ring(sb, "dn_a", 3, [128, kcn, 128], BF16) if a_res is None else None
            ysr = Ring(sb, "dn_ys", 3, [128, 512], F32)
            junk = sb("dn_junk", [128, 512], BF16)
            S.op("pool", lambda e: e.memset(stat[:], 0.0), writes=["stat"])
            iters = [(dg, tt) for dg in range(4) for tt in range(NT)]
            loaded = {}

            def load_a(j):
                if a_res is not None or j >= len(iters) or j in loaded:
                    return
                at, ka = ar.next()
                dma("sp", at[:], a_src[iters[j][1]], writes=[ka], flat=True)
                loaded[j] = (at, ka)
            wts = {}

            def load_w(dg):
                if dg >= 4 or dg in wts:
                    return
                w, kw = wr.next()
                for q in range(nq):
                    dma("pool", w[:, q * per:(q + 1) * per, :], w_d[dg][:, q * per:(q + 1) * per, :], writes=[(kw, q)], flat=True)
                wts[dg] = (w, kw)
            load_w(0)
            load_a(0)
            load_a(1)
            for j, (dg, tt) in enumerate(iters):
                if tt == 0:
                    load_w(dg + 1)
                load_a(j + 2)
                w, kw = wts[dg]
                b = j % 4
                if a_res is None:
                    at, ka = loaded.pop(j)
                    lhs = lambda kc, at=at: at[:, kc, :]
                    rd = [ka]
                else:
                    lhs = lambda kc, tt=tt: a_res[:, kc, tt * 128:(tt + 1) * 128]
                    rd = []

                def mm(e, lhs=lhs, w=w, b=b):
                    ins = None
                    for kc in range(kcn):
                        ins = e.matmul(banks[b][:], lhsT=lhs(kc), rhs=w[:, kc, :], start=(kc == 0), stop=(kc == kcn - 1))
                    return ins
                S.op("pe", mm, reads=rd + [(kw, q) for q in range(nq)], writes=[bk(b)])
                c = tt * 4 + dg
                ys, kys = ysr.next()
                S.op("dve", lambda e, ys=ys, b=b: e.tensor_copy(out=ys[:], in_=banks[b][:]), reads=[bk(b)], writes=[kys])
                S.op("act", lambda e, ys=ys, c=c: e.activation(out=junk[:], in_=ys[:], func=AF.Square, accum_out=stat[:, c:c + 1]),
                     reads=[kys, "stat"], writes=["dn_junk", ("stat", c)])
                dma("sp", Y_d[tt * 128:(tt + 1) * 128, dg * 512:(dg + 1) * 512], ys[:], reads=[kys], writes=[("Y", tt, dg)])

        def ffn_block(idx, src_x, dst_x, final_pre):
            pass

        wg1, wu1, wd1 = ffn_w[0]
        wg2, wu2, wd2 = ffn_w[1]
        with ExitStack() as stA:
            hT = stA.enter_context(nc.sbuf_tensor("hT_a", [128, KC, S_TOK], BF16))
            if stage == 0:
                phase(lambda sb: rn_pass(sb, hT, x_d, None, None, 1.0, out_d, 0))
                return nc
            import os as _os
            if not _os.environ.get("K_SKIP_GU"):
                phase(lambda sb: rn_pass(sb, hT, x_d, None, None, 1.0, None, 0))
                phase(lambda sb: gu_phase(sb, hT, wg1, wu1))
            if stage == 0.5:
                phase(lambda sb: rn_pass(sb, hT, x_d, None, None, 1.0, out_d, None))
                return nc
        phase(lambda sb: dn_phase(sb, FC, wd1, A1_d))
        if stage == 0.75:
            phase(lambda sb: rn_pass(sb, None, x_d, None, None, 1.0, out_d, None))
            return nc
        if stage == 1:
            phase(lambda sb: rn_pass(sb, None, x_d, Y_d, 1, 0.5, out_d, None))
            return nc

        with ExitStack() as stCD:
            sbCD = lambda name, shape, dt: stCD.enter_context(nc.sbuf_tensor(uname(name), list(shape), dt))
            krT = sbCD("krT", [64, S_TOK], BF16)
            with ExitStack() as stC:
                hT = stC.enter_context(nc.sbuf_tensor("hT_c", [128, KC, S_TOK], BF16))
                phase(lambda sb: rn_pass(sb, hT, x_d, Y_d, 1, 0.5, X1_d, 2))

                def p_ma0(sb):
                    cos2 = sb("cos2a", [64, S_TOK], F32)
                    sin2 = sb("sin2a", [64, S_TOK], F32)
                    pos_i = sb("pos_i", [64, S_TOK], I32)
                    ang = sb("ang", [64, S_TOK], F32)
                    tq = sb("tq", [64, S_TOK], F32)
                    rr = sb("rr", [64, S_TOK], F32)
                    mm_ = sb("mm_", [64, S_TOK], F32)
                    ki = pos_i
                    kf = tq
                    dma("sp", pos_i[:], pos_d, writes=["pos_i"])
                    S.op("dve", lambda e: e.tensor_copy(out=ang[:], in_=pos_i[:]), reads=["pos_i"], writes=["ang"])
                    S.op("dve", lambda e: e.tensor_scalar(out=ang[:], in0=ang[:], scalar1=cst[:, 0:1], scalar2=None, op0=ALU.mult),
                         reads=["ang"], writes=["ang"])
                    for which in range(2):
                        shift = 0.0 if which == 0 else math.pi / 2
                        S.op("dve", lambda e, shift=shift: e.tensor_scalar(out=tq[:], in0=ang[:], scalar1=shift, scalar2=1.0 / (2 * math.pi),
                                                                            op0=ALU.add, op1=ALU.mult), reads=["ang"], writes=["tq"])
                        S.op("dve", lambda e: e.tensor_copy(out=ki[:], in_=tq[:]), reads=["tq", "pos_i"], writes=["pos_i"])
                        S.op("dve", lambda e: e.tensor_copy(out=kf[:], in_=ki[:]), reads=["pos_i"], writes=["tq"])
                        S.op("dve", lambda e: e.scalar_tensor_tensor(out=rr[:], in0=kf[:], scalar=-2 * math.pi, in1=ang[:],
                                                                     op0=ALU.mult, op1=ALU.add), reads=["tq", "ang"], writes=["rr"])
                        if which == 1:
                            S.op("dve", lambda e: e.tensor_scalar(out=rr[:], in0=rr[:], scalar1=math.pi / 2, scalar2=None, op0=ALU.add),
                                 reads=["rr"], writes=["rr"])
                        S.op("dve", lambda e: e.tensor_scalar(out=mm_[:], in0=rr[:], scalar1=0.0, scalar2=2 * math.pi,
                                                               op0=ALU.is_lt, op1=ALU.mult), reads=["rr"], writes=["mm_"])
                        S.op("dve", lambda e: e.tensor_tensor(out=rr[:], in0=rr[:], in1=mm_[:], op=ALU.add), reads=["rr", "mm_"], writes=["rr"])
                        S.op("dve", lambda e: e.tensor_scalar(out=rr[:], in0=rr[:], scalar1=0.0, scalar2=2 * math.pi,
                                                               op0=ALU.max, op1=ALU.min), reads=["rr"], writes=["rr"])
                        if which == 0:
                            S.op("act", lambda e: e.activation(out=sin2[:], in_=rr[:], func=AF.Sin, scale=cst[:, 1:2], bias=cst[:, 2:3]),
                                 reads=["rr"], writes=["sin2"])
                        else:
                            S.op("act", lambda e: e.activation(out=cos2[:], in_=rr[:], func=AF.Sin, scale=cst[:, 3:4], bias=cst[:, 4:5]),
                                 reads=["rr"], writes=["cos2"])
                    dma("sp", CS_d[0], cos2[:], reads=["cos2"], writes=["CS0"])
                    dma("sp", CS_d[1], sin2[:], reads=["sin2"], writes=["CS1"])
                phase(p_ma0)

                def p_ma(sb):
                    cos2 = sb("cos2b", [64, S_TOK], F32)
                    sin2 = sb("sin2b", [64, S_TOK], F32)
                    latT = sb("latTb", [128, 8, S_TOK], BF16)
                    dma("sp", cos2[:], CS_d[0], writes=["cos2"])
                    dma("sp", sin2[:], CS_d[1], writes=["sin2"])
                    glat = sb("glat_s", [128, 1024], F32)
                    dma("sp", glat[:], glat_d, writes=["glat"])
                    wr = Ring(sb, "ma_w", 2, [128, KC, 512], BF16)
                    scring = Ring(sb, "ma_sc", 4, [128, 4], F32)
                    cnr = Ring(sb, "ma_cn", 2, [128, 512], BF16)
                    junk = sb("ma_junk", [128, 512], BF16)
                    it = 0
                    for cg in range(2):
                        w, kw = wr.next()
                        dma("pool", w[:], wlat_d[cg], writes=[kw], flat=True)
                        for tt in range(NT):
                            b = it % 3
                            it += 1

                            def mm(e, w=w, b=b, tt=tt):
                                ins = None
                                for kc in range(KC):
                                    ins = e.matmul(banks[b][:], lhsT=hT[:, kc, tt * 128:(tt + 1) * 128], rhs=w[:, kc, :],
                                                   start=(kc == 0), stop=(kc == KC - 1))
                                return ins
                            S.op("pe", mm, reads=[kw], writes=[bk(b)])
                            sc, ksc = scring.next()
                            S.op("pool", lambda e, sc=sc: e.memset(sc[:], 0.0), writes=[(ksc, i) for i in range(4)])
                            S.op("act", lambda e, sc=sc, b=b: e.activation(out=junk[:], in_=banks[b][:], func=AF.Square, accum_out=sc[:, 0:1]),
                                 reads=[bk(b)], writes=["ma_junk", (ksc, 0)])
                            rstd_chain(sc, ksc, 0, 1, 1.0 / 512, EPS)
                            cn, kcn_ = cnr.next()
                            S.op("dve", lambda e, cn=cn, sc=sc, b=b, cg=cg: e.scalar_tensor_tensor(
                                out=cn[:], in0=banks[b][:], scalar=sc[:, 3:4], in1=glat[:, cg * 512:(cg + 1) * 512], op0=ALU.mult, op1=ALU.mult),
                                reads=[bk(b), (ksc, 3), "glat"], writes=[kcn_])
                            b2 = 3 + (it % 2)
                            bv = banks[b2][:].bitcast(BF16)

                            def tr(e, cn=cn, bv=bv):
                                ins = None
                                for j in range(4):
                                    ins = e.transpose(bv[:, j * 128:(j + 1) * 128], cn[:, j * 128:(j + 1) * 128], ident[:])
                                return ins
                            S.op("pe", tr, reads=[kcn_], writes=[bk(b2)])
                            dst = latT[:, cg * 4:(cg + 1) * 4, tt * 128:(tt + 1) * 128]
                            src = bv[:, 0:512].rearrange("p (a b) -> p a b", b=128)
                            S.op("act", lambda e, dst=dst, src=src: e.copy(out=dst, in_=src), reads=[bk(b2)], writes=[("latT", cg, tt)])
                    wk = sb("ma_wkr", [128, KC, 128], BF16)
                    dma("pool", wk[:], wkr_d, writes=["wkr"], flat=True)
                    t1r = Ring(sb, "ma_t1", 2, [64, 512], F32)
                    t2r = Ring(sb, "ma_t2", 2, [64, 512], F32)
                    for tg in range(NG):
                        cols = slice(tg * 512, (tg + 1) * 512)
                        bA, bB = 5, 6

                        def mmk(e, b, c0, cols=cols):
                            ins = None
                            for kc in range(KC):
                                ins = e.matmul(banks[b][0:64, :], lhsT=wk[:, kc, c0:c0 + 64], rhs=hT[:, kc, cols], start=(kc == 0), stop=(kc == KC - 1))
                            return ins
                        S.op("pe", lambda e, mmk=mmk: mmk(e, bA, 0), reads=["wkr"], writes=[bk(bA)])
                        S.op("pe", lambda e, mmk=mmk: mmk(e, bB, 64), reads=["wkr"], writes=[bk(bB)])
                        t1, k1 = t1r.next()
                        t2, k2 = t2r.next()
                        S.op("dve", lambda e, t1=t1, cols=cols: e.tensor_tensor(out=t1[:], in0=banks[bA][0:64, :], in1=cos2[:, cols], op=ALU.mult),
                             reads=[bk(bA), "cos2"], writes=[k1])
                        S.op("dve", lambda e, t2=t2, cols=cols: e.tensor_tensor(out=t2[:], in0=banks[bB][0:64, :], in1=sin2[:, cols], op=ALU.mult),
                             reads=[bk(bB), "sin2"], writes=[k2])
                        S.op("pool", lambda e, t1=t1, t2=t2, cols=cols: e.tensor_tensor(out=krT[:, cols], in0=t1[:], in1=t2[:], op=ALU.add),
                             reads=[k1, k2], writes=[("krT", tg)])
                    dma("sp", LAT_d, latT[:], reads=[("latT", cg, tt) for cg in range(2) for tt in range(NT)], writes=["LAT"], flat=True)
                phase(p_ma)

                def p_mb(sb):
                    wr = Ring(sb, "mb_w", 3, [128, KC, 128], BF16)
                    gsr = Ring(sb, "mb_gs", 2, [128, S_TOK], BF16)
                    it = 0
                    for oc in range(32):
                        w, kw = wr.next()
                        dma("pool", w[:], wgate_d[oc], writes=[kw], flat=True)
                        gs, kgs = gsr.next()
                        for tg in range(NG):
                            cols = slice(tg * 512, (tg + 1) * 512)
                            b = it % 4
                            it += 1

                            def mm(e, w=w, b=b, cols=cols):
                                ins = None
                                for kc in range(KC):
                                    ins = e.matmul(banks[b][:], lhsT=w[:, kc, :], rhs=hT[:, kc, cols], start=(kc == 0), stop=(kc == KC - 1))
                                return ins
                            S.op("pe", mm, reads=[kw], writes=[bk(b)])
                            S.op("act", lambda e, gs=gs, b=b, cols=cols: e.activation(out=gs[:, cols], in_=banks[b][:], func=AF.Sigmoid),
                                 reads=[bk(b)], writes=[(kgs, tg)])
                        dma("sp", GG_d[oc], gs[:], reads=[(kgs, tg) for tg in range(NG)], writes=[("GG", oc)])
                phase(p_mb)

                def p_mc(sb):
                    lb_bc = sb("lb_bc", [128, D], F32)
                    l1_bc = sb("l1_bc", [128, D], F32)
                    oml_bc = l1_bc
                    dma("sp", lb_bc[:], lbl_d[0], writes=["lb"])
                    dma("sp", l1_bc[:], lbl_d[1], writes=["l1"])
                    S.op("dve", lambda e: e.tensor_tensor(out=l1_bc[:], in0=lb_bc[:], in1=l1_bc[:], op=ALU.subtract), reads=["lb", "l1"], writes=["l1"])
                    S.op("act", lambda e: e.activation(out=lb_bc[:], in_=l1_bc[:], func=AF.Sigmoid), reads=["l1"], writes=["lb"])
                    S.op("dve", lambda e: e.tensor_scalar(out=oml_bc[:], in0=lb_bc[:], scalar1=-1.0, scalar2=1.0, op0=ALU.mult, op1=ALU.add),
                         reads=["lb", "l1"], writes=["oml", "l1"])
                    GW = 256
                    wqf = sb("mc_wqf", [128, KC, 512], BF16)
                    wi4 = sb("mc_wi4", [128, KC, 512], BF16)
                    whgr = Ring(sb, "mc_whg", 2, [128, KC, 128], BF16)
                    qT = sb("mc_qT", [128, 2, S_TOK], BF16)
                    kT = sb("mc_kT", [128, 2, S_TOK], BF16)
                    kd = sb("mc_kd", [128, NT, GW], BF16)
                    vt = sb("mc_vt", [128, NT, 512], BF16)
                    dec = sb("mc_dec", [128, 2, 32], F32)
                    tmp = {n: Ring(sb, "mc_" + n, r, [128, GW], F32) for n, r in
                           (("tq", 2), ("tf", 2), ("qs", 3), ("f", 2), ("lf", 3), ("k", 3), ("e1", 2), ("e2", 2), ("e3", 2))}
                    qtr = Ring(sb, "mc_qt", 3, [128, GW], BF16)
                    ktr = Ring(sb, "mc_kt", 3, [128, GW], BF16)
                    st32 = [sb(f"mc_st32_{h}", [128, 128], F32) for h in range(2)]
                    stbf = [sb(f"mc_stbf_{h}", [128, 128], BF16) for h in range(2)]
                    scmr = Ring(sb, "mc_scm", 4, [128, 128], BF16)
                    oaccr = Ring(sb, "mc_oacc", 2, [128, 512], F32)
                    sqr = Ring(sb, "mc_sq", 2, [128, 512], BF16)
                    tr_ = Ring(sb, "mc_t", 1, [128, 512], F32)
                    ter = Ring(sb, "mc_te", 1, [128, 512], F32)
                    sgr = Ring(sb, "mc_sg", 1, [128, 512], F32)
                    o1r = Ring(sb, "mc_o1", 1, [128, 512], F32)
                    ogr = Ring(sb, "mc_og", 1, [128, 512], BF16)

                    def act_sigmoid_inplace(t, kt_, src, rd):
                        S.op("act", lambda e, t=t, src=src: e.activation(out=t, in_=src, func=AF.Exp, scale=-1.0), reads=rd, writes=[kt_])
                        S.op("act", lambda e, t=t: e.activation(out=t, in_=t, func=AF.Ln, bias=1.0), reads=[kt_], writes=[kt_])
                        S.op("act", lambda e, t=t: e.activation(out=t, in_=t, func=AF.Exp, scale=-1.0), reads=[kt_], writes=[kt_])

                    for g in range(8):
                        gcols = slice(g * GW, (g + 1) * GW)
                        dma("pool", wqf[:], wqf_d[g], writes=["wqf"], flat=True)
                        if g % 2 == 0:
                            dma("pool", wi4[:], wi4_d[g // 2], writes=["wi4"], flat=True)
                        vo = (g % 2) * 256
                        whg = []
                        for hh in range(2):
                            w, kw = whgr.next()
                            dma("pool", w[:], whg_d[g * 2 + hh], writes=[kw], flat=True)
                            whg.append((w, kw))
                        for hh in range(2):
                            S.op("pool", lambda e, hh=hh: e.memset(st32[hh][:], 0.0), writes=[("st32", hh)])
                            S.op("pool", lambda e, hh=hh: e.memset(stbf[hh][:], 0.0), writes=[("stbf", hh)])
                        stA = {}
                        stB = {}

                        def stage_a(tt):
                            tcols = slice(tt * 128, (tt + 1) * 128)
                            qb = tt % 2
                            def mmqf(e, tcols=tcols, qb=qb):
                                ins = None
                                for kc in range(KC):
                                    ins = e.matmul(banks[qb][:], lhsT=hT[:, kc, tcols], rhs=wqf[:, kc, :], start=(kc == 0), stop=(kc == KC - 1))
                                return ins
                            S.op("pe", mmqf, reads=["wqf"], writes=[bk(qb)])
                            if g % 2 == 0:
                                def mmv(e, tcols=tcols):
                                    ins = None
                                    for kc in range(KC):
                                        ins = e.matmul(banks[2][:], lhsT=hT[:, kc, tcols], rhs=wi4[:, kc, :], start=(kc == 0), stop=(kc == KC - 1))
                                    return ins
                                S.op("pe", mmv, reads=["wi4"], writes=[bk(2)])
                                S.op("dve", lambda e, tt=tt: e.tensor_copy(out=vt[:, tt, :], in_=banks[2][:]), reads=[bk(2)], writes=[("vt", tt)])
                            tq, ktq = tmp["tq"].next()
                            tf, ktf = tmp["tf"].next()
                            qs, kqs = tmp["qs"].next()
                            f, kf_ = tmp["f"].next()
                            lf, klf = tmp["lf"].next()
                            k_, kk = tmp["k"].next()
                            act_sigmoid_inplace(tq[:], ktq, banks[qb][:, 0:GW], [bk(qb)])
                            act_sigmoid_inplace(tf[:], ktf, banks[qb][:, GW:2 * GW], [bk(qb)])
                            S.op("dve", lambda e, qs=qs, tq=tq, qb=qb: e.tensor_tensor(out=qs[:], in0=banks[qb][:, 0:GW], in1=tq[:], op=ALU.mult),
                                 reads=[bk(qb), ktq, ktf], writes=[kqs])
                            S.op("dve", lambda e, f=f, tf=tf, gcols=gcols: e.tensor_tensor(out=f[:], in0=tf[:], in1=oml_bc[:, gcols], op=ALU.mult),
                                 reads=[ktf, "oml"], writes=[kf_])
                            S.op("dve", lambda e, f=f, gcols=gcols: e.tensor_tensor(out=f[:], in0=f[:], in1=lb_bc[:, gcols], op=ALU.add),
                                 reads=[kf_, "lb"], writes=[kf_])
                            S.op("act", lambda e, f=f, lf=lf: e.activation(out=lf[:], in_=f[:], func=AF.Ln), reads=[kf_], writes=[klf])
                            S.op("pool", lambda e, f=f, k_=k_: e.tensor_scalar(out=k_[:], in0=f[:], scalar1=-1.0, scalar2=1.0, op0=ALU.mult, op1=ALU.add),
                                 reads=[kf_], writes=[kk])
                            stA[tt] = (qs, kqs, lf, klf, k_, kk)

                        def stage_b(tt):
                            qs, kqs, lf, klf, k_, kk = stA.pop(tt)
                            e1, ke1 = tmp["e1"].next()
                            e2, ke2 = tmp["e2"].next()
                            e3, ke3 = tmp["e3"].next()
                            S.op("pe", lambda e, lf=lf: e.matmul(banks[3][:, 0:GW], lhsT=tri[:], rhs=lf[:], start=True, stop=True),
                                 reads=[klf], writes=[bk(3)])
                            S.op("pe", lambda e, lf=lf: e.matmul(banks[4][:, 0:GW], lhsT=tris[:], rhs=lf[:], start=True, stop=True),
                                 reads=[klf], writes=[bk(4)])

                            def mmdec(e, lf=lf):
                                ins = None
                                for hh in range(2):
                                    ins = e.matmul(banks[6][:, hh * 2:hh * 2 + 2], lhsT=lf[:, hh * 128:(hh + 1) * 128], rhs=blk2[:], start=True, stop=True)
                                return ins
                            S.op("pe", mmdec, reads=[klf], writes=[bk(6)])
                            S.op("act", lambda e, e1=e1: e.activation(out=e1[:], in_=banks[3][:, 0:GW], func=AF.Exp), reads=[bk(3)], writes=[ke1])
                            S.op("act", lambda e, e2=e2: e.activation(out=e2[:], in_=banks[3][:, 0:GW], func=AF.Exp, scale=-1.0), reads=[bk(3)], writes=[ke2])
                            S.op("act", lambda e, e3=e3: e.activation(out=e3[:], in_=banks[4][:, 0:GW], func=AF.Exp), reads=[bk(4)], writes=[ke3])
                            S.op("act", lambda e, tt=tt: e.activation(out=dec[:, :, tt * 2:tt * 2 + 2],
                                                                      in_=banks[6][:, 0:4].rearrange("p (h c) -> p h c", c=2), func=AF.Exp),
                                 reads=[bk(6)], writes=[("dec", tt)])
                            qt, kqt = qtr.next()
                            kt, kkt = ktr.next()
                            S.op("dve", lambda e, qt=qt, qs=qs, e1=e1: e.tensor_tensor(out=qt[:], in0=qs[:], in1=e1[:], op=ALU.mult), reads=[kqs, ke1], writes=[kqt])
                            S.op("dve", lambda e, kt=kt, k_=k_, e2=e2: e.tensor_tensor(out=kt[:], in0=k_[:], in1=e2[:], op=ALU.mult), reads=[kk, ke2], writes=[kkt])
                            S.op("pool", lambda e, tt=tt, k_=k_, e3=e3: e.tensor_tensor(out=kd[:, tt, :], in0=k_[:], in1=e3[:], op=ALU.mult),
                                 reads=[kk, ke3], writes=[("kd", tt)])
                            stB[tt] = (qt, kqt, kt, kkt)

                        def stage_c(tt):
                            tcols = slice(tt * 128, (tt + 1) * 128)
                            qt, kqt, kt, kkt = stB.pop(tt)
                            bv = banks[5][:].bitcast(BF16)

                            def tr4(e, qt=qt, kt=kt, bv=bv):
                                ins = None
                                for hh in range(2):
                                    ins = e.transpose(bv[:, hh * 128:(hh + 1) * 128], qt[:, hh * 128:(hh + 1) * 128], ident[:])
                                for hh in range(2):
                                    ins = e.transpose(bv[:, 256 + hh * 128:256 + (hh + 1) * 128], kt[:, hh * 128:(hh + 1) * 128], ident[:])
                                return ins
                            S.op("pe", tr4, reads=[kqt, kkt], writes=[bk(5)])
                            S.op("dve", lambda e, bv=bv, tcols=tcols: e.tensor_copy(out=qT[:, :, tcols], in_=bv[:, 0:256].rearrange("p (h c) -> p h c", c=128)),
                                 reads=[bk(5)], writes=[("qT", tt)])
                            S.op("dve", lambda e, bv=bv, tcols=tcols: e.tensor_copy(out=kT[:, :, tcols], in_=bv[:, 256:512].rearrange("p (h c) -> p h c", c=128)),
                                 reads=[bk(5)], writes=[("kT", tt)])

                        for step in range(NT + 2):
                            if step < NT:
                                stage_a(step)
                            if 0 <= step - 1 < NT:
                                stage_b(step - 1)
                            if 0 <= step - 2 < NT:
                                stage_c(step - 2)
                        for tt in range(NT):
                            tok0 = tt * 128
                            q4 = (tt % 4) * 128
                            scms = []
                            for hh in range(2):
                                S.op("pe", lambda e, hh=hh, tok0=tok0: e.matmul(banks[hh][:, 0:128], lhsT=kT[:, hh, tok0:tok0 + 128],
                                                                                 rhs=qT[:, hh, tok0:tok0 + 128], start=True, stop=True),
                                     reads=[("qT", tt), ("kT", tt)], writes=[bk(hh)])
                            for hh in range(2):
                                scm, kscm = scmr.next()
                                scms.append((scm, kscm))
                                S.op("dve", lambda e, scm=scm, hh=hh: e.tensor_tensor(out=scm[:], in0=banks[hh][:, 0:128], in1=tri[:], op=ALU.mult),
                                     reads=[bk(hh)], writes=[kscm])
                            for half in range(2):
                                prow = slice(half * 64, (half + 1) * 64)
                                for hh in range(2):
                                    scm, kscm = scms[hh]
                                    hc = slice(hh * 128, (hh + 1) * 128)
                                    oc_ = slice(q4 + half * 64, q4 + (half + 1) * 64)
                                    qc = slice(tok0 + half * 64, tok0 + (half + 1) * 64)

                                    vc = slice(vo + hh * 128, vo + (hh + 1) * 128)

                                    def mmo(e, hh=hh, scm=scm, vc=vc, oc_=oc_, qc=qc, half=half, tt=tt):
                                        e.matmul(banks[2 + hh][:, oc_], lhsT=stbf[hh][:], rhs=qT[:, hh, qc], start=True, stop=False)
                                        return e.matmul(banks[2 + hh][:, oc_], lhsT=vt[:, tt, vc], rhs=scm[:, half * 64:(half + 1) * 64], start=False, stop=True)
                                    S.op("pe", mmo, reads=[("stbf", hh), kscm, ("qT", tt), ("vt", tt)], writes=[bk(2 + hh)])
                                for hh in range(2):
                                    hc = slice(hh * 128, (hh + 1) * 128)
                                    vc = slice(vo + hh * 128, vo + (hh + 1) * 128)
                                    S.op("pe", lambda e, hh=hh, hc=hc, vc=vc, prow=prow, tt=tt: e.matmul(banks[6 + hh][:, 0:128], lhsT=kd[prow, tt, hc], rhs=vt[prow, tt, vc],
                                                                                                   start=True, stop=True),
                                         reads=[("kd", tt), ("vt", tt)], writes=[bk(6 + hh)])
                                for hh in range(2):
                                    ci = tt * 2 + half
                                    S.op("dve", lambda e, hh=hh, ci=ci: e.scalar_tensor_tensor(out=st32[hh][:], in0=st32[hh][:], scalar=dec[:, hh, ci:ci + 1],
                                                                                                in1=banks[6 + hh][:, 0:128], op0=ALU.mult, op1=ALU.add),
                                         reads=[bk(6 + hh), ("st32", hh), ("dec", tt)], writes=[("st32", hh)])
                                    S.op("act", lambda e, hh=hh: e.copy(out=stbf[hh][:], in_=st32[hh][:]), reads=[("st32", hh)], writes=[("stbf", hh)])
                            if tt % 4 == 3:
                                tg = tt // 4
                                cols = slice(tg * 512, (tg + 1) * 512)
                                for hh in range(2):
                                    head = g * 2 + hh
                                    w, kw = whg[hh]
                                    oa, koa = oaccr.next()
                                    S.op("dve", lambda e, oa=oa, hh=hh: e.tensor_copy(out=oa[:], in_=banks[2 + hh][:]), reads=[bk(2 + hh)], writes=[koa])
                                    sq, ksq = sqr.next()
                                    S.op("pool", lambda e, sq=sq, oa=oa: e.tensor_tensor(out=sq[:], in0=oa[:], in1=oa[:], op=ALU.mult), reads=[koa], writes=[ksq])
                                    S.op("pe", lambda e, sq=sq: e.matmul(banks[4][:], lhsT=ones_bf[:], rhs=sq[:], start=True, stop=True), reads=[ksq], writes=[bk(4)])
                                    t_, kt_ = tr_.next()
                                    S.op("dve", lambda e, t_=t_: e.tensor_scalar(out=t_[:], in0=banks[4][:], scalar1=1.0 / 128, scalar2=EPS, op0=ALU.mult, op1=ALU.add),
                                         reads=[bk(4)], writes=[kt_])
                                    S.op("act", lambda e, t_=t_: e.activation(out=t_[:], in_=t_[:], func=AF.Ln), reads=[kt_], writes=[kt_])
                                    S.op("act", lambda e, t_=t_: e.activation(out=t_[:], in_=t_[:], func=AF.Exp, scale=-0.5), reads=[kt_], writes=[kt_])

                                    def mmg(e, w=w, cols=cols):
                                        ins = None
                                        for kc in range(KC):
                                            ins = e.matmul(banks[5][:], lhsT=w[:, kc, :], rhs=hT[:, kc, cols], start=(kc == 0), stop=(kc == KC - 1))
                                        return ins
                                    S.op("pe", mmg, reads=[kw], writes=[bk(5)])
                                    te, kte = ter.next()
                                    sg, ksg = sgr.next()
                                    o1, ko1 = o1r.next()
                                    og, kog = ogr.next()
                                    act_sigmoid_inplace(te[:], kte, banks[5][:], [bk(5)])
                                    S.op("dve", lambda e, sg=sg, te=te: e.tensor_tensor(out=sg[:], in0=banks[5][:], in1=te[:], op=ALU.mult), reads=[bk(5), kte], writes=[ksg])
                                    S.op("dve", lambda e, o1=o1, oa=oa, t_=t_: e.scalar_tensor_tensor(out=o1[:], in0=oa[:], scalar=gout[:, 0:1], in1=t_[:],
                                                                                                       op0=ALU.mult, op1=ALU.mult),
                                         reads=[koa, kt_], writes=[ko1])
                                    S.op("pool", lambda e, og=og, o1=o1, sg=sg: e.tensor_tensor(out=og[:], in0=o1[:], in1=sg[:], op=ALU.mult), reads=[ko1, ksg], writes=[kog])
                                    dma("sp", OG_d[head][:, cols], og[:], reads=[kog], writes=[("OG", head, tg)])
                phase(p_mc)

            def p_md(sb):
                cos2 = sb("cos2d", [64, S_TOK], F32)
                sin2 = sb("sin2d", [64, S_TOK], F32)
                latT = sb("latTd", [128, 8, S_TOK], BF16)
                dma("sp", cos2[:], CS_d[0], writes=["cos2"])
                dma("sp", sin2[:], CS_d[1], writes=["sin2"])
                dma("sp", latT[:], LAT_d, writes=["latT"], flat=True)
                wait_all(["cos2", "sin2", "latT"])
                wqr = Ring(sb, "md_wq", 2, [128, 4, 256], BF16)
                wknr = Ring(sb, "md_wkn", 2, [128, 4, 128], BF16)
                wvr = Ring(sb, "md_wv", 2, [128, 4, 512], BF16)
                qnr = Ring(sb, "md_qn", 2, [128, S_TOK], BF16)
                qrr = Ring(sb, "md_qr", 2, [64, S_TOK], BF16)
                knr = Ring(sb, "md_kn", 2, [128, S_TOK], BF16)
                vtr = Ring(sb, "md_vt", 2, [128, NT, 512], BF16)
                ptr = Ring(sb, "md_pt", 4, [128, 512], BF16)
                t1r = Ring(sb, "md_t1", 2, [64, 512], F32)
                t2r = Ring(sb, "md_t2", 2, [64, 512], F32)
                rsr = Ring(sb, "md_rs", 2, [128, 512], F32)
                osr = Ring(sb, "md_os", 2, [128, 512], BF16)
                pacr = Ring(sb, "md_pa", 2, [128, 512], F32)
                ones_f = sb("md_ones_f", [128, 128], F32)
                S.op("pool", lambda e: e.memset(ones_f[:], 1.0), writes=["ones_f"])
                zi = 0
                ai = 0
                vt = kvt = None
                for h in range(NH):
                    wq, kwq = wqr.next()
                    wkn, kwkn = wknr.next()
                    dma("pool", wq[:], wq_d[h], writes=[kwq], flat=True)
                    dma("pool", wkn[:], wkn_d[h], writes=[kwkn], flat=True)
                    if h % 4 == 0:
                        wv, kwv = wvr.next()
                        dma("pool", wv[:], wv_d[h // 4], writes=[kwv], flat=True)
                        vt, kvt = vtr.next()
                        for tt in range(NT):
                            b = zi % 4
                            zi += 1

                            def mmv(e, b=b, tt=tt, wv=wv):
                                ins = None
                                for kc in range(4):
                                    ins = e.matmul(banks[b][:], lhsT=latT[:, 4 + kc, tt * 128:(tt + 1) * 128], rhs=wv[:, kc, :], start=(kc == 0), stop=(kc == 3))
                                return ins
                            S.op("pe", mmv, reads=[kwv], writes=[bk(b)])
                            if tt % 2 == 0:
                                S.op("act", lambda e, vt=vt, b=b, tt=tt: e.copy(out=vt[:, tt, :], in_=banks[b][:]), reads=[bk(b)], writes=[(kvt, tt)])
                            else:
                                S.op("dve", lambda e, vt=vt, b=b, tt=tt: e.tensor_copy(out=vt[:, tt, :], in_=banks[b][:]), reads=[bk(b)], writes=[(kvt, tt)])
                    hh = h % 4
                    qn, kqn = qnr.next()
                    qr, kqr = qrr.next()
                    kn, kkn = knr.next()
                    for tg in range(NG):
                        cols = slice(tg * 512, (tg + 1) * 512)
                        b = zi % 4
                        zi += 1

                        def mmq(e, b=b, wq=wq, cols=cols):
                            ins = None
                            for kc in range(4):
                                ins = e.matmul(banks[b][:], lhsT=wq[:, kc, 0:128], rhs=latT[:, kc, cols], start=(kc == 0), stop=(kc == 3))
                            return ins
                        S.op("pe", mmq, reads=[kwq], writes=[bk(b)])
                        S.op("act", lambda e, qn=qn, b=b, cols=cols: e.copy(out=qn[:, cols], in_=banks[b][:]), reads=[bk(b)], writes=[(kqn, tg)])
                        b = zi % 4
                        zi += 1

                        def mmk(e, b=b, wkn=wkn, cols=cols):
                            ins = None
                            for kc in range(4):
                                ins = e.matmul(banks[b][:], lhsT=wkn[:, kc, :], rhs=latT[:, 4 + kc, cols], start=(kc == 0), stop=(kc == 3))
                            return ins
                        S.op("pe", mmk, reads=[kwkn], writes=[bk(b)])
                        S.op("dve", lambda e, kn=kn, b=b, cols=cols: e.tensor_copy(out=kn[:, cols], in_=banks[b][:]), reads=[bk(b)], writes=[(kkn, tg)])
                        bA = zi % 4
                        zi += 1
                        bB = zi % 4
                        zi += 1

                        def mmr(e, b, c0, wq=wq, cols=cols):
                            ins = None
                            for kc in range(4):
                                ins = e.matmul(banks[b][0:64, :], lhsT=wq[:, kc, c0:c0 + 64], rhs=latT[:, kc, cols], start=(kc == 0), stop=(kc == 3))
                            return ins
                        S.op("pe", lambda e, mmr=mmr, bA=bA: mmr(e, bA, 128), reads=[kwq], writes=[bk(bA)])
                        S.op("pe", lambda e, mmr=mmr, bB=bB: mmr(e, bB, 192), reads=[kwq], writes=[bk(bB)])
                        t1, k1 = t1r.next()
                        t2, k2 = t2r.next()
                        S.op("dve", lambda e, t1=t1, bA=bA, cols=cols: e.tensor_tensor(out=t1[:], in0=banks[bA][0:64, :], in1=cos2[:, cols], op=ALU.mult),
                             reads=[bk(bA)], writes=[k1])
                        S.op("dve", lambda e, t2=t2, bB=bB, cols=cols: e.tensor_tensor(out=t2[:], in0=banks[bB][0:64, :], in1=sin2[:, cols], op=ALU.mult),
                             reads=[bk(bB)], writes=[k2])
                        S.op("pool", lambda e, qr=qr, t1=t1, t2=t2, cols=cols: e.tensor_tensor(out=qr[:, cols], in0=t1[:], in1=t2[:], op=ALU.add),
                             reads=[k1, k2], writes=[(kqr, tg)])
                    allq = [(kqn, t) for t in range(NG)] + [(kqr, t) for t in range(NG)] + [(kkn, t) for t in range(NG)]
                    steps = [(qg, kb) for qg in range(NG) for kb in range(4 * (qg + 1))]
                    qk = {}
                    accs = {}

                    def rec_qk(si, zi0):
                        qg, kb = steps[si]
                        i = kb - 4 * qg
                        c0 = max(i, 0) * 128
                        kcols = slice(kb * 128, (kb + 1) * 128)
                        qcols = slice(qg * 512 + c0, (qg + 1) * 512)
                        b = zi0 % 4

                        def mms(e, b=b, c0=c0, kcols=kcols, qcols=qcols, qn=qn, kn=kn, qr=qr):
                            e.matmul(banks[b][:, c0:512], lhsT=kn[:, kcols], rhs=qn[:, qcols], start=True, stop=False)
                            return e.matmul(banks[b][:, c0:512], lhsT=krT[:, kcols], rhs=qr[:, qcols], start=False, stop=True)
                        S.op("pe", mms, reads=allq, writes=[bk(b)])
                        pt, kpt = ptr.next()
                        S.op("act", lambda e, pt=pt, b=b, c0=c0: e.activation(out=pt[:, c0:512], in_=banks[b][:, c0:512], func=AF.Exp, scale=ATT_SCALE),
                             reads=[bk(b)], writes=[kpt])
                        if i >= 0:
                            S.op("pool", lambda e, pt=pt, c0=c0: e.tensor_tensor(out=pt[:, c0:c0 + 128], in0=pt[:, c0:c0 + 128], in1=caus[:], op=ALU.mult),
                                 reads=[kpt], writes=[kpt])
                        qk[si] = (pt, kpt, c0)

                    rec_qk(0, zi)
                    zi += 1
                    for si, (qg, kb) in enumerate(steps):
                        if si + 1 < len(steps):
                            rec_qk(si + 1, zi)
                            zi += 1
                        nkb = 4 * (qg + 1)
                        if kb == 0:
                            bO = 4 + 2 * (ai % 2)
                            bS = bO + 1
                            ai += 1
                            pa, kpa = pacr.next()
                            accs[qg] = (bO, bS, pa, kpa)
                        bO, bS, pa, kpa = accs[qg]
                        pt, kpt, c0 = qk.pop(si)

                        def mmpv(e, pt=pt, c0=c0, kb=kb, nkb=nkb, bO=bO, vt=vt, hh=hh):
                            return e.matmul(banks[bO][:, c0:512], lhsT=vt[:, kb, hh * 128:(hh + 1) * 128], rhs=pt[:, c0:512], start=(kb == 0), stop=(kb == nkb - 1),
                                            skip_group_check=True)
                        S.op("pe", mmpv, reads=[kpt, (kvt, kb)], writes=[bk(bO)])
                        if kb == 0:
                            S.op("dve", lambda e, pa=pa, pt=pt: e.tensor_copy(out=pa[:], in_=pt[:]), reads=[kpt], writes=[kpa])
                        else:
                            S.op("dve", lambda e, pa=pa, pt=pt, c0=c0: e.tensor_tensor(out=pa[:, c0:512], in0=pa[:, c0:512], in1=pt[:, c0:512], op=ALU.add),
                                 reads=[kpt, kpa], writes=[kpa])
                        if kb == nkb - 1:
                            S.op("pe", lambda e, pa=pa, bS=bS: e.matmul(banks[bS][:], lhsT=ones_f[:], rhs=pa[:], start=True, stop=True), reads=[kpa, "ones_f"], writes=[bk(bS)])
                            rs, krs = rsr.next()
                            os_, kos = osr.next()
                            S.op("dve", lambda e, rs=rs, bS=bS: e.reciprocal(out=rs[:], in_=banks[bS][:]), reads=[bk(bS)], writes=[krs])
                            S.op("dve", lambda e, os_=os_, rs=rs, bO=bO: e.tensor_tensor(out=os_[:], in0=banks[bO][:], in1=rs[:], op=ALU.mult),
                                 reads=[bk(bO), krs], writes=[kos])
                            dma("sp", OA_d[h][:, qg * 512:(qg + 1) * 512], os_[:], reads=[kos], writes=[("OA", h, qg)])
            phase(p_md)
        def p_me(sb):
            OAs = sb("me_oa", [128, 16, S_TOK], BF16)
            OGs = sb("me_og", [128, 16, S_TOK], BF16)
            for h in range(16):
                dma("sp", OAs[:, h, :], OA_d[h], writes=[("oas", h)])
                dma("sp", OGs[:, h, :], OG_d[h], writes=[("ogs", h)])
            allo = [("oas", h) for h in range(16)] + [("ogs", h) for h in range(16)]
            war = Ring(sb, "me_wa", 2, [128, KC, 128], BF16)
            wbr = Ring(sb, "me_wb", 2, [128, KC, 128], BF16)
            gar = Ring(sb, "me_ga", 2, [128, S_TOK], BF16)
            gbr = Ring(sb, "me_gb", 2, [128, S_TOK], BF16)
            m1r = Ring(sb, "me_m1", 2, [128, 512], F32)
            m2r = Ring(sb, "me_m2", 2, [128, 512], F32)
            mgr = Ring(sb, "me_mg", 2, [128, S_TOK], BF16)
            it = 0
            pre = {}

            def load_dc(dc):
                if dc >= 16 or dc in pre:
                    return
                wa, kwa = war.next()
                wb, kwb = wbr.next()
                ga, kga = gar.next()
                gb, kgb = gbr.next()
                dma("pool", wa[:], wo_d[dc], writes=[kwa], flat=True)
                dma("pool", wb[:], wob_d[dc], writes=[kwb], flat=True)
                dma("sp", ga[:], GG_d[dc], writes=[kga])
                dma("sp", gb[:], GG_d[16 + dc], writes=[kgb])
                pre[dc] = (wa, kwa, wb, kwb, ga, kga, gb, kgb)
            load_dc(0)
            for dc in range(16):
                load_dc(dc + 1)
                wa, kwa, wb, kwb, ga, kga, gb, kgb = pre.pop(dc)
                mg, kmg = mgr.next()
                for tg in range(NG):
                    cols = slice(tg * 512, (tg + 1) * 512)
                    iA, iB = 2 * (it % 4), 2 * (it % 4) + 1
                    it += 1

                    def mm(e, w, src, b, cols=cols):
                        ins = None
                        for kc in range(KC):
                            ins = e.matmul(banks[b][:], lhsT=w[:, kc, :], rhs=src[:, kc, cols], start=(kc == 0), stop=(kc == KC - 1))
                        return ins
                    S.op("pe", lambda e, mm=mm, wa=wa, iA=iA: mm(e, wa, OAs, iA), reads=[kwa] + allo, writes=[bk(iA)])
                    S.op("pe", lambda e, mm=mm, wb=wb, iB=iB: mm(e, wb, OGs, iB), reads=[kwb] + allo, writes=[bk(iB)])
                    m1, km1 = m1r.next()
                    m2, km2 = m2r.next()
                    S.op("dve", lambda e, m1=m1, iA=iA, ga=ga, cols=cols: e.tensor_tensor(out=m1[:], in0=banks[iA][:], in1=ga[:, cols], op=ALU.mult),
                         reads=[bk(iA), kga], writes=[km1])
                    S.op("dve", lambda e, m2=m2, iB=iB, gb=gb, cols=cols: e.tensor_tensor(out=m2[:], in0=banks[iB][:], in1=gb[:, cols], op=ALU.mult),
                         reads=[bk(iB), kgb], writes=[km2])
                    S.op("pool", lambda e, mg=mg, m1=m1, m2=m2, cols=cols: e.tensor_tensor(out=mg[:, cols], in0=m1[:], in1=m2[:], op=ALU.add),
                         reads=[km1, km2], writes=[(kmg, tg)])
                dma("sp", MT_d[dc], mg[:], reads=[(kmg, tg) for tg in range(NG)], writes=[("MT", dc)])
        phase(p_me)

        def p_mf(sb):
            mT = sb("mf_mT", [128, 16, S_TOK], BF16)
            for dc in range(16):
                dma("sp", mT[:, dc, :], MT_d[dc], writes=[("mT", dc)])
            S.op("pe", lambda e: e.nop(), reads=[("mT", dc) for dc in range(16)])
            dn_phase(sb, KC, wout_d, None, a_res=mT)
        phase(p_mf)
        if stage == 2:
            phase(lambda sb: rn_pass(sb, None, X1_d, Y_d, 3, 1.0, out_d, None))
            return nc

        with ExitStack() as stE:
            hT = stE.enter_context(nc.sbuf_tensor("hT_e", [128, KC, S_TOK], BF16))
            phase(lambda sb: rn_pass(sb, hT, X1_d, Y_d, 3, 1.0, X2_d, 4))
            phase(lambda sb: gu_phase(sb, hT, wg2, wu2))
        phase(lambda sb: dn_phase(sb, FC, wd2, A1_d))
        phase(lambda sb: rn_pass(sb, None, X2_d, Y_d, 5, 0.5, out_d, None))
    return nc


def _fm(W, kc):
    K, N = W.shape
    return np.ascontiguousarray(W.reshape(kc, 128, N // 128, 128).transpose(2, 1, 0, 3))


def _tm(W, kc, n=512):
    K, N = W.shape
    return np.ascontiguousarray(W.reshape(kc, 128, N // n, n).transpose(2, 1, 0, 3))


def _prep_shared(inp):
    f32 = np.float32
    sh = {}
    gains = [inp[k][0] for k in ("ffn1_norm_pre", "ffn1_norm_post", "mix_norm_pre", "mix_norm_post", "ffn2_norm_pre", "ffn2_norm_post")]
    sh["gains"] = np.ascontiguousarray(np.broadcast_to(np.stack(gains)[:, None, :], (6, 128, D))).astype(f32, copy=False)
    glat = np.concatenate([inp["mla_q_norm"][0], inp["mla_kv_norm"][0]])
    sh["glat"] = np.ascontiguousarray(np.broadcast_to(glat[None, :], (128, 1024)))
    sh["lbl"] = np.ascontiguousarray(np.broadcast_to(inp["hgrn_lb_logits"][:, None, :], (2, 128, D)))
    sh["gout"] = np.ascontiguousarray(inp["hgrn_out_norm"][0].reshape(128, 1))
    sh["ident"] = np.eye(128, dtype=f32)
    s = np.arange(128)[:, None]
    t = np.arange(128)[None, :]
    same = (s // 64) == (t // 64)
    sh["tri"] = (same & (s <= t)).astype(f32)
    sh["tris"] = (same & (s > t)).astype(f32)
    sh["caus"] = (t >= s).astype(f32)
    sh["blk2"] = ((np.arange(128)[:, None] // 64) == np.arange(2)[None, :]).astype(f32)
    cst = np.zeros((64, 8), f32)
    half = 32
    invf = (10000.0 ** (-(np.arange(half, dtype=np.float32)) / half)).astype(f32)
    cst[:, 0] = np.concatenate([invf, invf])
    cst[:32, 1] = 1.0
    cst[32:, 1] = -1.0
    cst[:32, 2] = -math.pi
    cst[32:, 2] = math.pi
    cst[:, 3] = -1.0
    cst[:, 4] = math.pi
    sh["cst"] = cst
    for i in (1, 2):
        sh[f"wg{i}"] = _fm(inp[f"ffn{i}_w_gate"][0], KC)
        sh[f"wu{i}"] = _fm(inp[f"ffn{i}_w_up"][0], KC)
        sh[f"wd{i}"] = _tm(inp[f"ffn{i}_w_down"][0], FC)
    w_in = inp["w_in"][0]
    o = 0
    sh["wlat"] = _tm(w_in[:, 0:1024], KC)
    kr = w_in[:, 1024:1088]
    krs = np.concatenate([kr[:, 32:64], kr[:, 0:32]], axis=1)
    sh["wkr"] = np.ascontiguousarray(np.concatenate([kr, krs], axis=1).reshape(KC, 128, 128).transpose(1, 0, 2))
    o = 1088
    hq = w_in[:, o:o + 2048]; hf = w_in[:, o + 2048:o + 4096]; hi = w_in[:, o + 4096:o + 6144]; hg = w_in[:, o + 6144:o + 8192]
    ga = w_in[:, o + 8192:o + 10240]; gb = w_in[:, o + 10240:o + 12288]
    sh["wqf"] = np.concatenate([_tm(hq, KC, 256), _tm(hf, KC, 256)], axis=3)
    sh["wi4"] = _tm(hi, KC, 512)
    sh["whg"] = _fm(hg, KC)
    sh["wgate"] = np.concatenate([_fm(ga, KC), _fm(gb, KC)], axis=0)
    wq = inp["mla_w_q_up"][0].reshape(512, 16, 192)
    qn = wq[:, :, 0:128]; qr = wq[:, :, 128:192]
    qrs = np.concatenate([qr[:, :, 32:64], qr[:, :, 0:32]], axis=2)
    wq_all = np.concatenate([qn, qr, qrs], axis=2)
    sh["wq"] = np.ascontiguousarray(wq_all.reshape(4, 128, 16, 256).transpose(2, 1, 0, 3))
    wkv = inp["mla_w_kv_up"][0].reshape(512, 16, 256)
    sh["wkn"] = np.ascontiguousarray(wkv[:, :, 0:128].reshape(4, 128, 16, 128).transpose(2, 1, 0, 3))
    wv = wkv[:, :, 128:256].reshape(512, 4, 512)
    sh["wv"] = np.ascontiguousarray(wv.reshape(4, 128, 4, 512).transpose(2, 1, 0, 3))
    sh["wo"] = _fm(inp["mla_w_o"][0], KC)
    sh["wob"] = _fm(inp["hgrn_w_o"][0], KC)
    sh["wout"] = _tm(inp["w_out"][0], KC)
    return {k: np.ascontiguousarray(v, dtype=np.float32) for k, v in sh.items()}


def run(inputs, n_cores=8, stage=3, trace=False):
    inp = {k: np.asarray(v) for k, v in inputs.items()}
    sh = _prep_shared(inp)
    nc = build(stage)
    in_maps = []
    for b in range(n_cores):
        m = dict(sh)
        m["x"] = np.ascontiguousarray(inp["x"][b], dtype=np.float32)
        m["pos"] = np.ascontiguousarray(np.broadcast_to(inp["positions"][b][None, :], (64, S_TOK))).astype(np.int32, copy=False)
        in_maps.append(m)
    res = run_bass_kernel_spmd(nc, in_maps, core_ids=list(range(n_cores)), **({"trace": True} if trace else {}))
    out = np.stack([np.asarray(r["out"]) for r in res.results], axis=0)
    return out, res


def kernel(**inputs):
    out, _ = run(inputs, n_cores=8, stage=3)
    return out.astype(np.float32, copy=False)
```

```python
import math
from contextlib import ExitStack
import numpy as np
import concourse.bass as bass
import concourse.mybir as mybir
from concourse.bass_utils import run_bass_kernel_spmd

F32 = mybir.dt.float32
BF16 = mybir.dt.bfloat16
I32 = mybir.dt.int32
AF = mybir.ActivationFunctionType
ALU = mybir.AluOpType
AX = mybir.AxisListType

S_TOK = 2048
D = 2048
KC = 16
FF = 5632
FC = 44
NT = 16
NG = 4
EPS = 1e-6
NH = 16
ATT_SCALE = 192.0 ** -0.5


class Op:
    __slots__ = ("eng", "fn", "deps", "sig", "sigval", "dma_sem")


class Sched:
    ENGS = ("pe", "act", "dve", "pool", "sp")

    def __init__(self, nc, stack, n_dma_sems=8):
        self.nc = nc
        self.ops = {e: [] for e in self.ENGS}
        self.last_w = {}
        self.readers = {}
        self.sems = {e: stack.enter_context(nc.semaphore("s_" + e)) for e in ("pe", "act", "dve", "pool")}
        self.dma_sems = {}
        self.dma_cnt = {}
        self.dma_ring = {}
        for q in ("sp", "pool"):
            self.dma_sems[q] = [stack.enter_context(nc.semaphore(f"d_{q}{i}")) for i in range(n_dma_sems)]
            self.dma_cnt[q] = 0
            self.dma_ring[q] = [None] * n_dma_sems
        self.emitted = {e: 0 for e in self.ENGS}
        self.sigcnt = {e: 0 for e in self.ENGS}
        self.waited = {e: {} for e in self.ENGS}

    def op(self, eng, fn, reads=(), writes=(), dma=False):
        o = Op()
        o.eng = eng; o.fn = fn; o.sig = False; o.sigval = None; o.dma_sem = None
        deps = set()
        for r in reads:
            w = self.last_w.get(r)
            if w is not None:
                deps.add(w)
        for w_ in writes:
            w = self.last_w.get(w_)
            if w is not None:
                deps.add(w)
            for rd in self.readers.get(w_, ()):
                deps.add(rd)
        if dma:
            n = len(self.dma_sems[eng])
            i = self.dma_cnt[eng]
            slot = i % n
            prev = self.dma_ring[eng][slot]
            if prev is not None:
                deps.add(prev)
            o.dma_sem = (self.dma_sems[eng][slot], 16 * (i // n + 1))
            self.dma_ring[eng][slot] = o
            self.dma_cnt[eng] = i + 1
        o.deps = deps
        for d in deps:
            if d.dma_sem is None:
                d.sig = True
        for r in reads:
            self.readers.setdefault(r, []).append(o)
        for w_ in writes:
            self.last_w[w_] = o
            self.readers[w_] = []
        self.ops[eng].append(o)
        return o

    def finish_phase(self):
        lasts = []
        for e in ("pe", "act", "dve", "pool"):
            if len(self.ops[e]) > self.emitted[e]:
                lasts.append(self.ops[e][-1])
        for q in self.dma_ring:
            for o in self.dma_ring[q]:
                if o is not None:
                    lasts.append(o)
        for e in self.ENGS:
            o = self.op(e, lambda eng: eng.nop())
            o.deps = set(lasts)
            for d in o.deps:
                if d.dma_sem is None:
                    d.sig = True
        self.last_w = {}
        self.readers = {}

    def emit(self, block):
        for e in ("pe", "act", "dve", "pool"):
            c = self.sigcnt[e]
            for o in self.ops[e][self.emitted[e]:]:
                if o.dma_sem is None and o.sig:
                    c += 1
                    o.sigval = c
            self.sigcnt[e] = c

        def run_engine(ename, eng):
            waited = self.waited[ename]
            for o in self.ops[ename][self.emitted[ename]:]:
                for d in o.deps:
                    if d.dma_sem is not None:
                        sem, val = d.dma_sem
                    else:
                        if d.eng == "pe" and ename == "pe":
                            continue
                        sem, val = self.sems[d.eng], d.sigval
                    key = id(sem)
                    if waited.get(key, 0) >= val:
                        continue
                    waited[key] = val
                    eng.wait_ge(sem, val)
                ins = o.fn(eng)
                if o.dma_sem is not None:
                    ins.then_inc(o.dma_sem[0], 16)
                elif o.sig:
                    ins.then_inc(self.sems[ename], 1)
                o.fn = None
            self.emitted[ename] = len(self.ops[ename])

        @block.tensor
        def _(eng):
            run_engine("pe", eng)

        @block.scalar
        def _(eng):
            run_engine("act", eng)

        @block.vector
        def _(eng):
            run_engine("dve", eng)

        @block.gpsimd
        def _(eng):
            run_engine("pool", eng)

        @block.sync
        def _(eng):
            run_engine("sp", eng)


class Ring:
    def __init__(self, sb, name, n, shape, dtype):
        self.tiles = [sb(f"{name}{i}", shape, dtype) for i in range(n)]
        self.name = name
        self.i = 0

    def next(self):
        j = self.i % len(self.tiles)
        self.i += 1
        return self.tiles[j], (self.name, j)


def build(stage=3):
    nc = bass.Bass("TRN2", target_bir_lowering=False)
    dt_in = lambda name, shape, dt=F32: nc.dram_tensor(name, list(shape), dt, kind="ExternalInput").ap()
    scr = lambda name, shape, dt: nc.dram_tensor(name, list(shape), dt).ap()
    x_d = dt_in("x", [S_TOK, D])
    pos_d = dt_in("pos", [64, S_TOK], I32)
    gains_d = dt_in("gains", [6, 128, D])
    glat_d = dt_in("glat", [128, 1024])
    lbl_d = dt_in("lbl", [2, 128, D])
    gout_d = dt_in("gout", [128, 1])
    ident_d = dt_in("ident", [128, 128])
    tri_d = dt_in("tri", [128, 128])
    tris_d = dt_in("tris", [128, 128])
    caus_d = dt_in("caus", [128, 128])
    blk2_d = dt_in("blk2", [128, 2])
    cst_d = dt_in("cst", [64, 8])
    ffn_w = []
    for i in (1, 2):
        ffn_w.append((dt_in(f"wg{i}", [FC, 128, KC, 128]), dt_in(f"wu{i}", [FC, 128, KC, 128]),
                      dt_in(f"wd{i}", [4, 128, FC, 512])))
    wlat_d = dt_in("wlat", [2, 128, KC, 512])
    wkr_d = dt_in("wkr", [128, KC, 128])
    wgate_d = dt_in("wgate", [32, 128, KC, 128])
    wqf_d = dt_in("wqf", [8, 128, KC, 512])
    wi4_d = dt_in("wi4", [4, 128, KC, 512])
    whg_d = dt_in("whg", [16, 128, KC, 128])
    wq_d = dt_in("wq", [16, 128, 4, 256])
    wkn_d = dt_in("wkn", [16, 128, 4, 128])
    wv_d = dt_in("wv", [4, 128, 4, 512])
    wo_d = dt_in("wo", [16, 128, KC, 128])
    wob_d = dt_in("wob", [16, 128, KC, 128])
    wout_d = dt_in("wout", [4, 128, KC, 512])
    out_d = nc.dram_tensor("out", [S_TOK, D], F32, kind="ExternalOutput").ap()
    A1_d = scr("A1", [NT, 128, FC, 128], BF16)
    Y_d = scr("Y", [S_TOK, D], F32)
    X1_d = scr("X1", [S_TOK, D], F32)
    X2_d = scr("X2", [S_TOK, D], F32)
    GG_d = scr("GG", [32, 128, S_TOK], BF16)
    OG_d = scr("OG", [16, 128, S_TOK], BF16)
    OA_d = scr("OA", [16, 128, S_TOK], BF16)
    MT_d = scr("MT", [16, 128, S_TOK], BF16)
    CS_d = scr("CS", [2, 64, S_TOK], F32)
    LAT_d = scr("LAT", [128, 8, S_TOK], BF16)

    with ExitStack() as st0:
        S = Sched(nc, st0)
        uid = [0]

        def uname(name):
            uid[0] += 1
            return f"s{uid[0]}_{name}"
        sb0 = lambda name, shape, dt: st0.enter_context(nc.sbuf_tensor(uname(name), list(shape), dt))
        banks = [st0.enter_context(nc.psum_tensor(f"bank{i}", [128, 512], F32)) for i in range(8)]
        bk = lambda i: ("bank", i)
        ident = sb0("ident", [128, 128], BF16)
        ones_bf = sb0("ones_bf", [128, 128], BF16)
        tri = sb0("tri", [128, 128], F32)
        tris = sb0("tris", [128, 128], F32)
        caus = sb0("caus", [128, 128], BF16)
        blk2 = sb0("blk2", [128, 2], F32)
        cst = sb0("cst_s", [64, 8], F32)
        gout = sb0("gout_s", [128, 1], F32)
        stat = sb0("stat", [128, 64], F32)

        def phase(fn):
            with ExitStack() as st:
                sb = lambda name, shape, dt: st.enter_context(nc.sbuf_tensor(uname(name), list(shape), dt))
                fn(sb)
                S.finish_phase()
                with nc.Block() as block:
                    S.emit(block)

        def dma(q, out, in_, reads=(), writes=(), flat=False):
            if flat:
                out = out.rearrange("p a b -> p (a b)")
                in_ = in_.rearrange("p a b -> p (a b)")
            return S.op(q, lambda e: e.dma_start(out=out, in_=in_), reads=reads, writes=writes, dma=True)

        def wait_all(keys):
            for en in ("pe", "act", "dve", "pool"):
                S.op(en, lambda e: e.nop(), reads=list(keys))

        def p_consts(sb):
            dma("pool", ident[:], ident_d, writes=["ident"])
            dma("pool", caus[:], caus_d, writes=["caus"])
            dma("sp", tri[:], tri_d, writes=["tri"])
            dma("sp", tris[:], tris_d, writes=["tris"])
            dma("sp", blk2[:], blk2_d, writes=["blk2"])
            dma("sp", cst[:], cst_d, writes=["cst"])
            dma("sp", gout[:], gout_d, writes=["gout"])
            S.op("dve", lambda e: e.memset(ones_bf[:], 1.0), writes=["ones"])
            S.op("dve", lambda e: e.memset(stat[:], 0.0), writes=["stat"])
        phase(p_consts)

        def rstd_chain(sc, ksc, c_in, c0, mul, add, extra_reads=()):
            S.op("dve", lambda e: e.tensor_scalar(out=sc[:, c0:c0 + 1], in0=sc[:, c_in:c_in + 1], scalar1=mul, scalar2=add,
                                                   op0=ALU.mult, op1=ALU.add),
                 reads=[(ksc, c_in)] + list(extra_reads), writes=[(ksc, c0)])
            S.op("act", lambda e: e.activation(out=sc[:, c0 + 1:c0 + 2], in_=sc[:, c0:c0 + 1], func=AF.Sqrt),
                 reads=[(ksc, c0)], writes=[(ksc, c0 + 1)])
            S.op("dve", lambda e: e.reciprocal(out=sc[:, c0 + 2:c0 + 3], in_=sc[:, c0 + 1:c0 + 2]),
                 reads=[(ksc, c0 + 1)], writes=[(ksc, c0 + 2)])

        def rn_pass(sb, hT, src_x, y_d, post_idx, post_scale, dst_x, pre_idx):
            xring = Ring(sb, "rn_x", 3, [128, D], F32)
            yring = Ring(sb, "rn_y", 3, [128, D], F32) if y_d is not None else None
            xsring = Ring(sb, "rn_xs", 2, [128, D], BF16)
            scring = Ring(sb, "rn_sc", 4, [128, 12], F32)
            junk = sb("rn_junk", [128, D], BF16)
            gpost = sb("rn_gpost", [128, D], F32) if y_d is not None else None
            gpre = sb("rn_gpre", [128, D], F32) if pre_idx is not None else None
            if gpost is not None:
                dma("sp", gpost[:], gains_d[post_idx], writes=["gpost"])
            if gpre is not None:
                dma("sp", gpre[:], gains_d[pre_idx], writes=["gpre"])
            pre = {}

            def load_t(tt):
                if tt >= NT or tt in pre:
                    return
                rows = slice(tt * 128, (tt + 1) * 128)
                xt, kx = xring.next()
                dma("sp", xt[:], src_x[rows, :], writes=[kx])
                yt = ky = None
                if y_d is not None:
                    yt, ky = yring.next()
                    dma("sp", yt[:], y_d[rows, :], writes=[ky])
                pre[tt] = (xt, kx, yt, ky)
            load_t(0)
            for tt in range(NT):
                rows = slice(tt * 128, (tt + 1) * 128)
                load_t(tt + 1)
                xt, kx, yt, ky = pre.pop(tt)
                sc, ksc = scring.next()
                S.op("pool", lambda e, sc=sc: e.memset(sc[:], 0.0), writes=[(ksc, i) for i in range(12)])
                if y_d is not None:
                    S.op("dve", lambda e, sc=sc, tt=tt: e.reduce_sum(out=sc[:, 0:1], in_=stat[:, tt * 4:(tt + 1) * 4], axis=AX.X),
                         reads=[], writes=[(ksc, 0)])
                    s2 = post_scale * post_scale
                    rstd_chain(sc, ksc, 0, 1, 1.0 / (D * s2), EPS / s2)
                    S.op("dve", lambda e, yt=yt, sc=sc: e.scalar_tensor_tensor(out=yt[:], in0=yt[:], scalar=sc[:, 3:4], in1=gpost[:],
                                                                               op0=ALU.mult, op1=ALU.mult),
                         reads=[ky, (ksc, 3), "gpost"], writes=[ky])
                    S.op("pool", lambda e, xt=xt, yt=yt: e.tensor_tensor(out=xt[:], in0=xt[:], in1=yt[:], op=ALU.add),
                         reads=[kx, ky], writes=[kx])
                if dst_x is not None:
                    dma("sp", dst_x[rows, :], xt[:], reads=[kx], writes=[("dst", tt)])
                if pre_idx is not None:
                    S.op("act", lambda e, xt=xt, sc=sc: e.activation(out=junk[:], in_=xt[:], func=AF.Square, accum_out=sc[:, 4:5]),
                         reads=[kx], writes=["junk", (ksc, 4)])
                    rstd_chain(sc, ksc, 4, 5, 1.0 / D, EPS)
                    xs, kxs = xsring.next()
                    S.op("dve", lambda e, xs=xs, xt=xt, sc=sc: e.scalar_tensor_tensor(out=xs[:], in0=xt[:], scalar=sc[:, 7:8], in1=gpre[:],
                                                                                       op0=ALU.mult, op1=ALU.mult),
                         reads=[kx, (ksc, 7), "gpre"], writes=[kxs])
                    for half in range(2):
                        bi = (tt * 2 + half) % 4
                        bv = banks[bi][:].bitcast(BF16)

                        def tr(e, xs=xs, bv=bv, half=half):
                            ins = None
                            for j in range(8):
                                c = half * 8 + j
                                ins = e.transpose(bv[:, j * 128:(j + 1) * 128], xs[:, c * 128:(c + 1) * 128], ident[:])
                            return ins
                        S.op("pe", tr, reads=[kxs], writes=[bk(bi)])
                        dst = hT[:, half * 8:(half + 1) * 8, tt * 128:(tt + 1) * 128]
                        src = bv.rearrange("p (a b) -> p a b", b=128)
                        if half == 0:
                            S.op("act", lambda e, dst=dst, src=src: e.copy(out=dst, in_=src), reads=[bk(bi)], writes=[("hT", tt, half)])
                        else:
                            S.op("dve", lambda e, dst=dst, src=src: e.tensor_copy(out=dst, in_=src), reads=[bk(bi)], writes=[("hT", tt, half)])

        def gu_phase(sb, hT, wg_d, wu_d):
            wgr = Ring(sb, "gu_wg", 3, [128, KC, 128], BF16)
            wur = Ring(sb, "gu_wu", 3, [128, KC, 128], BF16)
            sgr = Ring(sb, "gu_sg", 2, [128, 512], F32)
            atr = Ring(sb, "gu_at", 3, [128, 512], BF16)
            it = 0
            for fc in range(FC):
                wg, kwg = wgr.next()
                wu, kwu = wur.next()
                dma("pool", wg[:], wg_d[fc], writes=[kwg], flat=True)
                dma("pool", wu[:], wu_d[fc], writes=[kwu], flat=True)
                for tg in range(NG):
                    cols = slice(tg * 512, (tg + 1) * 512)
                    iA, iB = 2 * (it % 4), 2 * (it % 4) + 1
                    it += 1

                    def mm(e, w, b, cols=cols):
                        ins = None
                        for kc in range(KC):
                            ins = e.matmul(banks[b][:], lhsT=w[:, kc, :], rhs=hT[:, kc, cols], start=(kc == 0), stop=(kc == KC - 1))
                        return ins
                    S.op("pe", lambda e, wg=wg, iA=iA, mm=mm: mm(e, wg, iA), reads=[kwg], writes=[bk(iA)])
                    S.op("pe", lambda e, wu=wu, iB=iB, mm=mm: mm(e, wu, iB), reads=[kwu], writes=[bk(iB)])
                    sg, ksg = sgr.next()
                    at, kat = atr.next()
                    S.op("act", lambda e, sg=sg, iA=iA: e.activation(out=sg[:], in_=banks[iA][:], func=AF.Silu), reads=[bk(iA)], writes=[ksg])
                    S.op("dve", lambda e, at=at, sg=sg, iB=iB: e.tensor_tensor(out=at[:], in0=sg[:], in1=banks[iB][:], op=ALU.mult),
                         reads=[ksg, bk(iB)], writes=[kat])
                    dst = A1_d[tg * 4:(tg + 1) * 4, :, fc, :].rearrange("t p n -> p t n")
                    src = at[:].rearrange("p (t n) -> p t n", n=128)
                    dma("sp", dst, src, reads=[kat], writes=[("A1", fc, tg)])

        def dn_phase(sb, kcn, w_d, a_src, a_res=None):
            nq = 4
            per = kcn // nq
            wr = Ring(sb, "dn_w", 2, [128, kcn, 512], BF16)
            ar = Ring(sb, "dn_a", 3, [128, kcn, 128], BF16) if a_res is None else None
            ysr = Ring(sb, "dn_ys", 3, [128, 512], F32)
            junk = sb("dn_junk", [128, 512], BF16)
            S.op("pool", lambda e: e.memset(stat[:], 0.0), writes=["stat"])
            iters = [(dg, tt) for dg in range(4) for tt in range(NT)]
            loaded = {}

            def load_a(j):
                if a_res is not None or j >= len(iters) or j in loaded:
                    return
                at, ka = ar.next()
                dma("sp", at[:], a_src[iters[j][1]], writes=[ka], flat=True)
                loaded[j] = (at, ka)
            wts = {}

            def load_w(dg):
                if dg >= 4 or dg in wts:
                    return
                w, kw = wr.next()
                for q in range(nq):
                    dma("pool", w[:, q * per:(q + 1) * per, :], w_d[dg][:, q * per:(q + 1) * per, :], writes=[(kw, q)], flat=True)
                wts[dg] = (w, kw)
            load_w(0)
            load_a(0)
            load_a(1)
            for j, (dg, tt) in enumerate(iters):
                if tt == 0:
                    load_w(dg + 1)
                load_a(j + 2)
                w, kw = wts[dg]
                b = j % 4
                if a_res is None:
                    at, ka = loaded.pop(j)
                    lhs = lambda kc, at=at: at[:, kc, :]
                    rd = [ka]
                else:
                    lhs = lambda kc, tt=tt: a_res[:, kc, tt * 128:(tt + 1) * 128]
                    rd = []

                def mm(e, lhs=lhs, w=w, b=b):
                    ins = None
                    for kc in range(kcn):
                        ins = e.matmul(banks[b][:], lhsT=lhs(kc), rhs=w[:, kc, :], start=(kc == 0), stop=(kc == kcn - 1))
                    return ins
                S.op("pe", mm, reads=rd + [(kw, q) for q in range(nq)], writes=[bk(b)])
                c = tt * 4 + dg
                ys, kys = ysr.next()
                S.op("dve", lambda e, ys=ys, b=b: e.tensor_copy(out=ys[:], in_=banks[b][:]), reads=[bk(b)], writes=[kys])
                S.op("act", lambda e, ys=ys, c=c: e.activation(out=junk[:], in_=ys[:], func=AF.Square, accum_out=stat[:, c:c + 1]),
                     reads=[kys, "stat"], writes=["dn_junk", ("stat", c)])
                dma("sp", Y_d[tt * 128:(tt + 1) * 128, dg * 512:(dg + 1) * 512], ys[:], reads=[kys], writes=[("Y", tt, dg)])

        def ffn_block(idx, src_x, dst_x, final_pre):
            pass

        wg1, wu1, wd1 = ffn_w[0]
        wg2, wu2, wd2 = ffn_w[1]
        with ExitStack() as stA:
            hT = stA.enter_context(nc.sbuf_tensor("hT_a", [128, KC, S_TOK], BF16))
            if stage == 0:
                phase(lambda sb: rn_pass(sb, hT, x_d, None, None, 1.0, out_d, 0))
                return nc
            import os as _os
            if not _os.environ.get("K_SKIP_GU"):
                phase(lambda sb: rn_pass(sb, hT, x_d, None, None, 1.0, None, 0))
                phase(lambda sb: gu_phase(sb, hT, wg1, wu1))
            if stage == 0.5:
                phase(lambda sb: rn_pass(sb, hT, x_d, None, None, 1.0, out_d, None))
                return nc
        phase(lambda sb: dn_phase(sb, FC, wd1, A1_d))
        if stage == 0.75:
            phase(lambda sb: rn_pass(sb, None, x_d, None, None, 1.0, out_d, None))
            return nc
        if stage == 1:
            phase(lambda sb: rn_pass(sb, None, x_d, Y_d, 1, 0.5, out_d, None))
            return nc

        with ExitStack() as stCD:
            sbCD = lambda name, shape, dt: stCD.enter_context(nc.sbuf_tensor(uname(name), list(shape), dt))
            krT = sbCD("krT", [64, S_TOK], BF16)
            with ExitStack() as stC:
                hT = stC.enter_context(nc.sbuf_tensor("hT_c", [128, KC, S_TOK], BF16))
                phase(lambda sb: rn_pass(sb, hT, x_d, Y_d, 1, 0.5, X1_d, 2))

                def p_ma0(sb):
                    cos2 = sb("cos2a", [64, S_TOK], F32)
                    sin2 = sb("sin2a", [64, S_TOK], F32)
                    pos_i = sb("pos_i", [64, S_TOK], I32)
                    ang = sb("ang", [64, S_TOK], F32)
                    tq = sb("tq", [64, S_TOK], F32)
                    rr = sb("rr", [64, S_TOK], F32)
                    mm_ = sb("mm_", [64, S_TOK], F32)
                    ki = pos_i
                    kf = tq
                    dma("sp", pos_i[:], pos_d, writes=["pos_i"])
                    S.op("dve", lambda e: e.tensor_copy(out=ang[:], in_=pos_i[:]), reads=["pos_i"], writes=["ang"])
                    S.op("dve", lambda e: e.tensor_scalar(out=ang[:], in0=ang[:], scalar1=cst[:, 0:1], scalar2=None, op0=ALU.mult),
                         reads=["ang"], writes=["ang"])
                    for which in range(2):
                        shift = 0.0 if which == 0 else math.pi / 2
                        S.op("dve", lambda e, shift=shift: e.tensor_scalar(out=tq[:], in0=ang[:], scalar1=shift, scalar2=1.0 / (2 * math.pi),
                                                                            op0=ALU.add, op1=ALU.mult), reads=["ang"], writes=["tq"])
                        S.op("dve", lambda e: e.tensor_copy(out=ki[:], in_=tq[:]), reads=["tq", "pos_i"], writes=["pos_i"])
                        S.op("dve", lambda e: e.tensor_copy(out=kf[:], in_=ki[:]), reads=["pos_i"], writes=["tq"])
                        S.op("dve", lambda e: e.scalar_tensor_tensor(out=rr[:], in0=kf[:], scalar=-2 * math.pi, in1=ang[:],
                                                                     op0=ALU.mult, op1=ALU.add), reads=["tq", "ang"], writes=["rr"])
                        if which == 1:
                            S.op("dve", lambda e: e.tensor_scalar(out=rr[:], in0=rr[:], scalar1=math.pi / 2, scalar2=None, op0=ALU.add),
                                 reads=["rr"], writes=["rr"])
                        S.op("dve", lambda e: e.tensor_scalar(out=mm_[:], in0=rr[:], scalar1=0.0, scalar2=2 * math.pi,
                                                               op0=ALU.is_lt, op1=ALU.mult), reads=["rr"], writes=["mm_"])
                        S.op("dve", lambda e: e.tensor_tensor(out=rr[:], in0=rr[:], in1=mm_[:], op=ALU.add), reads=["rr", "mm_"], writes=["rr"])
                        S.op("dve", lambda e: e.tensor_scalar(out=rr[:], in0=rr[:], scalar1=0.0, scalar2=2 * math.pi,
                                                               op0=ALU.max, op1=ALU.min), reads=["rr"], writes=["rr"])
                        if which == 0:
                            S.op("act", lambda e: e.activation(out=sin2[:], in_=rr[:], func=AF.Sin, scale=cst[:, 1:2], bias=cst[:, 2:3]),
                                 reads=["rr"], writes=["sin2"])
                        else:
                            S.op("act", lambda e: e.activation(out=cos2[:], in_=rr[:], func=AF.Sin, scale=cst[:, 3:4], bias=cst[:, 4:5]),
                                 reads=["rr"], writes=["cos2"])
                    dma("sp", CS_d[0], cos2[:], reads=["cos2"], writes=["CS0"])
                    dma("sp", CS_d[1], sin2[:], reads=["sin2"], writes=["CS1"])
                phase(p_ma0)

                def p_ma(sb):
                    cos2 = sb("cos2b", [64, S_TOK], F32)
                    sin2 = sb("sin2b", [64, S_TOK], F32)
                    latT = sb("latTb", [128, 8, S_TOK], BF16)
                    dma("sp", cos2[:], CS_d[0], writes=["cos2"])
                    dma("sp", sin2[:], CS_d[1], writes=["sin2"])
                    glat = sb("glat_s", [128, 1024], F32)
                    dma("sp", glat[:], glat_d, writes=["glat"])
                    wr = Ring(sb, "ma_w", 2, [128, KC, 512], BF16)
                    scring = Ring(sb, "ma_sc", 4, [128, 4], F32)
                    cnr = Ring(sb, "ma_cn", 2, [128, 512], BF16)
                    junk = sb("ma_junk", [128, 512], BF16)
                    it = 0
                    for cg in range(2):
                        w, kw = wr.next()
                        dma("pool", w[:], wlat_d[cg], writes=[kw], flat=True)
                        for tt in range(NT):
                            b = it % 3
                            it += 1

                            def mm(e, w=w, b=b, tt=tt):
                                ins = None
                                for kc in range(KC):
                                    ins = e.matmul(banks[b][:], lhsT=hT[:, kc, tt * 128:(tt + 1) * 128], rhs=w[:, kc, :],
                                                   start=(kc == 0), stop=(kc == KC - 1))
                                return ins
                            S.op("pe", mm, reads=[kw], writes=[bk(b)])
                            sc, ksc = scring.next()
                            S.op("pool", lambda e, sc=sc: e.memset(sc[:], 0.0), writes=[(ksc, i) for i in range(4)])
                            S.op("act", lambda e, sc=sc, b=b: e.activation(out=junk[:], in_=banks[b][:], func=AF.Square, accum_out=sc[:, 0:1]),
                                 reads=[bk(b)], writes=["ma_junk", (ksc, 0)])
                            rstd_chain(sc, ksc, 0, 1, 1.0 / 512, EPS)
                            cn, kcn_ = cnr.next()
                            S.op("dve", lambda e, cn=cn, sc=sc, b=b, cg=cg: e.scalar_tensor_tensor(
                                out=cn[:], in0=banks[b][:], scalar=sc[:, 3:4], in1=glat[:, cg * 512:(cg + 1) * 512], op0=ALU.mult, op1=ALU.mult),
                                reads=[bk(b), (ksc, 3), "glat"], writes=[kcn_])
                            b2 = 3 + (it % 2)
                            bv = banks[b2][:].bitcast(BF16)

                            def tr(e, cn=cn, bv=bv):
                                ins = None
                                for j in range(4):
                                    ins = e.transpose(bv[:, j * 128:(j + 1) * 128], cn[:, j * 128:(j + 1) * 128], ident[:])
                                return ins
                            S.op("pe", tr, reads=[kcn_], writes=[bk(b2)])
                            dst = latT[:, cg * 4:(cg + 1) * 4, tt * 128:(tt + 1) * 128]
                            src = bv[:, 0:512].rearrange("p (a b) -> p a b", b=128)
                            S.op("act", lambda e, dst=dst, src=src: e.copy(out=dst, in_=src), reads=[bk(b2)], writes=[("latT", cg, tt)])
                    wk = sb("ma_wkr", [128, KC, 128], BF16)
                    dma("pool", wk[:], wkr_d, writes=["wkr"], flat=True)
                    t1r = Ring(sb, "ma_t1", 2, [64, 512], F32)
                    t2r = Ring(sb, "ma_t2", 2, [64, 512], F32)
                    for tg in range(NG):
                        cols = slice(tg * 512, (tg + 1) * 512)
                        bA, bB = 5, 6

                        def mmk(e, b, c0, cols=cols):
                            ins = None
                            for kc in range(KC):
                                ins = e.matmul(banks[b][0:64, :], lhsT=wk[:, kc, c0:c0 + 64], rhs=hT[:, kc, cols], start=(kc == 0), stop=(kc == KC - 1))
                            return ins
                        S.op("pe", lambda e, mmk=mmk: mmk(e, bA, 0), reads=["wkr"], writes=[bk(bA)])
                        S.op("pe", lambda e, mmk=mmk: mmk(e, bB, 64), reads=["wkr"], writes=[bk(bB)])
                        t1, k1 = t1r.next()
                        t2, k2 = t2r.next()
                        S.op("dve", lambda e, t1=t1, cols=cols: e.tensor_tensor(out=t1[:], in0=banks[bA][0:64, :], in1=cos2[:, cols], op=ALU.mult),
                             reads=[bk(bA), "cos2"], writes=[k1])
                        S.op("dve", lambda e, t2=t2, cols=cols: e.tensor_tensor(out=t2[:], in0=banks[bB][0:64, :], in1=sin2[:, cols], op=ALU.mult),
                             reads=[bk(bB), "sin2"], writes=[k2])
                        S.op("pool", lambda e, t1=t1, t2=t2, cols=cols: e.tensor_tensor(out=krT[:, cols], in0=t1[:], in1=t2[:], op=ALU.add),
                             reads=[k1, k2], writes=[("krT", tg)])
                    dma("sp", LAT_d, latT[:], reads=[("latT", cg, tt) for cg in range(2) for tt in range(NT)], writes=["LAT"], flat=True)
                phase(p_ma)

                def p_mb(sb):
                    wr = Ring(sb, "mb_w", 3, [128, KC, 128], BF16)
                    gsr = Ring(sb, "mb_gs", 2, [128, S_TOK], BF16)
                    it = 0
                    for oc in range(32):
                        w, kw = wr.next()
                        dma("pool", w[:], wgate_d[oc], writes=[kw], flat=True)
                        gs, kgs = gsr.next()
                        for tg in range(NG):
                            cols = slice(tg * 512, (tg + 1) * 512)
                            b = it % 4
                            it += 1

                            def mm(e, w=w, b=b, cols=cols):
                                ins = None
                                for kc in range(KC):
                                    ins = e.matmul(banks[b][:], lhsT=w[:, kc, :], rhs=hT[:, kc, cols], start=(kc == 0), stop=(kc == KC - 1))
                                return ins
                            S.op("pe", mm, reads=[kw], writes=[bk(b)])
                            S.op("act", lambda e, gs=gs, b=b, cols=cols: e.activation(out=gs[:, cols], in_=banks[b][:], func=AF.Sigmoid),
                                 reads=[bk(b)], writes=[(kgs, tg)])
                        dma("sp", GG_d[oc], gs[:], reads=[(kgs, tg) for tg in range(NG)], writes=[("GG", oc)])
                phase(p_mb)

                def p_mc(sb):
                    lb_bc = sb("lb_bc", [128, D], F32)
                    l1_bc = sb("l1_bc", [128, D], F32)
                    oml_bc = l1_bc
                    dma("sp", lb_bc[:], lbl_d[0], writes=["lb"])
                    dma("sp", l1_bc[:], lbl_d[1], writes=["l1"])
                    S.op("dve", lambda e: e.tensor_tensor(out=l1_bc[:], in0=lb_bc[:], in1=l1_bc[:], op=ALU.subtract), reads=["lb", "l1"], writes=["l1"])
                    S.op("act", lambda e: e.activation(out=lb_bc[:], in_=l1_bc[:], func=AF.Sigmoid), reads=["l1"], writes=["lb"])
                    S.op("dve", lambda e: e.tensor_scalar(out=oml_bc[:], in0=lb_bc[:], scalar1=-1.0, scalar2=1.0, op0=ALU.mult, op1=ALU.add),
                         reads=["lb", "l1"], writes=["oml", "l1"])
                    GW = 256
                    wqf = sb("mc_wqf", [128, KC, 512], BF16)
                    wi4 = sb("mc_wi4", [128, KC, 512], BF16)
                    whgr = Ring(sb, "mc_whg", 2, [128, KC, 128], BF16)
                    qT = sb("mc_qT", [128, 2, S_TOK], BF16)
                    kT = sb("mc_kT", [128, 2, S_TOK], BF16)
                    kd = sb("mc_kd", [128, NT, GW], BF16)
                    vt = sb("mc_vt", [128, NT, 512], BF16)
                    dec = sb("mc_dec", [128, 2, 32], F32)
                    tmp = {n: Ring(sb, "mc_" + n, r, [128, GW], F32) for n, r in
                           (("tq", 2), ("tf", 2), ("qs", 4), ("f", 3), ("lf", 3), ("k", 3), ("e1", 2), ("e2", 2), ("e3", 2))}
                    qtr = Ring(sb, "mc_qt", 3, [128, GW], BF16)
                    ktr = Ring(sb, "mc_kt", 3, [128, GW], BF16)
                    st32 = [sb(f"mc_st32_{h}", [128, 128], F32) for h in range(2)]
                    stbf = [sb(f"mc_stbf_{h}", [128, 128], BF16) for h in range(2)]
                    scmr = Ring(sb, "mc_scm", 4, [128, 128], BF16)
                    oaccr = Ring(sb, "mc_oacc", 1, [128, 512], F32)
                    sqr = Ring(sb, "mc_sq", 2, [128, 512], BF16)
                    tr_ = Ring(sb, "mc_t", 1, [128, 512], F32)
                    ter = Ring(sb, "mc_te", 1, [128, 512], F32)
                    sgr = Ring(sb, "mc_sg", 1, [128, 512], F32)
                    o1r = Ring(sb, "mc_o1", 1, [128, 512], F32)
                    ogr = Ring(sb, "mc_og", 1, [128, 512], BF16)

                    def act_sigmoid_inplace(t, kt_, src, rd):
                        S.op("act", lambda e, t=t, src=src: e.activation(out=t, in_=src, func=AF.Exp, scale=-1.0), reads=rd, writes=[kt_])
                        S.op("act", lambda e, t=t: e.activation(out=t, in_=t, func=AF.Ln, bias=1.0), reads=[kt_], writes=[kt_])
                        S.op("act", lambda e, t=t: e.activation(out=t, in_=t, func=AF.Exp, scale=-1.0), reads=[kt_], writes=[kt_])

                    for g in range(8):
                        gcols = slice(g * GW, (g + 1) * GW)
                        dma("pool", wqf[:], wqf_d[g], writes=["wqf"], flat=True)
                        if g % 2 == 0:
                            dma("pool", wi4[:], wi4_d[g // 2], writes=["wi4"], flat=True)
                        vo = (g % 2) * 256
                        whg = []
                        for hh in range(2):
                            w, kw = whgr.next()
                            dma("pool", w[:], whg_d[g * 2 + hh], writes=[kw], flat=True)
                            whg.append((w, kw))
                        for hh in range(2):
                            S.op("pool", lambda e, hh=hh: e.memset(st32[hh][:], 0.0), writes=[("st32", hh)])
                            S.op("pool", lambda e, hh=hh: e.memset(stbf[hh][:], 0.0), writes=[("stbf", hh)])
                        stA = {}
                        stA1 = {}
                        stB = {}

                        def stage_a(tt):
                            tcols = slice(tt * 128, (tt + 1) * 128)
                            qb = tt % 2
                            def mmqf(e, tcols=tcols, qb=qb):
                                ins = None
                                for kc in range(KC):
                                    ins = e.matmul(banks[qb][:], lhsT=hT[:, kc, tcols], rhs=wqf[:, kc, :], start=(kc == 0), stop=(kc == KC - 1))
                                return ins
                            S.op("pe", mmqf, reads=["wqf"], writes=[bk(qb)])
                            if g % 2 == 0:
                                def mmv(e, tcols=tcols):
                                    ins = None
                                    for kc in range(KC):
                                        ins = e.matmul(banks[2][:], lhsT=hT[:, kc, tcols], rhs=wi4[:, kc, :], start=(kc == 0), stop=(kc == KC - 1))
                                    return ins
                                S.op("pe", mmv, reads=["wi4"], writes=[bk(2)])
                                S.op("dve", lambda e, tt=tt: e.tensor_copy(out=vt[:, tt, :], in_=banks[2][:]), reads=[bk(2)], writes=[("vt", tt)])
                            tq, ktq = tmp["tq"].next()
                            tf, ktf = tmp["tf"].next()
                            qs, kqs = tmp["qs"].next()
                            f, kf_ = tmp["f"].next()
                            act_sigmoid_inplace(tq[:], ktq, banks[qb][:, 0:GW], [bk(qb)])
                            act_sigmoid_inplace(tf[:], ktf, banks[qb][:, GW:2 * GW], [bk(qb)])
                            S.op("dve", lambda e, qs=qs, tq=tq, qb=qb: e.tensor_tensor(out=qs[:], in0=banks[qb][:, 0:GW], in1=tq[:], op=ALU.mult),
                                 reads=[bk(qb), ktq, ktf], writes=[kqs])
                            S.op("dve", lambda e, f=f, tf=tf, gcols=gcols: e.tensor_tensor(out=f[:], in0=tf[:], in1=oml_bc[:, gcols], op=ALU.mult),
                                 reads=[ktf, "oml"], writes=[kf_])
                            S.op("dve", lambda e, f=f, gcols=gcols: e.tensor_tensor(out=f[:], in0=f[:], in1=lb_bc[:, gcols], op=ALU.add),
                                 reads=[kf_, "lb"], writes=[kf_])
                            stA1[tt] = (qs, kqs, f, kf_)

                        def stage_a2(tt):
                            qs, kqs, f, kf_ = stA1.pop(tt)
                            lf, klf = tmp["lf"].next()
                            k_, kk = tmp["k"].next()
                            S.op("act", lambda e, f=f, lf=lf: e.activation(out=lf[:], in_=f[:], func=AF.Ln), reads=[kf_], writes=[klf])
                            S.op("pool", lambda e, f=f, k_=k_: e.tensor_scalar(out=k_[:], in0=f[:], scalar1=-1.0, scalar2=1.0, op0=ALU.mult, op1=ALU.add),
                                 reads=[kf_], writes=[kk])
                            stA[tt] = (qs, kqs, lf, klf, k_, kk)

                        def stage_b(tt):
                            qs, kqs, lf, klf, k_, kk = stA.pop(tt)
                            e1, ke1 = tmp["e1"].next()
                            e2, ke2 = tmp["e2"].next()
                            e3, ke3 = tmp["e3"].next()
                            S.op("pe", lambda e, lf=lf: e.matmul(banks[3][:, 0:GW], lhsT=tri[:], rhs=lf[:], start=True, stop=True),
                                 reads=[klf], writes=[bk(3)])
                            S.op("pe", lambda e, lf=lf: e.matmul(banks[4][:, 0:GW], lhsT=tris[:], rhs=lf[:], start=True, stop=True),
                                 reads=[klf], writes=[bk(4)])

                            def mmdec(e, lf=lf):
                                ins = None
                                for hh in range(2):
                                    ins = e.matmul(banks[6][:, hh * 2:hh * 2 + 2], lhsT=lf[:, hh * 128:(hh + 1) * 128], rhs=blk2[:], start=True, stop=True)
                                return ins
                            S.op("pe", mmdec, reads=[klf], writes=[bk(6)])
                            S.op("act", lambda e, e1=e1: e.activation(out=e1[:], in_=banks[3][:, 0:GW], func=AF.Exp), reads=[bk(3)], writes=[ke1])
                            S.op("act", lambda e, e2=e2: e.activation(out=e2[:], in_=banks[3][:, 0:GW], func=AF.Exp, scale=-1.0), reads=[bk(3)], writes=[ke2])
                            S.op("act", lambda e, e3=e3: e.activation(out=e3[:], in_=banks[4][:, 0:GW], func=AF.Exp), reads=[bk(4)], writes=[ke3])
                            S.op("act", lambda e, tt=tt: e.activation(out=dec[:, :, tt * 2:tt * 2 + 2],
                                                                      in_=banks[6][:, 0:4].rearrange("p (h c) -> p h c", c=2), func=AF.Exp),
                                 reads=[bk(6)], writes=[("dec", tt)])
                            qt, kqt = qtr.next()
                            kt, kkt = ktr.next()
                            S.op("dve", lambda e, qt=qt, qs=qs, e1=e1: e.tensor_tensor(out=qt[:], in0=qs[:], in1=e1[:], op=ALU.mult), reads=[kqs, ke1], writes=[kqt])
                            S.op("dve", lambda e, kt=kt, k_=k_, e2=e2: e.tensor_tensor(out=kt[:], in0=k_[:], in1=e2[:], op=ALU.mult), reads=[kk, ke2], writes=[kkt])
                            S.op("pool", lambda e, tt=tt, k_=k_, e3=e3: e.tensor_tensor(out=kd[:, tt, :], in0=k_[:], in1=e3[:], op=ALU.mult),
                                 reads=[kk, ke3], writes=[("kd", tt)])
                            stB[tt] = (qt, kqt, kt, kkt)

                        def stage_c(tt):
                            tcols = slice(tt * 128, (tt + 1) * 128)
                            qt, kqt, kt, kkt = stB.pop(tt)
                            bv = banks[5][:].bitcast(BF16)

                            def tr4(e, qt=qt, kt=kt, bv=bv):
                                ins = None
                                for hh in range(2):
                                    ins = e.transpose(bv[:, hh * 128:(hh + 1) * 128], qt[:, hh * 128:(hh + 1) * 128], ident[:])
                                for hh in range(2):
                                    ins = e.transpose(bv[:, 256 + hh * 128:256 + (hh + 1) * 128], kt[:, hh * 128:(hh + 1) * 128], ident[:])
                                return ins
                            S.op("pe", tr4, reads=[kqt, kkt], writes=[bk(5)])
                            S.op("dve", lambda e, bv=bv, tcols=tcols: e.tensor_copy(out=qT[:, :, tcols], in_=bv[:, 0:256].rearrange("p (h c) -> p h c", c=128)),
                                 reads=[bk(5)], writes=[("qT", tt)])
                            S.op("dve", lambda e, bv=bv, tcols=tcols: e.tensor_copy(out=kT[:, :, tcols], in_=bv[:, 256:512].rearrange("p (h c) -> p h c", c=128)),
                                 reads=[bk(5)], writes=[("kT", tt)])

                        for step in range(NT + 3):
                            if step < NT:
                                stage_a(step)
                            if 0 <= step - 1 < NT:
                                stage_a2(step - 1)
                            if 0 <= step - 2 < NT:
                                stage_b(step - 2)
                            if 0 <= step - 3 < NT:
                                stage_c(step - 3)
                        for tt in range(NT):
                            tok0 = tt * 128
                            q4 = (tt % 4) * 128
                            scms = []
                            for hh in range(2):
                                S.op("pe", lambda e, hh=hh, tok0=tok0: e.matmul(banks[hh][:, 0:128], lhsT=kT[:, hh, tok0:tok0 + 128],
                                                                                 rhs=qT[:, hh, tok0:tok0 + 128], start=True, stop=True),
                                     reads=[("qT", tt), ("kT", tt)], writes=[bk(hh)])
                            for hh in range(2):
                                scm, kscm = scmr.next()
                                scms.append((scm, kscm))
                                S.op("dve", lambda e, scm=scm, hh=hh: e.tensor_tensor(out=scm[:], in0=banks[hh][:, 0:128], in1=tri[:], op=ALU.mult),
                                     reads=[bk(hh)], writes=[kscm])
                            for half in range(2):
                                prow = slice(half * 64, (half + 1) * 64)
                                for hh in range(2):
                                    scm, kscm = scms[hh]
                                    hc = slice(hh * 128, (hh + 1) * 128)
                                    oc_ = slice(q4 + half * 64, q4 + (half + 1) * 64)
                                    qc = slice(tok0 + half * 64, tok0 + (half + 1) * 64)

                                    vc = slice(vo + hh * 128, vo + (hh + 1) * 128)

                                    def mmo(e, hh=hh, scm=scm, vc=vc, oc_=oc_, qc=qc, half=half, tt=tt):
                                        e.matmul(banks[2 + hh][:, oc_], lhsT=stbf[hh][:], rhs=qT[:, hh, qc], start=True, stop=False)
                                        return e.matmul(banks[2 + hh][:, oc_], lhsT=vt[:, tt, vc], rhs=scm[:, half * 64:(half + 1) * 64], start=False, stop=True)
                                    S.op("pe", mmo, reads=[("stbf", hh), kscm, ("qT", tt), ("vt", tt)], writes=[bk(2 + hh)])
                                for hh in range(2):
                                    hc = slice(hh * 128, (hh + 1) * 128)
                                    vc = slice(vo + hh * 128, vo + (hh + 1) * 128)
                                    S.op("pe", lambda e, hh=hh, hc=hc, vc=vc, prow=prow, tt=tt: e.matmul(banks[6 + hh][:, 0:128], lhsT=kd[prow, tt, hc], rhs=vt[prow, tt, vc],
                                                                                                   start=True, stop=True),
                                         reads=[("kd", tt), ("vt", tt)], writes=[bk(6 + hh)])
                                for hh in range(2):
                                    ci = tt * 2 + half
                                    S.op("dve", lambda e, hh=hh, ci=ci: e.scalar_tensor_tensor(out=st32[hh][:], in0=st32[hh][:], scalar=dec[:, hh, ci:ci + 1],
                                                                                                in1=banks[6 + hh][:, 0:128], op0=ALU.mult, op1=ALU.add),
                                         reads=[bk(6 + hh), ("st32", hh), ("dec", tt)], writes=[("st32", hh)])
                                    S.op("act", lambda e, hh=hh: e.copy(out=stbf[hh][:], in_=st32[hh][:]), reads=[("st32", hh)], writes=[("stbf", hh)])
                            if tt % 4 == 3:
                                tg = tt // 4
                                cols = slice(tg * 512, (tg + 1) * 512)
                                for hh in range(2):
                                    head = g * 2 + hh
                                    w, kw = whg[hh]
                                    oa, koa = oaccr.next()
                                    S.op("dve", lambda e, oa=oa, hh=hh: e.tensor_copy(out=oa[:], in_=banks[2 + hh][:]), reads=[bk(2 + hh)], writes=[koa])
                                    sq, ksq = sqr.next()
                                    S.op("pool", lambda e, sq=sq, oa=oa: e.tensor_tensor(out=sq[:], in0=oa[:], in1=oa[:], op=ALU.mult), reads=[koa], writes=[ksq])
                                    S.op("pe", lambda e, sq=sq: e.matmul(banks[4][:], lhsT=ones_bf[:], rhs=sq[:], start=True, stop=True), reads=[ksq], writes=[bk(4)])
                                    t_, kt_ = tr_.next()
                                    S.op("dve", lambda e, t_=t_: e.tensor_scalar(out=t_[:], in0=banks[4][:], scalar1=1.0 / 128, scalar2=EPS, op0=ALU.mult, op1=ALU.add),
                                         reads=[bk(4)], writes=[kt_])
                                    S.op("act", lambda e, t_=t_: e.activation(out=t_[:], in_=t_[:], func=AF.Ln), reads=[kt_], writes=[kt_])
                                    S.op("act", lambda e, t_=t_: e.activation(out=t_[:], in_=t_[:], func=AF.Exp, scale=-0.5), reads=[kt_], writes=[kt_])

                                    def mmg(e, w=w, cols=cols):
                                        ins = None
                                        for kc in range(KC):
                                            ins = e.matmul(banks[5][:], lhsT=w[:, kc, :], rhs=hT[:, kc, cols], start=(kc == 0), stop=(kc == KC - 1))
                                        return ins
                                    S.op("pe", mmg, reads=[kw], writes=[bk(5)])
                                    te, kte = ter.next()
                                    sg, ksg = sgr.next()
                                    o1, ko1 = o1r.next()
                                    og, kog = ogr.next()
                                    act_sigmoid_inplace(te[:], kte, banks[5][:], [bk(5)])
                                    S.op("dve", lambda e, sg=sg, te=te: e.tensor_tensor(out=sg[:], in0=banks[5][:], in1=te[:], op=ALU.mult), reads=[bk(5), kte], writes=[ksg])
                                    S.op("dve", lambda e, o1=o1, oa=oa, t_=t_: e.scalar_tensor_tensor(out=o1[:], in0=oa[:], scalar=gout[:, 0:1], in1=t_[:],
                                                                                                       op0=ALU.mult, op1=ALU.mult),
                                         reads=[koa, kt_], writes=[ko1])
                                    S.op("pool", lambda e, og=og, o1=o1, sg=sg: e.tensor_tensor(out=og[:], in0=o1[:], in1=sg[:], op=ALU.mult), reads=[ko1, ksg], writes=[kog])
                                    dma("sp", OG_d[head][:, cols], og[:], reads=[kog], writes=[("OG", head, tg)])
                phase(p_mc)

            def p_md(sb):
                cos2 = sb("cos2d", [64, S_TOK], F32)
                sin2 = sb("sin2d", [64, S_TOK], F32)
                latT = sb("latTd", [128, 8, S_TOK], BF16)
                dma("sp", cos2[:], CS_d[0], writes=["cos2"])
                dma("sp", sin2[:], CS_d[1], writes=["sin2"])
                dma("sp", latT[:], LAT_d, writes=["latT"], flat=True)
                wait_all(["cos2", "sin2", "latT"])
                wqr = Ring(sb, "md_wq", 2, [128, 4, 256], BF16)
                wknr = Ring(sb, "md_wkn", 2, [128, 4, 128], BF16)
                wvr = Ring(sb, "md_wv", 2, [128, 4, 512], BF16)
                qnr = Ring(sb, "md_qn", 2, [128, S_TOK], BF16)
                qrr = Ring(sb, "md_qr", 2, [64, S_TOK], BF16)
                knr = Ring(sb, "md_kn", 2, [128, S_TOK], BF16)
                vtr = Ring(sb, "md_vt", 2, [128, NT, 512], BF16)
                ptr = Ring(sb, "md_pt", 4, [128, 512], BF16)
                t1r = Ring(sb, "md_t1", 2, [64, 512], F32)
                t2r = Ring(sb, "md_t2", 2, [64, 512], F32)
                rsr = Ring(sb, "md_rs", 2, [128, 512], F32)
                osr = Ring(sb, "md_os", 2, [128, 512], BF16)
                pacr = Ring(sb, "md_pa", 2, [128, 512], F32)
                ones_f = sb("md_ones_f", [128, 128], F32)
                S.op("pool", lambda e: e.memset(ones_f[:], 1.0), writes=["ones_f"])
                zi = 0
                ai = 0
                vt = kvt = None
                for h in range(NH):
                    wq, kwq = wqr.next()
                    wkn, kwkn = wknr.next()
                    dma("pool", wq[:], wq_d[h], writes=[kwq], flat=True)
                    dma("pool", wkn[:], wkn_d[h], writes=[kwkn], flat=True)
                    if h % 4 == 0:
                        wv, kwv = wvr.next()
                        dma("pool", wv[:], wv_d[h // 4], writes=[kwv], flat=True)
                        vt, kvt = vtr.next()
                        for tt in range(NT):
                            b = zi % 4
                            zi += 1

                            def mmv(e, b=b, tt=tt, wv=wv):
                                ins = None
                                for kc in range(4):
                                    ins = e.matmul(banks[b][:], lhsT=latT[:, 4 + kc, tt * 128:(tt + 1) * 128], rhs=wv[:, kc, :], start=(kc == 0), stop=(kc == 3))
                                return ins
                            S.op("pe", mmv, reads=[kwv], writes=[bk(b)])
                            if tt % 2 == 0:
                                S.op("act", lambda e, vt=vt, b=b, tt=tt: e.copy(out=vt[:, tt, :], in_=banks[b][:]), reads=[bk(b)], writes=[(kvt, tt)])
                            else:
                                S.op("dve", lambda e, vt=vt, b=b, tt=tt: e.tensor_copy(out=vt[:, tt, :], in_=banks[b][:]), reads=[bk(b)], writes=[(kvt, tt)])
                    hh = h % 4
                    qn, kqn = qnr.next()
                    qr, kqr = qrr.next()
                    kn, kkn = knr.next()
                    for tg in range(NG):
                        cols = slice(tg * 512, (tg + 1) * 512)
                        b = zi % 4
                        zi += 1

                        def mmq(e, b=b, wq=wq, cols=cols):
                            ins = None
                            for kc in range(4):
                                ins = e.matmul(banks[b][:], lhsT=wq[:, kc, 0:128], rhs=latT[:, kc, cols], start=(kc == 0), stop=(kc == 3))
                            return ins
                        S.op("pe", mmq, reads=[kwq], writes=[bk(b)])
                        S.op("act", lambda e, qn=qn, b=b, cols=cols: e.copy(out=qn[:, cols], in_=banks[b][:]), reads=[bk(b)], writes=[(kqn, tg)])
                        b = zi % 4
                        zi += 1

                        def mmk(e, b=b, wkn=wkn, cols=cols):
                            ins = None
                            for kc in range(4):
                                ins = e.matmul(banks[b][:], lhsT=wkn[:, kc, :], rhs=latT[:, 4 + kc, cols], start=(kc == 0), stop=(kc == 3))
                            return ins
                        S.op("pe", mmk, reads=[kwkn], writes=[bk(b)])
                        S.op("dve", lambda e, kn=kn, b=b, cols=cols: e.tensor_copy(out=kn[:, cols], in_=banks[b][:]), reads=[bk(b)], writes=[(kkn, tg)])
                        bA = zi % 4
                        zi += 1
                        bB = zi % 4
                        zi += 1

                        def mmr(e, b, c0, wq=wq, cols=cols):
                            ins = None
                            for kc in range(4):
                                ins = e.matmul(banks[b][0:64, :], lhsT=wq[:, kc, c0:c0 + 64], rhs=latT[:, kc, cols], start=(kc == 0), stop=(kc == 3))
                            return ins
                        S.op("pe", lambda e, mmr=mmr, bA=bA: mmr(e, bA, 128), reads=[kwq], writes=[bk(bA)])
                        S.op("pe", lambda e, mmr=mmr, bB=bB: mmr(e, bB, 192), reads=[kwq], writes=[bk(bB)])
                        t1, k1 = t1r.next()
                        t2, k2 = t2r.next()
                        S.op("dve", lambda e, t1=t1, bA=bA, cols=cols: e.tensor_tensor(out=t1[:], in0=banks[bA][0:64, :], in1=cos2[:, cols], op=ALU.mult),
                             reads=[bk(bA)], writes=[k1])
                        S.op("dve", lambda e, t2=t2, bB=bB, cols=cols: e.tensor_tensor(out=t2[:], in0=banks[bB][0:64, :], in1=sin2[:, cols], op=ALU.mult),
                             reads=[bk(bB)], writes=[k2])
                        S.op("pool", lambda e, qr=qr, t1=t1, t2=t2, cols=cols: e.tensor_tensor(out=qr[:, cols], in0=t1[:], in1=t2[:], op=ALU.add),
                             reads=[k1, k2], writes=[(kqr, tg)])
                    allq = [(kqn, t) for t in range(NG)] + [(kqr, t) for t in range(NG)] + [(kkn, t) for t in range(NG)]
                    steps = [(qg, kb) for qg in range(NG) for kb in range(4 * (qg + 1))]
                    qk = {}
                    accs = {}

                    def rec_qk(si, zi0):
                        qg, kb = steps[si]
                        i = kb - 4 * qg
                        c0 = max(i, 0) * 128
                        kcols = slice(kb * 128, (kb + 1) * 128)
                        qcols = slice(qg * 512 + c0, (qg + 1) * 512)
                        b = zi0 % 4

                        def mms(e, b=b, c0=c0, kcols=kcols, qcols=qcols, qn=qn, kn=kn, qr=qr):
                            e.matmul(banks[b][:, c0:512], lhsT=kn[:, kcols], rhs=qn[:, qcols], start=True, stop=False)
                            return e.matmul(banks[b][:, c0:512], lhsT=krT[:, kcols], rhs=qr[:, qcols], start=False, stop=True)
                        S.op("pe", mms, reads=allq, writes=[bk(b)])
                        pt, kpt = ptr.next()
                        S.op("act", lambda e, pt=pt, b=b, c0=c0: e.activation(out=pt[:, c0:512], in_=banks[b][:, c0:512], func=AF.Exp, scale=ATT_SCALE),
                             reads=[bk(b)], writes=[kpt])
                        if i >= 0:
                            S.op("pool", lambda e, pt=pt, c0=c0: e.tensor_tensor(out=pt[:, c0:c0 + 128], in0=pt[:, c0:c0 + 128], in1=caus[:], op=ALU.mult),
                                 reads=[kpt], writes=[kpt])
                        qk[si] = (pt, kpt, c0)

                    rec_qk(0, zi)
                    zi += 1
                    for si, (qg, kb) in enumerate(steps):
                        if si + 1 < len(steps):
                            rec_qk(si + 1, zi)
                            zi += 1
                        nkb = 4 * (qg + 1)
                        if kb == 0:
                            bO = 4 + 2 * (ai % 2)
                            bS = bO + 1
                            ai += 1
                            pa, kpa = pacr.next()
                            accs[qg] = (bO, bS, pa, kpa)
                        bO, bS, pa, kpa = accs[qg]
                        pt, kpt, c0 = qk.pop(si)

                        def mmpv(e, pt=pt, c0=c0, kb=kb, nkb=nkb, bO=bO, vt=vt, hh=hh):
                            return e.matmul(banks[bO][:, c0:512], lhsT=vt[:, kb, hh * 128:(hh + 1) * 128], rhs=pt[:, c0:512], start=(kb == 0), stop=(kb == nkb - 1),
                                            skip_group_check=True)
                        S.op("pe", mmpv, reads=[kpt, (kvt, kb)], writes=[bk(bO)])
                        if kb == 0:
                            S.op("dve", lambda e, pa=pa, pt=pt: e.tensor_copy(out=pa[:], in_=pt[:]), reads=[kpt], writes=[kpa])
                        else:
                            S.op("dve", lambda e, pa=pa, pt=pt, c0=c0: e.tensor_tensor(out=pa[:, c0:512], in0=pa[:, c0:512], in1=pt[:, c0:512], op=ALU.add),
                                 reads=[kpt, kpa], writes=[kpa])
                        if kb == nkb - 1:
                            S.op("pe", lambda e, pa=pa, bS=bS: e.matmul(banks[bS][:], lhsT=ones_f[:], rhs=pa[:], start=True, stop=True), reads=[kpa, "ones_f"], writes=[bk(bS)])
                            rs, krs = rsr.next()
                            os_, kos = osr.next()
                            S.op("dve", lambda e, rs=rs, bS=bS: e.reciprocal(out=rs[:], in_=banks[bS][:]), reads=[bk(bS)], writes=[krs])
                            S.op("dve", lambda e, os_=os_, rs=rs, bO=bO: e.tensor_tensor(out=os_[:], in0=banks[bO][:], in1=rs[:], op=ALU.mult),
                                 reads=[bk(bO), krs], writes=[kos])
                            dma("sp", OA_d[h][:, qg * 512:(qg + 1) * 512], os_[:], reads=[kos], writes=[("OA", h, qg)])
            phase(p_md)
        def p_me(sb):
            OAs = sb("me_oa", [128, 16, S_TOK], BF16)
            OGs = sb("me_og", [128, 16, S_TOK], BF16)
            for h in range(16):
                dma("sp", OAs[:, h, :], OA_d[h], writes=[("oas", h)])
                dma("sp", OGs[:, h, :], OG_d[h], writes=[("ogs", h)])
            allo = [("oas", h) for h in range(16)] + [("ogs", h) for h in range(16)]
            war = Ring(sb, "me_wa", 2, [128, KC, 128], BF16)
            wbr = Ring(sb, "me_wb", 2, [128, KC, 128], BF16)
            gar = Ring(sb, "me_ga", 2, [128, S_TOK], BF16)
            gbr = Ring(sb, "me_gb", 2, [128, S_TOK], BF16)
            m1r = Ring(sb, "me_m1", 2, [128, 512], F32)
            m2r = Ring(sb, "me_m2", 2, [128, 512], F32)
            mgr = Ring(sb, "me_mg", 2, [128, S_TOK], BF16)
            it = 0
            pre = {}

            def load_dc(dc):
                if dc >= 16 or dc in pre:
                    return
                wa, kwa = war.next()
                wb, kwb = wbr.next()
                ga, kga = gar.next()
                gb, kgb = gbr.next()
                dma("pool", wa[:], wo_d[dc], writes=[kwa], flat=True)
                dma("pool", wb[:], wob_d[dc], writes=[kwb], flat=True)
                dma("sp", ga[:], GG_d[dc], writes=[kga])
                dma("sp", gb[:], GG_d[16 + dc], writes=[kgb])
                pre[dc] = (wa, kwa, wb, kwb, ga, kga, gb, kgb)
            load_dc(0)
            for dc in range(16):
                load_dc(dc + 1)
                wa, kwa, wb, kwb, ga, kga, gb, kgb = pre.pop(dc)
                mg, kmg = mgr.next()
                for tg in range(NG):
                    cols = slice(tg * 512, (tg + 1) * 512)
                    iA, iB = 2 * (it % 4), 2 * (it % 4) + 1
                    it += 1

                    def mm(e, w, src, b, cols=cols):
                        ins = None
                        for kc in range(KC):
                            ins = e.matmul(banks[b][:], lhsT=w[:, kc, :], rhs=src[:, kc, cols], start=(kc == 0), stop=(kc == KC - 1))
                        return ins
                    S.op("pe", lambda e, mm=mm, wa=wa, iA=iA: mm(e, wa, OAs, iA), reads=[kwa] + allo, writes=[bk(iA)])
                    S.op("pe", lambda e, mm=mm, wb=wb, iB=iB: mm(e, wb, OGs, iB), reads=[kwb] + allo, writes=[bk(iB)])
                    m1, km1 = m1r.next()
                    m2, km2 = m2r.next()
                    S.op("dve", lambda e, m1=m1, iA=iA, ga=ga, cols=cols: e.tensor_tensor(out=m1[:], in0=banks[iA][:], in1=ga[:, cols], op=ALU.mult),
                         reads=[bk(iA), kga], writes=[km1])
                    S.op("dve", lambda e, m2=m2, iB=iB, gb=gb, cols=cols: e.tensor_tensor(out=m2[:], in0=banks[iB][:], in1=gb[:, cols], op=ALU.mult),
                         reads=[bk(iB), kgb], writes=[km2])
                    S.op("pool", lambda e, mg=mg, m1=m1, m2=m2, cols=cols: e.tensor_tensor(out=mg[:, cols], in0=m1[:], in1=m2[:], op=ALU.add),
                         reads=[km1, km2], writes=[(kmg, tg)])
                dma("sp", MT_d[dc], mg[:], reads=[(kmg, tg) for tg in range(NG)], writes=[("MT", dc)])
        phase(p_me)

        def p_mf(sb):
            mT = sb("mf_mT", [128, 16, S_TOK], BF16)
            for dc in range(16):
                dma("sp", mT[:, dc, :], MT_d[dc], writes=[("mT", dc)])
            S.op("pe", lambda e: e.nop(), reads=[("mT", dc) for dc in range(16)])
            dn_phase(sb, KC, wout_d, None, a_res=mT)
        phase(p_mf)
        if stage == 2:
            phase(lambda sb: rn_pass(sb, None, X1_d, Y_d, 3, 1.0, out_d, None))
            return nc

        with ExitStack() as stE:
            hT = stE.enter_context(nc.sbuf_tensor("hT_e", [128, KC, S_TOK], BF16))
            phase(lambda sb: rn_pass(sb, hT, X1_d, Y_d, 3, 1.0, X2_d, 4))
            phase(lambda sb: gu_phase(sb, hT, wg2, wu2))
        phase(lambda sb: dn_phase(sb, FC, wd2, A1_d))
        phase(lambda sb: rn_pass(sb, None, X2_d, Y_d, 5, 0.5, out_d, None))
    return nc


def _fm(W, kc):
    K, N = W.shape
    return np.ascontiguousarray(W.reshape(kc, 128, N // 128, 128).transpose(2, 1, 0, 3))


def _tm(W, kc, n=512):
    K, N = W.shape
    return np.ascontiguousarray(W.reshape(kc, 128, N // n, n).transpose(2, 1, 0, 3))


def _prep_shared(inp):
    f32 = np.float32
    sh = {}
    gains = [inp[k][0] for k in ("ffn1_norm_pre", "ffn1_norm_post", "mix_norm_pre", "mix_norm_post", "ffn2_norm_pre", "ffn2_norm_post")]
    sh["gains"] = np.ascontiguousarray(np.broadcast_to(np.stack(gains)[:, None, :], (6, 128, D))).astype(f32, copy=False)
    glat = np.concatenate([inp["mla_q_norm"][0], inp["mla_kv_norm"][0]])
    sh["glat"] = np.ascontiguousarray(np.broadcast_to(glat[None, :], (128, 1024)))
    sh["lbl"] = np.ascontiguousarray(np.broadcast_to(inp["hgrn_lb_logits"][:, None, :], (2, 128, D)))
    sh["gout"] = np.ascontiguousarray(inp["hgrn_out_norm"][0].reshape(128, 1))
    sh["ident"] = np.eye(128, dtype=f32)
    s = np.arange(128)[:, None]
    t = np.arange(128)[None, :]
    same = (s // 64) == (t // 64)
    sh["tri"] = (same & (s <= t)).astype(f32)
    sh["tris"] = (same & (s > t)).astype(f32)
    sh["caus"] = (t >= s).astype(f32)
    sh["blk2"] = ((np.arange(128)[:, None] // 64) == np.arange(2)[None, :]).astype(f32)
    cst = np.zeros((64, 8), f32)
    half = 32
    invf = (10000.0 ** (-(np.arange(half, dtype=np.float32)) / half)).astype(f32)
    cst[:, 0] = np.concatenate([invf, invf])
    cst[:32, 1] = 1.0
    cst[32:, 1] = -1.0
    cst[:32, 2] = -math.pi
    cst[32:, 2] = math.pi
    cst[:, 3] = -1.0
    cst[:, 4] = math.pi
    sh["cst"] = cst
    for i in (1, 2):
        sh[f"wg{i}"] = _fm(inp[f"ffn{i}_w_gate"][0], KC)
        sh[f"wu{i}"] = _fm(inp[f"ffn{i}_w_up"][0], KC)
        sh[f"wd{i}"] = _tm(inp[f"ffn{i}_w_down"][0], FC)
    w_in = inp["w_in"][0]
    o = 0
    sh["wlat"] = _tm(w_in[:, 0:1024], KC)
    kr = w_in[:, 1024:1088]
    krs = np.concatenate([kr[:, 32:64], kr[:, 0:32]], axis=1)
    sh["wkr"] = np.ascontiguousarray(np.concatenate([kr, krs], axis=1).reshape(KC, 128, 128).transpose(1, 0, 2))
    o = 1088
    hq = w_in[:, o:o + 2048]; hf = w_in[:, o + 2048:o + 4096]; hi = w_in[:, o + 4096:o + 6144]; hg = w_in[:, o + 6144:o + 8192]
    ga = w_in[:, o + 8192:o + 10240]; gb = w_in[:, o + 10240:o + 12288]
    sh["wqf"] = np.concatenate([_tm(hq, KC, 256), _tm(hf, KC, 256)], axis=3)
    sh["wi4"] = _tm(hi, KC, 512)
    sh["whg"] = _fm(hg, KC)
    sh["wgate"] = np.concatenate([_fm(ga, KC), _fm(gb, KC)], axis=0)
    wq = inp["mla_w_q_up"][0].reshape(512, 16, 192)
    qn = wq[:, :, 0:128]; qr = wq[:, :, 128:192]
    qrs = np.concatenate([qr[:, :, 32:64], qr[:, :, 0:32]], axis=2)
    wq_all = np.concatenate([qn, qr, qrs], axis=2)
    sh["wq"] = np.ascontiguousarray(wq_all.reshape(4, 128, 16, 256).transpose(2, 1, 0, 3))
    wkv = inp["mla_w_kv_up"][0].reshape(512, 16, 256)
    sh["wkn"] = np.ascontiguousarray(wkv[:, :, 0:128].reshape(4, 128, 16, 128).transpose(2, 1, 0, 3))
    wv = wkv[:, :, 128:256].reshape(512, 4, 512)
    sh["wv"] = np.ascontiguousarray(wv.reshape(4, 128, 4, 512).transpose(2, 1, 0, 3))
    sh["wo"] = _fm(inp["mla_w_o"][0], KC)
    sh["wob"] = _fm(inp["hgrn_w_o"][0], KC)
    sh["wout"] = _tm(inp["w_out"][0], KC)
    return {k: np.ascontiguousarray(v, dtype=np.float32) for k, v in sh.items()}


def run(inputs, n_cores=8, stage=3, trace=False):
    inp = {k: np.asarray(v) for k, v in inputs.items()}
    sh = _prep_shared(inp)
    nc = build(stage)
    in_maps = []
    for b in range(n_cores):
        m = dict(sh)
        m["x"] = np.ascontiguousarray(inp["x"][b], dtype=np.float32)
        m["pos"] = np.ascontiguousarray(np.broadcast_to(inp["positions"][b][None, :], (64, S_TOK))).astype(np.int32, copy=False)
        in_maps.append(m)
    res = run_bass_kernel_spmd(nc, in_maps, core_ids=list(range(n_cores)), **({"trace": True} if trace else {}))
    out = np.stack([np.asarray(r["out"]) for r in res.results], axis=0)
    return out, res


def kernel(**inputs):
    out, _ = run(inputs, n_cores=8, stage=3)
    return out.astype(np.float32, copy=False)
```
